# Optimizing a Trainium2 kernel written in Bass

```python
import jax, jax.numpy as jnp
from jax import lax
import numpy as np

D_MODEL = 1024
BATCH = 4
SEQ = 8192
DEPTH = 2

HEAD_DIM = 64
N_HEADS = 16
DIL_WINDOWS = (128, 512, 2048)
DIL_RATES = (1, 4, 16)
N_DIL = 3
N_KV_HEADS = 2
CMP_STRIDE = 16
CMP_LEN = 32
CMP_HIDDEN = 128
SEL_BLOCK = 64
N_SELECT = 16
SLIDE_WINDOW = 512
D_FF = 2816
Q_BLOCK = 128
ROPE_THETA = 10000.0
EPS = 1e-6

kernel_name = 'yoco_dilated_nsa_macaron_adaln'


def rms_norm(x, g):
    xf = x.astype(jnp.float32)
    y = xf * lax.rsqrt(jnp.mean(xf * xf, axis=-1, keepdims=True) + EPS)
    return (y * g.astype(jnp.float32)).astype(x.dtype)


def modulate(x, shift, scale):
    return x * (1.0 + scale) + shift


def rope(x):
    s = x.shape[1]
    half = HEAD_DIM // 2
    inv = ROPE_THETA ** (-jnp.arange(half, dtype=jnp.float32) / half)
    ang = jnp.arange(s, dtype=jnp.float32)[:, None] * inv[None, :]
    cos = jnp.cos(ang)[None, :, None, :]
    sin = jnp.sin(ang)[None, :, None, :]
    xf = x.astype(jnp.float32)
    x1, x2 = xf[..., :half], xf[..., half:]
    return jnp.concatenate([x1 * cos - x2 * sin, x1 * sin + x2 * cos], axis=-1).astype(x.dtype)


def swiglu(x, w_in, w_out):
    g, u = jnp.split(x @ w_in, 2, axis=-1)
    return (jax.nn.silu(g) * u) @ w_out


def masked_softmax(s, mask):
    s = jnp.where(mask, s, -jnp.inf)
    m = jnp.max(s, axis=-1, keepdims=True)
    m = jnp.where(jnp.isfinite(m), m, 0.0)
    p = jnp.exp(s - m)
    return p / jnp.maximum(jnp.sum(p, axis=-1, keepdims=True), 1e-30)


def _ada(mod, s):
    return mod[:, s, 0][:, None, :], mod[:, s, 1][:, None, :], 1.0 + mod[:, s, 2][:, None, :]


def banded_causal_attention(q, k, v, n_back):
    n, h, l, hd = q.shape
    nb = -(-l // Q_BLOCK)
    lp = nb * Q_BLOCK
    front = (-(-n_back // Q_BLOCK)) * Q_BLOCK
    kw = front + Q_BLOCK
    qp = jnp.pad(q, ((0, 0), (0, 0), (0, lp - l), (0, 0)))
    kp = jnp.pad(k, ((0, 0), (0, 0), (front, lp - l), (0, 0)))
    vp = jnp.pad(v, ((0, 0), (0, 0), (front, lp - l), (0, 0)))
    qi = jnp.arange(Q_BLOCK)[:, None]
    ki = jnp.arange(kw)[None, :]
    dist = front + qi - ki
    band = (dist >= 0) & (dist <= n_back)
    scale = HEAD_DIM ** -0.5

    def step(bi):
        start = bi * Q_BLOCK
        qb = lax.dynamic_slice_in_dim(qp, start, Q_BLOCK, axis=2)
        kb = lax.dynamic_slice_in_dim(kp, start, kw, axis=2)
        vb = lax.dynamic_slice_in_dim(vp, start, kw, axis=2)
        s = jnp.einsum('nhqd,nhkd->nhqk', qb, kb).astype(jnp.float32) * scale
        s = jnp.where(band & (start - front + ki >= 0), s, -jnp.inf)
        m = jnp.max(s, axis=-1, keepdims=True)
        p = jnp.exp(s - m)
        den = jnp.sum(p, axis=-1, keepdims=True)
        o = jnp.einsum('nhqk,nhkd->nhqd', (p / den).astype(vb.dtype), vb)
        return o, (m + jnp.log(den))[..., 0]

    o, lse = lax.map(step, jnp.arange(nb))
    o = jnp.moveaxis(o, 0, 2).reshape(n, h, lp, hd)[:, :, :l]
    lse = jnp.moveaxis(lse, 0, 2).reshape(n, h, lp)[:, :, :l]
    return o, lse


def dilated_attention(q, k, v, window, rate):
    b, s, h, hd = q.shape
    l = s // rate

    def to_res(t):
        return t.reshape(b, l, rate, h, hd).transpose(0, 2, 3, 1, 4).reshape(b * rate, h, l, hd)

    o, lse = banded_causal_attention(to_res(q), to_res(k), to_res(v), window // rate)
    o = o.reshape(b, rate, h, l, hd).transpose(0, 3, 1, 2, 4).reshape(b, s, h, hd)
    lse = lse.reshape(b, rate, h, l).transpose(0, 3, 1, 2).reshape(b, s, h)
    return o, lse


def dilated_mixer(u, w_qkv, q_gain, k_gain, w_o):
    b, s, _ = u.shape
    qkv = (u @ w_qkv).reshape(b, s, N_DIL, 3, N_HEADS, HEAD_DIM)
    outs, lses = [], []
    for g in range(N_DIL):
        q = rope(rms_norm(qkv[:, :, g, 0], q_gain[g]))
        k = rope(rms_norm(qkv[:, :, g, 1], k_gain[g]))
        o, lse = dilated_attention(q, k, qkv[:, :, g, 2], DIL_WINDOWS[g], DIL_RATES[g])
        outs.append(o.astype(jnp.float32))
        lses.append(lse)
    w = jax.nn.softmax(jnp.stack(lses, axis=0), axis=0)
    o = jnp.sum(w[..., None] * jnp.stack(outs, axis=0), axis=0).astype(u.dtype)
    return o.reshape(b, s, N_HEADS * HEAD_DIM) @ w_o


def shared_kv(h, shift, scale, kv_norm_g, w_kv, kv_k_gain, cmp_pos, phi_w1, phi_w2):
    b, s, _ = h.shape
    u = modulate(rms_norm(h, kv_norm_g), shift, scale)
    kv = (u @ w_kv).reshape(b, s, 3, 2, N_KV_HEADS, HEAD_DIM)
    nsb = s // CMP_STRIDE
    nper = CMP_LEN // CMP_STRIDE
    n_c = nsb - nper + 1

    def compress(t, i):
        tb = t.reshape(b, nsb, CMP_STRIDE, N_KV_HEADS, HEAD_DIM)
        blocks = jnp.concatenate([tb[:, j:j + n_c] for j in range(nper)], axis=2)
        blocks = blocks + cmp_pos[i][None, None, :, None, :]
        flat = blocks.transpose(0, 1, 3, 2, 4).reshape(b, n_c, N_KV_HEADS, CMP_LEN * HEAD_DIM)
        return jax.nn.silu(flat @ phi_w1[i]) @ phi_w2[i]

    k_cmp = rms_norm(compress(kv[:, :, 0, 0], 0), kv_k_gain[0]).transpose(0, 2, 1, 3)
    v_cmp = compress(kv[:, :, 0, 1], 1).transpose(0, 2, 1, 3)
    n_s = s // SEL_BLOCK

    def to_blocks(t):
        return t.reshape(b, n_s, SEL_BLOCK, N_KV_HEADS, HEAD_DIM).transpose(0, 3, 1, 2, 4)

    k_slc = to_blocks(rope(rms_norm(kv[:, :, 1, 0], kv_k_gain[1])))
    v_slc = to_blocks(kv[:, :, 1, 1])
    pad = ((0, 0), (0, 0), (SLIDE_WINDOW, 0), (0, 0))
    k_win = jnp.pad(rope(rms_norm(kv[:, :, 2, 0], kv_k_gain[2])).transpose(0, 2, 1, 3), pad)
    v_win = jnp.pad(kv[:, :, 2, 1].transpose(0, 2, 1, 3), pad)
    return k_cmp, v_cmp, k_slc, v_slc, k_win, v_win


def block_importance(p, n_s):
    r = SEL_BLOCK // CMP_STRIDE
    nper = CMP_LEN // CMP_STRIDE
    left = nper - 1
    n_c = p.shape[-1]
    right = r * n_s + r - n_c
    pp = jnp.pad(p, [(0, 0)] * (p.ndim - 1) + [(left, right)])
    imp = None
    for o in range(-left, r):
        w = sum(1 for m in range(r) for n in range(nper) if m - n == o)
        sl = pp[..., o + left:o + left + r * n_s:r]
        imp = w * sl if imp is None else imp + w * sl
    return imp


def nsa_mixer(u, k_cmp, v_cmp, k_slc, v_slc, k_win, v_win, w_qg, q_gain, w_o):
    b, s, _ = u.shape
    grp = N_HEADS // N_KV_HEADS
    hdim = N_HEADS * HEAD_DIM
    qg = u @ w_qg
    q = rms_norm(qg[..., :hdim].reshape(b, s, N_HEADS, HEAD_DIM), q_gain)
    gates = jax.nn.sigmoid(qg[..., hdim:].astype(jnp.float32))
    gates = gates.reshape(b, s, 3, N_KV_HEADS, grp).transpose(2, 0, 3, 4, 1)

    def to_grp(t):
        return t.reshape(b, s, N_KV_HEADS, grp, HEAD_DIM).transpose(0, 2, 3, 1, 4)

    q_nope = to_grp(q)
    q_rope = to_grp(rope(q))
    n_c = k_cmp.shape[2]
    n_s = k_slc.shape[2]
    topk = min(N_SELECT, n_s)
    cmp_end = jnp.arange(n_c) * CMP_STRIDE + CMP_LEN - 1
    blk = jnp.arange(n_s)
    qi = jnp.arange(Q_BLOCK)
    ki = jnp.arange(SLIDE_WINDOW + Q_BLOCK)[None, :]
    wdist = SLIDE_WINDOW + qi[:, None] - ki
    wband = (wdist >= 0) & (wdist < SLIDE_WINDOW)
    bi_idx = jnp.arange(b)[:, None, None, None]
    hi_idx = jnp.arange(N_KV_HEADS)[None, :, None, None]
    scale = HEAD_DIM ** -0.5

    def step(bi):
        start = bi * Q_BLOCK
        t = start + qi
        qn = lax.dynamic_slice_in_dim(q_nope, start, Q_BLOCK, axis=3)
        qr = lax.dynamic_slice_in_dim(q_rope, start, Q_BLOCK, axis=3)
        gb = lax.dynamic_slice_in_dim(gates, start, Q_BLOCK, axis=4)
        sc = jnp.einsum('bkgqd,bkcd->bkgqc', qn, k_cmp).astype(jnp.float32) * scale
        p_cmp = masked_softmax(sc, cmp_end[None, :] <= t[:, None])
        o_cmp = jnp.einsum('bkgqc,bkcd->bkgqd', p_cmp.astype(v_cmp.dtype), v_cmp)
        imp = block_importance(jnp.sum(p_cmp, axis=2), n_s)
        jt = (t // SEL_BLOCK)[:, None]
        forced = (blk == 0) | (blk == jt) | (blk == jt - 1)
        imp = jnp.where(blk > jt, -jnp.inf, jnp.where(forced, jnp.inf, imp))
        _, idx = lax.top_k(imp, topk)
        ks = k_slc[bi_idx, hi_idx, idx].reshape(b, N_KV_HEADS, Q_BLOCK, topk * SEL_BLOCK, HEAD_DIM)
        vs = v_slc[bi_idx, hi_idx, idx].reshape(b, N_KV_HEADS, Q_BLOCK, topk * SEL_BLOCK, HEAD_DIM)
        kpos = (idx[..., None] * SEL_BLOCK + jnp.arange(SEL_BLOCK)).reshape(b, N_KV_HEADS, Q_BLOCK, topk * SEL_BLOCK)
        ss = jnp.einsum('bkgqd,bkqnd->bkgqn', qr, ks).astype(jnp.float32) * scale
        p_s = masked_softmax(ss, (kpos <= t[:, None])[:, :, None])
        o_slc = jnp.einsum('bkgqn,bkqnd->bkgqd', p_s.astype(vs.dtype), vs)
        kw = lax.dynamic_slice_in_dim(k_win, start, SLIDE_WINDOW + Q_BLOCK, axis=2)
        vw = lax.dynamic_slice_in_dim(v_win, start, SLIDE_WINDOW + Q_BLOCK, axis=2)
        sw = jnp.einsum('bkgqd,bknd->bkgqn', qr, kw).astype(jnp.float32) * scale
        p_w = masked_softmax(sw, wband & (start - SLIDE_WINDOW + ki >= 0))
        o_win = jnp.einsum('bkgqn,bknd->bkgqd', p_w.astype(vw.dtype), vw)
        o = gb[0][..., None] * o_cmp + gb[1][..., None] * o_slc + gb[2][..., None] * o_win
        return o.astype(u.dtype)

    o = lax.map(step, jnp.arange(s // Q_BLOCK))
    o = o.transpose(1, 0, 4, 2, 3, 5).reshape(b, s, hdim)
    return o @ w_o


def setup_inputs(seed: int = 0) -> dict:
    key = jax.random.key(seed)
    ks = jax.random.split(key, 22)
    n_a = DEPTH // 2
    n_b = DEPTH - n_a
    d = D_MODEL
    hdim = N_HEADS * HEAD_DIM

    def nrm(k, shape, fan_in, gain=1.0):
        return jax.random.normal(k, shape, jnp.float32) * (gain * fan_in ** -0.5)

    def gain_init(k, shape):
        return 1.0 + 0.02 * jax.random.normal(k, shape, jnp.float32)

    return {
        'x': jax.random.normal(ks[0], (BATCH, SEQ, d), jnp.float32),
        'c': jax.random.normal(ks[1], (BATCH, d), jnp.float32),
        'norm_g': gain_init(ks[2], (DEPTH, 3, d)),
        'w_ada': nrm(ks[3], (DEPTH, d, 9 * d), d, 0.1),
        'b_ada': 0.01 * jax.random.normal(ks[4], (DEPTH, 9 * d), jnp.float32),
        'ffn_w_in': nrm(ks[5], (DEPTH, 2, d, 2 * D_FF), d),
        'ffn_w_out': nrm(ks[6], (DEPTH, 2, D_FF, d), D_FF),
        'a_w_qkv': nrm(ks[7], (n_a, d, N_DIL * 3 * hdim), d),
        'a_q_gain': gain_init(ks[8], (n_a, N_DIL, HEAD_DIM)),
        'a_k_gain': gain_init(ks[9], (n_a, N_DIL, HEAD_DIM)),
        'a_w_o': nrm(ks[10], (n_a, hdim, d), hdim),
        'kv_norm_g': gain_init(ks[11], (d,)),
        'w_ada_kv': nrm(ks[12], (d, 2 * d), d, 0.1),
        'b_ada_kv': 0.01 * jax.random.normal(ks[13], (2 * d,), jnp.float32),
        'w_kv': nrm(ks[14], (d, 3 * 2 * N_KV_HEADS * HEAD_DIM), d),
        'kv_k_gain': gain_init(ks[15], (3, HEAD_DIM)),
        'cmp_pos': 0.1 * jax.random.normal(ks[16], (2, CMP_LEN, HEAD_DIM), jnp.float32),
        'phi_w1': nrm(ks[17], (2, CMP_LEN * HEAD_DIM, CMP_HIDDEN), CMP_LEN * HEAD_DIM),
        'phi_w2': nrm(ks[18], (2, CMP_HIDDEN, HEAD_DIM), CMP_HIDDEN),
        'b_w_qg': nrm(ks[19], (n_b, d, hdim + 3 * N_HEADS), d),
        'b_q_gain': gain_init(ks[20], (n_b, HEAD_DIM)),
        'b_w_o': nrm(ks[21], (n_b, hdim, d), hdim),
    }


def reference(x, c, norm_g, w_ada, b_ada, ffn_w_in, ffn_w_out, a_w_qkv, a_q_gain, a_k_gain, a_w_o,
              kv_norm_g, w_ada_kv, b_ada_kv, w_kv, kv_k_gain, cmp_pos, phi_w1, phi_w2,
              b_w_qg, b_q_gain, b_w_o):
    n_a = DEPTH // 2
    bsz = x.shape[0]
    c_act = jax.nn.silu(c)
    h = x
    shared = None
    for l in range(DEPTH):
        mod = (c_act @ w_ada[l] + b_ada[l]).reshape(bsz, 3, 3, D_MODEL)
        sh, sc, gt = _ada(mod, 0)
        h = h + 0.5 * gt * swiglu(modulate(rms_norm(h, norm_g[l, 0]), sh, sc), ffn_w_in[l, 0], ffn_w_out[l, 0])
        sh, sc, gt = _ada(mod, 1)
        u = modulate(rms_norm(h, norm_g[l, 1]), sh, sc)
        if l < n_a:
            y = dilated_mixer(u, a_w_qkv[l], a_q_gain[l], a_k_gain[l], a_w_o[l])
        else:
            j = l - n_a
            y = nsa_mixer(u, *shared, b_w_qg[j], b_q_gain[j], b_w_o[j])
        h = h + gt * y
        sh, sc, gt = _ada(mod, 2)
        h = h + 0.5 * gt * swiglu(modulate(rms_norm(h, norm_g[l, 2]), sh, sc), ffn_w_in[l, 1], ffn_w_out[l, 1])
        if l == n_a - 1:
            kv_mod = (c_act @ w_ada_kv + b_ada_kv).reshape(bsz, 2, D_MODEL)
            shared = shared_kv(h, kv_mod[:, 0][:, None, :], kv_mod[:, 1][:, None, :], kv_norm_g, w_kv,
                               kv_k_gain, cmp_pos, phi_w1, phi_w2)
    return h
```

```python
import numpy as np
import ml_dtypes
import concourse.bass as bass
import concourse.mybir as mybir
from concourse.bass_utils import run_bass_kernel_spmd

F32 = mybir.dt.float32
BF16 = mybir.dt.bfloat16
AF = mybir.ActivationFunctionType
ALU = mybir.AluOpType
AX = mybir.AxisListType

D = 1024
S = 8192
NKC = 8
DFF = 2816
NJ = 22
EPS = 1e-6
NCORES = 4
BIG = 1.0e30


class Stream:
    def __init__(self, name, sem, inc, handle=None):
        self.name, self.sem, self.inc, self.h = name, sem, inc, handle
        self.count = 0
        self.seen = {}
        self.snaps = {}
        self.observed = 0


class Prog:
    def __init__(self, nc, ndma=24):
        self.nc = nc
        self.es = {}
        self._ctx = []
        for name, h in (("pe", nc.tensor), ("act", nc.scalar), ("dve", nc.vector),
                        ("pool", nc.gpsimd), ("sp", nc.sync)):
            cm = nc.semaphore("sem_" + name)
            sem = cm.__enter__()
            self._ctx.append(cm)
            self.es[name] = Stream(name, sem, 1, h)
        self.dmas = []
        for i in range(ndma):
            cm = nc.semaphore("semd%d" % i)
            sem = cm.__enter__()
            self._ctx.append(cm)
            self.dmas.append(Stream("d%d" % i, sem, 16))
        self.streams = dict(self.es)
        for d in self.dmas:
            self.streams[d.name] = d
        self.res = {}
        self.dma_rr = 0
        self.nwaits = 0
        self.nops = 0

    def close(self):
        for cm in reversed(self._ctx):
            cm.__exit__(None, None, None)

    def _deps(self, r, w):
        deps = {}
        def add(sn):
            if sn is None:
                return
            s, n = sn
            if deps.get(s, 0) < n:
                deps[s] = n
        for k in r:
            st = self.res.get(k)
            if st:
                add(st[0])
        for k in w:
            st = self.res.get(k)
            if st:
                add(st[0])
                for s, n in st[1].items():
                    add((s, n))
        return deps

    def _wait(self, E, deps, skip_self=False):
        for s, n in deps.items():
            if skip_self and s == E.name:
                continue
            if E.seen.get(s, 0) >= n:
                continue
            S_ = self.streams[s]
            E.h.wait_ge(S_.sem, n * S_.inc)
            self.nwaits += 1
            E.seen[s] = n
            if S_.observed < n:
                S_.observed = n
            snap = S_.snaps.get(n)
            if snap:
                for k2, v2 in snap.items():
                    if E.seen.get(k2, 0) < v2:
                        E.seen[k2] = v2

    def _commit(self, sname, n, r, w):
        for k in r:
            st = self.res.setdefault(k, [None, {}])
            if st[1].get(sname, 0) < n:
                st[1][sname] = n
        for k in w:
            self.res[k] = [(sname, n), {}]

    def op(self, eng, fn, r=(), w=(), sig=True):
        E = self.es[eng]
        deps = self._deps(r, w)
        self._wait(E, deps, skip_self=(eng == "pe"))
        ins = fn(E.h)
        self.nops += 1
        n = E.count + 1
        if sig:
            ins.then_inc(E.sem, 1)
            E.count = n
            E.snaps[n] = dict(E.seen)
        self._commit(eng, n, r, w)
        return ins

    def dma(self, q, out, in_, r=(), w=()):
        E = self.es[q]
        deps = self._deps(r, w)
        self._wait(E, deps)
        Dm = None
        for i in range(len(self.dmas)):
            c = self.dmas[(self.dma_rr + i) % len(self.dmas)]
            if c.observed >= c.count:
                Dm = c
                self.dma_rr = (self.dma_rr + i + 1) % len(self.dmas)
                break
        if Dm is None:
            Dm = min(self.dmas, key=lambda c: c.count)
        if Dm.count > 0:
            self._wait(E, {Dm.name: Dm.count})
        E.h.dma_start(out=out, in_=in_).then_inc(Dm.sem, 16)
        Dm.count += 1
        Dm.snaps[Dm.count] = dict(E.seen)
        self._commit(Dm.name, Dm.count, r, w)

    def finish(self):
        E = self.es["sp"]
        deps = {}
        for d in self.dmas:
            if d.count > 0:
                deps[d.name] = d.count
        for e in ("pe", "act", "dve", "pool"):
            if self.es[e].count > 0:
                deps[e] = self.es[e].count
        self._wait(E, deps)


class Pool:
    _n = [0]

    def __init__(self, nc):
        self.nc = nc
        self.cms = []
        Pool._n[0] += 1
        self.sfx = "_%d" % Pool._n[0]

    def sb(self, name, shape, dt):
        cm = self.nc.sbuf_tensor(name + self.sfx, list(shape), dt)
        t = cm.__enter__()
        self.cms.append(cm)
        return t

    def ps(self, name, shape, dt=F32):
        cm = self.nc.psum_tensor(name + self.sfx, list(shape), dt)
        t = cm.__enter__()
        self.cms.append(cm)
        return t

    def free(self):
        for cm in reversed(self.cms):
            cm.__exit__(None, None, None)
        self.cms = []


class Builder:
    def __init__(self, stop_after=None, dbg=()):
        self.stop_after = stop_after
        self.dbg = set(dbg)
        self.nc = bass.Bass("TRN2", target_bir_lowering=False)
        self.p = Prog(self.nc)
        self.ins = {}
        self.outs = {}
        self.uid = 0
        self.ntok_dbg = None
        self.dbgG = 1
        self.only_branch = None

    def din(self, name, shape, dt=F32):
        t = self.nc.dram_tensor(name, list(shape), dt, kind="ExternalInput").ap()
        self.ins[name] = t
        return t

    def dout(self, name, shape, dt=F32):
        t = self.nc.dram_tensor(name, list(shape), dt, kind="ExternalOutput").ap()
        self.outs[name] = t
        return t

    def dscr(self, name, shape, dt):
        return self.nc.dram_tensor(name, list(shape), dt, kind="Internal").ap()

    def key(self, base):
        self.uid += 1
        return "%s#%d" % (base, self.uid)

    def declare(self):
        b = self
        b.xT = b.din("xT", [D, S])
        b.c_in = b.din("c_in", [128, NKC])
        b.normg = b.din("normg", [128, 2 * 3 * NKC])
        b.bada = b.din("bada", [128, 2 * 72])
        b.wada = b.din("wada", [2, 128, 72 * NKC * 128])
        b.ffn_in = b.din("ffn_in", [4, 128, 44 * NKC * 128])
        b.ffn_out = b.din("ffn_out", [4, 128, 8 * NJ * 128])
        b.ones_in = b.din("ones_in", [128, 128])
        b.wqk = b.din("wqk", [128, 48 * NKC * 128])
        b.wv = b.din("wv", [128, 3 * NKC * 1024])
        b.wo_a = b.din("wo_a", [128, 8 * 8 * 128])
        b.gains_a = b.din("gains_a", [128, 6])
        b.prot = b.din("prot", [128, 128])
        b.bones = b.din("bones", [128, 128])
        b.cosT = b.din("cosT", [128, S])
        b.sinT = b.din("sinT", [128, S])
        b.mask4 = b.din("mask4", [128, 512])
        b.wada_kv = b.din("wada_kv", [128, 16 * NKC * 128])
        b.bada_kv = b.din("bada_kv", [128, 16])
        b.kvng = b.din("kvng", [128, 8])
        b.wkvf = b.din("wkvf", [128, 4 * NKC * 128])
        b.wkvv = b.din("wkvv", [128, 2 * NKC * 128])
        b.gains_kv = b.din("gains_kv", [128, 4])
        b.w1r = b.din("w1r", [128, 2 * 32 * 128])
        b.posT = b.din("posT", [128, 64])
        b.w2pad = b.din("w2pad", [128, 2 * 2 * 128])
        b.w2v = b.din("w2v", [128, 64])
        b.wqg = b.din("wqg", [128, 8 * NKC * 128])
        b.wgate = b.din("wgate", [128, NKC * 48])
        b.wo_b = b.din("wo_b", [128, 8 * 8 * 128])
        b.Ex = b.din("Ex", [128, S])
        b.pats = b.din("pats", [128, 16])
        b.maskgt = b.din("maskgt", [128, 128])
        b.ident = b.din("ident", [128, 128])
        b.out = b.dout("outT", [D, S])
        b.wkvf_b = b.dscr("wkvf_b", [128, 4 * NKC * 128], BF16)
        b.wkvv_b = b.dscr("wkvv_b", [128, 2 * NKC * 128], BF16)
        b.w1r_b = b.dscr("w1r_b", [128, 2 * 32 * 128], BF16)
        b.wqg_b = b.dscr("wqg_b", [128, 8 * NKC * 128], BF16)
        b.wgate_b = b.dscr("wgate_b", [128, NKC * 48], BF16)
        b.wo_b_b = b.dscr("wo_b_b", [128, 8 * 8 * 128], BF16)
        b.Ex_b = b.dscr("Ex_b", [128, S], BF16)
        b.KS = b.dscr("KS", [2, 128, S], BF16)
        b.VS = b.dscr("VS", [2, S, 128], BF16)
        b.wqk_b = b.dscr("wqk_b", [128, 48 * NKC * 128], BF16)
        b.wv_b = b.dscr("wv_b", [128, 3 * NKC * 1024], BF16)
        b.wo_a_b = b.dscr("wo_a_b", [128, 8 * 8 * 128], BF16)
        b.QT = b.dscr("QT", [3, 8, 128, S], BF16)
        b.KT = b.dscr("KT", [3, 8, 128, S], BF16)
        b.V = b.dscr("V", [3, S, 1024], BF16)
        b.H = b.dscr("hres_scratch", [D, S], F32)
        b.ffn_in_b = b.dscr("ffn_in_b", [4, 128, 44 * NKC * 128], BF16)
        b.ffn_out_b = b.dscr("ffn_out_b", [4, 128, 8 * NJ * 128], BF16)

    def convert_weights(self):
        p = self.p
        for i in range(4):
            n = 4096
            for jg in range(11):
                p.dma("pool", self.ffn_in_b[i, :, jg * n:(jg + 1) * n], self.ffn_in[i, :, jg * n:(jg + 1) * n],
                      w=[("ffn_in_b", i, jg)])
            n = NJ * 128
            for o in range(8):
                p.dma("pool", self.ffn_out_b[i, :, o * n:(o + 1) * n], self.ffn_out[i, :, o * n:(o + 1) * n],
                      w=[("ffn_out_b", i, o)])

    def convert_weights2(self):
        p = self.p
        n = NKC * 128 * 8
        for i in range(6):
            p.dma("pool", self.wqk_b[:, i * n:(i + 1) * n], self.wqk[:, i * n:(i + 1) * n], w=[("wqk_b", i)])
        n = NKC * 1024
        for i in range(3):
            p.dma("pool", self.wv_b[:, i * n:(i + 1) * n], self.wv[:, i * n:(i + 1) * n], w=[("wv_b", i)])
        p.dma("pool", self.wo_a_b[:, :], self.wo_a[:, :], w=["wo_a_b"])
        for nm in ("wkvf", "wkvv", "w1r", "wqg", "wgate", "wo_b", "Ex"):
            p.dma("pool", getattr(self, nm + "_b")[:, :], getattr(self, nm)[:, :], w=[nm + "_b"])

    def consts(self):
        b, p, nc = self, self.p, self.nc
        P = self.cpool = Pool(nc)
        b.ones_f = P.sb("ones_f", [128, 128], F32)
        b.ones_b = P.sb("ones_b", [128, 128], BF16)
        b.mod = P.sb("mod", [128, 2 * 72 + 16], F32)
        b.ng = P.sb("ng", [128, 56], F32)
        b.cact = P.sb("cact", [128, NKC], F32)
        b.Amod = P.sb("Amod", [128, 56], F32)
        b.Gmod = P.sb("Gmod", [128, 48], F32)
        b.bones_b = P.sb("bones_b", [128, 128], BF16)
        b.prot_f = P.sb("prot_f", [128, 128], F32)
        b.pg = P.sb("pg", [128, 6 * 128], BF16)
        b.ga = P.sb("ga", [128, 6], F32)
        b.mask4_b = P.sb("mask4_b", [128, 512], BF16)
        p.dma("sp", b.mod[:, 144:160], b.bada_kv[:, :], w=["mod"])
        b.gkv = P.sb("gkv", [128, 4], F32)
        p.dma("sp", b.gkv[:], b.gains_kv[:, :], w=["gkv"])
        b.pgk = P.sb("pgk", [128, 4 * 128], BF16)
        b.ident_f = P.sb("ident_f", [128, 128], F32)
        b.ident_b = P.sb("ident_b", [128, 128], BF16)
        b.mgt_b = P.sb("mgt_b", [128, 128], BF16)
        b.pats_f = P.sb("pats_f", [128, 16], F32)
        b.posT_f = P.sb("posT_f", [128, 64], F32)
        b.posT_b = P.sb("posT_b", [128, 64], BF16)
        b.w2pad_b = P.sb("w2pad_b", [128, 512], BF16)
        b.w2v_b = P.sb("w2v_b", [128, 64], BF16)
        b.kcT = P.sb("kcT", [128, 512], BF16)
        b.vca = P.sb("vca", [128, 4, 192], BF16)
        p.dma("sp", b.ident_f[:], b.ident[:, :], w=["ident_f"])
        p.dma("sp", b.pats_f[:], b.pats[:, :], w=["pats_f"])
        p.dma("sp", b.posT_f[:], b.posT[:, :], w=["posT_f"])
        stg = P.sb("stg", [128, 512], F32)
        p.dma("sp", stg[:, 0:128], b.bones[:, :], w=["stg"])
        p.op("dve", lambda e: e.tensor_copy(out=b.bones_b[:], in_=stg[:, 0:128]), r=["stg"], w=["bones_b"])
        p.dma("sp", stg[:], b.mask4[:, :], w=["stg"])
        p.op("dve", lambda e: e.tensor_copy(out=b.mask4_b[:], in_=stg[:]), r=["stg"], w=["mask4_b"])
        p.dma("sp", b.prot_f[:], b.prot[:, :], w=["prot_f"])
        p.dma("sp", b.ga[:], b.gains_a[:, :], w=["ga"])
        for i in range(1, 4):
            p.op("dve", lambda e, i=i: e.tensor_scalar(out=b.pgk[:, i * 128:(i + 1) * 128], in0=b.prot_f[:],
                                                       scalar1=b.gkv[:, i:i + 1], scalar2=None, op0=ALU.mult),
                 r=["prot_f", "gkv"], w=["pgk"])
        p.op("dve", lambda e: e.tensor_copy(out=b.ident_b[:], in_=b.ident_f[:]), r=["ident_f"], w=["ident_b"])
        p.op("dve", lambda e: e.tensor_copy(out=b.posT_b[:], in_=b.posT_f[:]), r=["posT_f"], w=["posT_b"])
        p.dma("sp", stg[:, 0:128], b.maskgt[:, :], r=["mask4_b"], w=["stg"])
        p.op("dve", lambda e: e.tensor_copy(out=b.mgt_b[:], in_=stg[:, 0:128]), r=["stg"], w=["mgt_b"])
        p.dma("sp", stg[:], b.w2pad[:, :], r=["mgt_b"], w=["stg"])
        p.op("dve", lambda e: e.tensor_copy(out=b.w2pad_b[:], in_=stg[:]), r=["stg"], w=["w2pad_b"])
        p.dma("sp", stg[:, 0:64], b.w2v[:, :], r=["w2pad_b"], w=["stg"])
        p.op("dve", lambda e: e.tensor_copy(out=b.w2v_b[:], in_=stg[:, 0:64]), r=["stg"], w=["w2v_b"])
        for i in range(6):
            p.op("dve", lambda e, i=i: e.tensor_scalar(out=b.pg[:, i * 128:(i + 1) * 128], in0=b.prot_f[:],
                                                       scalar1=b.ga[:, i:i + 1], scalar2=None, op0=ALU.mult),
                 r=["prot_f", "ga"], w=["pg"])
        p.dma("sp", b.ones_f[:], b.ones_in[:, :], w=["ones_f"])
        p.dma("sp", b.ng[:, 0:48], b.normg[:, :], w=["ng"])
        p.dma("sp", b.ng[:, 48:56], b.kvng[:, :], w=["ng"])
        p.dma("sp", b.cact[:], b.c_in[:, :], w=["cact"])
        p.dma("sp", b.mod[:, 0:144], b.bada[:, :], w=["mod"])
        p.op("dve", lambda e: e.tensor_copy(out=b.ones_b[:], in_=b.ones_f[:]), r=["ones_f"], w=["ones_b"])
        p.op("act", lambda e: e.activation(out=b.cact[:], in_=b.cact[:], func=AF.Silu), r=["cact"], w=["cact"])
        T = Pool(nc)
        wbuf = [T.sb("wada%d" % i, [128, 8 * NKC * 128], F32) for i in range(2)]
        mps = T.ps("mod_ps", [128, 512])
        gi = 0
        for l in range(2):
            for jg in range(9):
                wb = wbuf[gi % 2]
                wk = ("wada", gi % 2)
                n = 8 * NKC * 128
                p.dma("sp" if gi % 2 == 0 else "act", wb[:], b.wada[l, :, jg * n:(jg + 1) * n], w=[wk])
                for jj in range(8):
                    j = jg * 8 + jj
                    for kc in range(NKC):
                        o = (jj * NKC + kc) * 128
                        p.op("pe", lambda e, wb=wb, o=o, kc=kc, col=l * 72 + j: e.matmul(
                            mps[:, col:col + 1], lhsT=wb[:, o:o + 128], rhs=b.cact[:, kc:kc + 1],
                            start=(kc == 0), stop=(kc == NKC - 1)),
                            r=[wk, "cact"], w=["mod_ps"], sig=(kc == NKC - 1))
                gi += 1
        for jg in range(2):
            wb = wbuf[gi % 2]
            wk = ("wada", gi % 2)
            n = 8 * NKC * 128
            p.dma("sp" if gi % 2 == 0 else "act", wb[:], b.wada_kv[:, jg * n:(jg + 1) * n], w=[wk])
            for jj in range(8):
                for kc in range(NKC):
                    o = (jj * NKC + kc) * 128
                    p.op("pe", lambda e, wb=wb, o=o, kc=kc, col=144 + jg * 8 + jj: e.matmul(
                        mps[:, col:col + 1], lhsT=wb[:, o:o + 128], rhs=b.cact[:, kc:kc + 1],
                        start=(kc == 0), stop=(kc == NKC - 1)),
                        r=[wk, "cact"], w=["mod_ps"], sig=(kc == NKC - 1))
            gi += 1
        p.op("dve", lambda e: e.tensor_tensor(out=b.mod[:], in0=mps[:, 0:160], in1=b.mod[:], op=ALU.add),
             r=["mod_ps", "mod"], w=["mod"])
        p.op("dve", lambda e: e.tensor_scalar(out=b.Amod[:, 48:56], in0=b.mod[:, 152:160], scalar1=1.0, scalar2=1.0,
                                              op0=ALU.add, op1=ALU.mult), r=["mod"], w=["Amod"])
        p.op("dve", lambda e: e.tensor_tensor(out=b.Amod[:, 48:56], in0=b.Amod[:, 48:56], in1=b.ng[:, 48:56],
                                              op=ALU.mult), r=["Amod", "ng"], w=["Amod"])
        for l in range(2):
            for s in range(3):
                c0 = (l * 3 + s) * NKC
                m0 = l * 72 + s * 24
                sc = b.mod[:, m0 + 8:m0 + 16]
                gt = b.mod[:, m0 + 16:m0 + 24]
                p.op("dve", lambda e, c0=c0, sc=sc: e.tensor_scalar(
                    out=b.Amod[:, c0:c0 + 8], in0=sc, scalar1=1.0, scalar2=1.0, op0=ALU.add, op1=ALU.mult),
                    r=["mod"], w=["Amod"])
                p.op("dve", lambda e, c0=c0: e.tensor_tensor(
                    out=b.Amod[:, c0:c0 + 8], in0=b.Amod[:, c0:c0 + 8], in1=b.ng[:, c0:c0 + 8], op=ALU.mult),
                    r=["Amod", "ng"], w=["Amod"])
                f = 1.0 if s == 1 else 0.5
                p.op("dve", lambda e, c0=c0, gt=gt, f=f: e.tensor_scalar(
                    out=b.Gmod[:, c0:c0 + 8], in0=gt, scalar1=1.0, scalar2=f, op0=ALU.add, op1=ALU.mult),
                    r=["mod"], w=["Gmod"])
        self.phase_end(T, [("wada", 0), ("wada", 1), "mod_ps"])

    def norm_tile(self, P, ht, hk, ut, uk, ssq, ss_ps, rstd, tmp, ls, TT):
        b, p = self, self.p
        for kc in range(NKC):
            p.op("act", lambda e, kc=kc: e.activation(out=ssq[kc % 2][:], in_=ht[:, kc, :], func=AF.Square),
                 r=[hk], w=[("ssq", kc % 2)])
            p.op("pe", lambda e, kc=kc: e.matmul(ss_ps[:], lhsT=b.ones_b[:], rhs=ssq[kc % 2][:],
                                                  start=(kc == 0), stop=(kc == NKC - 1)),
                 r=[("ssq", kc % 2), "ones_b"], w=["ss_ps"])
        p.op("act", lambda e: e.activation(out=rstd[:], in_=ss_ps[:], func=AF.Sqrt, scale=1.0 / D, bias=EPS),
             r=["ss_ps"], w=["rstd"])
        p.op("dve", lambda e: e.reciprocal(out=rstd[:], in_=rstd[:]), r=["rstd"], w=["rstd"])
        l, s = divmod(ls, 3)
        m0 = l * 72 + s * 24
        for kc in range(NKC):
            p.op("dve", lambda e, kc=kc: e.tensor_tensor(out=tmp[kc % 2][:], in0=ht[:, kc, :], in1=rstd[:],
                                                         op=ALU.mult), r=[hk, "rstd"], w=[("ntmp", kc % 2)])
            p.op("pool", lambda e, kc=kc: e.tensor_scalar(
                out=ut[:, kc, :], in0=tmp[kc % 2][:], scalar1=b.Amod[:, ls * 8 + kc:ls * 8 + kc + 1],
                scalar2=b.mod[:, m0 + kc:m0 + kc + 1], op0=ALU.mult, op1=ALU.add),
                r=[("ntmp", kc % 2), "Amod", "mod"], w=[uk])

    def ffn_phase(self, src, dst, widx, ls, ntok, TT=512):
        b, p, nc = self, self.p, self.nc
        P = Pool(nc)
        u = b.key("ffn")
        ht = [P.sb("ht%d" % i, [128, NKC, TT], F32) for i in range(2)]
        ut = [P.sb("ut%d" % i, [128, NKC, TT], BF16) for i in range(2)]
        at = P.sb("at", [128, NJ, TT], BF16)
        ssq = [P.sb("ssq%d" % i, [128, TT], BF16) for i in range(2)]
        tmp = [P.sb("ntmp%d" % i, [128, TT], F32) for i in range(2)]
        rstd = P.sb("rstd", [128, TT], F32)
        sg = [P.sb("sg%d" % i, [128, TT], F32) for i in range(2)]
        GJ = 2
        NWI = 4
        wi = [P.sb("wi%d" % i, [128, GJ * 2 * NKC * 128], BF16) for i in range(NWI)]
        NWO = 4
        wo = [P.sb("wo%d" % i, [128, NJ * 128], BF16) for i in range(NWO)]
        ss_ps = P.ps("ss_ps", [128, TT])
        g_ps = [P.ps("g_ps%d" % i, [128, TT]) for i in range(2)]
        u_ps = [P.ps("u_ps%d" % i, [128, TT]) for i in range(2)]
        o_ps = [P.ps("o_ps%d" % i, [128, TT]) for i in range(2)]
        srcv = src.rearrange("(kc p) t -> p kc t", p=128)
        dstv = dst.rearrange("(kc p) t -> p kc t", p=128)
        nt = ntok // TT
        NG = NJ // GJ
        items = []
        for tt in range(nt):
            for jg in range(NG):
                items.append(("wi", jg))
            for o_ in range(NKC):
                items.append(("wo", o_))
        cnt = {"wi": 0, "wo": 0}
        slot = []
        for kind, idx in items:
            slot.append(cnt[kind] % (NWI if kind == "wi" else NWO))
            cnt[kind] += 1
        state = {"issued": 0}
        LEAD = 3

        def ensure(k):
            while state["issued"] <= min(k, len(items) - 1):
                i = state["issued"]
                kind, idx = items[i]
                if kind == "wi":
                    n = GJ * 2 * NKC * 128
                    p.dma("sp", wi[slot[i]][:], b.ffn_in_b[widx, :, idx * n:(idx + 1) * n],
                          r=[("ffn_in_b", widx, idx)], w=[("wi", slot[i])])
                else:
                    n = NJ * 128
                    p.dma("sp", wo[slot[i]][:], b.ffn_out_b[widx, :, idx * n:(idx + 1) * n],
                          r=[("ffn_out_b", widx, idx)], w=[("wo", slot[i])])
                state["issued"] += 1

        def load_h(tt):
            p.dma("sp", ht[tt % 2][:], srcv[:, :, tt * TT:(tt + 1) * TT], r=["H"], w=[("ht", tt % 2)])

        def norm(tt):
            self.norm_tile(P, ht[tt % 2], ("ht", tt % 2), ut[tt % 2], ("ut", tt % 2), ssq, ss_ps, rstd, tmp, ls, TT)

        load_h(0)
        ensure(LEAD)
        norm(0)
        it = 0
        for tt in range(nt):
            hb = ht[tt % 2]
            hk = ("ht", tt % 2)
            ub = ut[tt % 2]
            uk = ("ut", tt % 2)
            if tt + 1 < nt:
                load_h(tt + 1)
            for jg in range(NG):
                ensure(it + LEAD)
                wb = wi[slot[it]]
                wk = ("wi", slot[it])
                it += 1
                for j2 in range(GJ):
                    jj = jg * GJ + j2
                    pi = jj % 2
                    for half, pst, pk in ((0, g_ps[pi], ("g_ps", pi)), (1, u_ps[pi], ("u_ps", pi))):
                        for kc in range(NKC):
                            o = ((j2 * 2 + half) * NKC + kc) * 128
                            p.op("pe", lambda e, pst=pst, o=o, kc=kc, wb=wb: e.matmul(
                                pst[:], lhsT=wb[:, o:o + 128], rhs=ub[:, kc, :],
                                start=(kc == 0), stop=(kc == NKC - 1)),
                                r=[wk, uk], w=[pk], sig=(kc == NKC - 1))
                    p.op("act", lambda e, pi=pi: e.activation(out=sg[pi][:], in_=g_ps[pi][:], func=AF.Silu),
                         r=[("g_ps", pi)], w=[("sg", pi)])
                    p.op("dve", lambda e, pi=pi, jj=jj: e.tensor_tensor(
                        out=at[:, jj, :], in0=u_ps[pi][:], in1=sg[pi][:], op=ALU.mult),
                        r=[("u_ps", pi), ("sg", pi)], w=[("at", jj)])
            if tt + 1 < nt:
                norm(tt + 1)
            for o_ in range(NKC):
                ensure(it + LEAD)
                wb = wo[slot[it]]
                wk = ("wo", slot[it])
                it += 1
                pi = o_ % 2
                for kc in range(NJ):
                    p.op("pe", lambda e, pi=pi, kc=kc, wb=wb: e.matmul(
                        o_ps[pi][:], lhsT=wb[:, kc * 128:(kc + 1) * 128], rhs=at[:, kc, :],
                        start=(kc == 0), stop=(kc == NJ - 1)),
                        r=[wk, ("at", kc)], w=[("o_ps", pi)], sig=(kc == NJ - 1))
                p.op("dve", lambda e, pi=pi, o_=o_, hb=hb: e.scalar_tensor_tensor(
                    out=hb[:, o_, :], in0=o_ps[pi][:], scalar=b.Gmod[:, ls * 8 + o_:ls * 8 + o_ + 1],
                    in1=hb[:, o_, :], op0=ALU.mult, op1=ALU.add),
                    r=[("o_ps", pi), hk, "Gmod"], w=[hk])
            p.dma("pool", dstv[:, :, tt * TT:(tt + 1) * TT], hb[:], r=[hk], w=["H"])
        self.phase_end(P, [("ht", 0), ("ht", 1), ("ut", 0), ("ut", 1), "ss_ps", "rstd"]
                       + [("at", j) for j in range(NJ)]
                       + [("wi", i) for i in range(NWI)] + [("wo", i) for i in range(NWO)]
                       + [(n_, i) for n_ in ("g_ps", "u_ps", "o_ps", "sg", "ssq", "ntmp") for i in range(2)])

    def qkv_phase(self, ntok, TT=512):
        b, p, nc = self, self.p, self.nc
        P = Pool(nc)
        ls = 1
        ht = [P.sb("ht%d" % i, [128, NKC, TT], F32) for i in range(2)]
        ut = [P.sb("ut%d" % i, [128, NKC, TT], BF16) for i in range(2)]
        ssq = [P.sb("ssq%d" % i, [128, TT], BF16) for i in range(2)]
        tmp = [P.sb("ntmp%d" % i, [128, TT], F32) for i in range(2)]
        rstd = P.sb("rstd", [128, TT], F32)
        cs = [P.sb("cs%d" % i, [128, 2, TT], F32) for i in range(2)]
        wv = P.sb("wv", [128, 3, NKC, 1024], BF16)
        NW = 3
        wq = [P.sb("wq%d" % i, [128, 8 * NKC * 128], BF16) for i in range(NW)]
        sq = [P.sb("sq%d" % i, [128, TT], BF16) for i in range(2)]
        xb = [P.sb("xb%d" % i, [128, TT], BF16) for i in range(2)]
        r2 = [P.sb("r2%d" % i, [128, TT], F32) for i in range(2)]
        t1 = [P.sb("t1%d" % i, [128, TT], F32) for i in range(2)]
        t2 = [P.sb("t2%d" % i, [128, TT], F32) for i in range(2)]
        qo = [P.sb("qo%d" % i, [128, TT], BF16) for i in range(3)]
        vo = [P.sb("vo%d" % i, [128, 1024], BF16) for i in range(2)]
        ss_ps = P.ps("ss_ps", [128, TT])
        x_ps = [P.ps("x_ps%d" % i, [128, TT]) for i in range(2)]
        s2_ps = P.ps("s2_ps", [128, TT])
        rot_ps = P.ps("rot_ps", [128, TT])
        v_ps = [P.ps("v_ps%d" % i, [128, 512]) for i in range(2)]
        srcv = b.H.rearrange("(kc p) t -> p kc t", p=128)
        nt = ntok // TT
        print("qkv sbuf remaining", nc.sbuf_bytes_remaining)
        import os
        B2 = os.environ.get("BIS2", "")
        for g in range(0 if "w" in B2 else 3):
            p.dma("sp", wv[:, g, :, :], b.wv_b[:, g * NKC * 1024:(g + 1) * NKC * 1024].rearrange(
                "p (kc n) -> p kc n", kc=NKC), r=[("wv_b", g)], w=[("wv", g)])
        wc = 0
        tc_ = 0
        vc = 0
        p.dma("sp", ht[0][:], srcv[:, :, 0:TT], r=["H"], w=[("ht", 0)])
        for tt in range(nt):
            hb, hk, ub, uk = ht[tt % 2], ("ht", tt % 2), ut[tt % 2], ("ut", tt % 2)
            csb, ck = cs[tt % 2], ("cs", tt % 2)
            if "c" not in B2:
                p.dma("sp", csb[:, 0, :], b.cosT[:, tt * TT:(tt + 1) * TT], w=[ck])
                p.dma("sp", csb[:, 1, :], b.sinT[:, tt * TT:(tt + 1) * TT], w=[ck])
            if "n" not in B2:
                self.norm_tile(P, hb, hk, ub, uk, ssq, ss_ps, rstd, tmp, ls, TT)
            if tt + 1 < nt:
                p.dma("sp", ht[(tt + 1) % 2][:], srcv[:, :, (tt + 1) * TT:(tt + 2) * TT], r=["H"],
                      w=[("ht", (tt + 1) % 2)])
            import os
            BIS = int(os.environ.get('BIS', '0'))
            for wg in range(0 if BIS in (2, 3) else 6):
                g, qk = divmod(wg, 2)
                wb, wk = wq[wc % NW], ("wq", wc % NW)
                wc += 1
                n = 8 * NKC * 128
                p.dma("sp", wb[:], b.wqk_b[:, wg * n:(wg + 1) * n], r=[("wqk_b", wg)], w=[wk])
                for hp in range(8):
                    i2 = tc_ % 2
                    i3 = tc_ % 3
                    tc_ += 1
                    xp, xk = x_ps[i2], ("x_ps", i2)
                    for kc in range(NKC):
                        o = (hp * NKC + kc) * 128
                        p.op("pe", lambda e, xp=xp, o=o, kc=kc, wb=wb: e.matmul(
                            xp[:], lhsT=wb[:, o:o + 128], rhs=ub[:, kc, :], start=(kc == 0), stop=(kc == NKC - 1)),
                            r=[wk, uk], w=[xk], sig=(kc == NKC - 1))
                    p.op("act", lambda e, i2=i2, xp=xp: e.activation(out=sq[i2][:], in_=xp[:], func=AF.Square),
                         r=[xk], w=[("sq", i2)])
                    p.op("act", lambda e, i2=i2, xp=xp: e.activation(out=xb[i2][:], in_=xp[:], func=AF.Copy),
                         r=[xk], w=[("xb", i2)])
                    p.op("pe", lambda e, i2=i2: e.matmul(s2_ps[:], lhsT=b.bones_b[:], rhs=sq[i2][:],
                                                         start=True, stop=True),
                         r=[("sq", i2), "bones_b"], w=["s2_ps"])
                    p.op("pe", lambda e, i2=i2, wg=wg: e.matmul(rot_ps[:], lhsT=b.pg[:, wg * 128:(wg + 1) * 128],
                                                                 rhs=xb[i2][:], start=True, stop=True),
                         r=[("xb", i2), "pg"], w=["rot_ps"])
                    p.op("act", lambda e, i2=i2: e.activation(out=r2[i2][:], in_=s2_ps[:], func=AF.Sqrt,
                                                              scale=1.0 / 64, bias=EPS),
                         r=["s2_ps"], w=[("r2", i2)])
                    p.op("dve", lambda e, i2=i2: e.reciprocal(out=r2[i2][:], in_=r2[i2][:]),
                         r=[("r2", i2)], w=[("r2", i2)])
                    p.op("dve", lambda e, i2=i2, xp=xp, wg=wg: e.scalar_tensor_tensor(
                        out=t1[i2][:], in0=xp[:], scalar=b.ga[:, wg:wg + 1], in1=csb[:, 0, :],
                        op0=ALU.mult, op1=ALU.mult), r=[xk, ck, "ga"], w=[("t1", i2)])
                    p.op("dve", lambda e, i2=i2: e.tensor_tensor(out=t2[i2][:], in0=rot_ps[:], in1=csb[:, 1, :],
                                                                 op=ALU.mult),
                         r=["rot_ps", ck], w=[("t2", i2)])
                    p.op("pool", lambda e, i2=i2: e.tensor_tensor(out=t1[i2][:], in0=t1[i2][:], in1=t2[i2][:],
                                                                  op=ALU.add),
                         r=[("t1", i2), ("t2", i2)], w=[("t1", i2)])
                    p.op("pool", lambda e, i2=i2, i3=i3: e.tensor_tensor(out=qo[i3][:], in0=t1[i2][:],
                                                                         in1=r2[i2][:], op=ALU.mult),
                         r=[("t1", i2), ("r2", i2)], w=[("qo", i3)])
                    dst = (b.QT if qk == 0 else b.KT)[g, hp, :, tt * TT:(tt + 1) * TT]
                    p.dma("sp", dst, qo[i3][:], r=[("qo", i3)], w=[("QK", g, qk, hp, tt)])
            for blk in range(0 if BIS in (1, 3) else TT // 128):
                for g in range(3):
                    vb, vk = vo[vc % 2], ("vo", vc % 2)
                    vc += 1
                    for hf in range(2):
                        vp, vpk = v_ps[hf], ("v_ps", hf)
                        for kc in range(NKC):
                            p.op("pe", lambda e, vp=vp, kc=kc, g=g, hf=hf, blk=blk: e.matmul(
                                vp[:], lhsT=ub[:, kc, blk * 128:(blk + 1) * 128],
                                rhs=wv[:, g, kc, hf * 512:(hf + 1) * 512], start=(kc == 0), stop=(kc == NKC - 1)),
                                r=[("wv", g), uk], w=[vpk], sig=(kc == NKC - 1))
                        p.op("act", lambda e, vp=vp, vb=vb, hf=hf: e.activation(
                            out=vb[:, hf * 512:(hf + 1) * 512], in_=vp[:], func=AF.Copy), r=[vpk], w=[vk])
                    t0 = tt * TT + blk * 128
                    p.dma("sp", b.V[g, t0:t0 + 128, :], vb[:], r=[vk], w=[("V", g, t0 // 128)])
        keys = [("ht", 0), ("ht", 1), ("ut", 0), ("ut", 1), "ss_ps", "rstd", "s2_ps", "rot_ps",
                ("wv", 0), ("wv", 1), ("wv", 2)]
        keys += [(n_, i) for n_ in ("ssq", "ntmp", "cs", "sq", "xb", "r2", "t1", "t2", "x_ps", "v_ps", "vo")
                 for i in range(2)]
        keys += [("wq", i) for i in range(NW)] + [("qo", i) for i in range(3)]
        self.phase_end(P, keys)

    def attn_a_phase(self, ntok):
        b, p, nc = self, self.p, self.nc
        P = Pool(nc)
        SP = 2048
        RATES = (1, 4, 16)
        kt = P.sb("kt", [128, 3, 2 * SP], BF16)
        qt = P.sb("qt", [128, 3, SP], BF16)
        NVB = 17 + 20 + 32
        va = P.sb("va", [128, NVB, 192], BF16)
        pt = [P.sb("pt%d" % i, [128, 512], BF16) for i in range(2)]
        ot = P.sb("ot", [128, 8, SP], BF16)
        rden = P.sb("rden", [128, SP], F32)
        wo = P.sb("wo", [128, 8, 8, 128], BF16)
        ht = P.sb("ht", [128, NKC, 512], F32)
        acc = P.ps("acc", [128, SP])
        s_ps = [P.ps("s_ps%d" % i, [128, 512]) for i in range(2)]
        y_ps = [P.ps("y_ps%d" % i, [128, 512]) for i in range(2)]
        Hv = b.H.rearrange("(kc p) t -> p kc t", p=128)
        p.dma("sp", wo[:], b.wo_a_b.rearrange("p (o hp m) -> p o hp m", o=8, hp=8), r=["wo_a_b"], w=["wo"])
        p.op("pool", lambda e: e.memset(va[:, :, 64:128], 1.0), w=["va"])
        ls = 1
        uc = 0
        for sp in range(ntok // SP):
            t0 = sp * SP
            for hp in range(8):
                for g in range(3):
                    if sp > 0:
                        p.dma("sp", kt[:, g, :], b.KT[g, hp, :, t0 - SP:t0 + SP],
                              r=[("QK", g, 1, hp, tt) for tt in range((t0 - SP) // 512, (t0 + SP) // 512)], w=["kt"])
                    else:
                        p.dma("sp", kt[:, g, SP:], b.KT[g, hp, :, 0:SP],
                              r=[("QK", g, 1, hp, tt) for tt in range(0, SP // 512)], w=["kt"])
                    p.dma("sp", qt[:, g, :], b.QT[g, hp, :, t0:t0 + SP],
                          r=[("QK", g, 0, hp, tt) for tt in range(t0 // 512, (t0 + SP) // 512)], w=["qt"])
                vidx = {}
                vi = 0
                for g, r in enumerate(RATES):
                    nq = SP // (128 * r)
                    for res in range(r):
                        j0 = -1 if sp > 0 else 0
                        nb = nq - j0
                        Vg = b.V[g]
                        tstart = ((sp * nq + j0) * 128) * r + res
                        for half in range(2):
                            tb_ = tstart - res
                            src = Vg[tb_:tb_ + nb * 128 * r, hp * 128 + half * 64: hp * 128 + half * 64 + 64]
                            src = src.rearrange("(jb i r) d -> i jb r d", i=128, r=r)[:, :, res, :]
                            dstc = 0 if half == 0 else 128
                            p.dma("act" if half else "sp", va[:, vi:vi + nb, dstc:dstc + 64], src,
                                  r=[("V", g, tb) for tb in range(tstart // 128, (tstart + nb * 128 * r + 127) // 128)],
                                  w=["va"])
                        for jb in range(j0, nq):
                            vidx[(g, res, jb)] = vi
                            vi += 1
                for a in range(2):
                    rows = slice(0, 64) if a == 0 else slice(64, 128)
                    vcols = slice(0, 128) if a == 0 else slice(64, 192)
                    started = [False] * 4
                    units = []
                    for g, r in enumerate(RATES):
                        nq = SP // (128 * r)
                        for res in range(r):
                            for j in range(nq):
                                units.append((g, r, res, j))
                    for u0 in range(0, len(units), 2):
                        pi = uc % 2
                        uc += 1
                        sp_, sk = s_ps[pi], ("s_ps", pi)
                        pb, pk = pt[pi], ("pt", pi)
                        for ui in range(2):
                            g, r, res, j = units[u0 + ui]
                            qs = res + j * 128 * r
                            qap = qt[rows, g, qs:qs + 127 * r + 1:r]
                            for kb in range(2):
                                jb = j - 1 + kb
                                col = ui * 256 + kb * 128
                                if (g, res, jb) not in vidx:
                                    p.op("dve", lambda e, sp_=sp_, col=col: e.memset(sp_[:, col:col + 128], -100.0),
                                         w=[sk])
                                    continue
                                ks = SP + res + jb * 128 * r
                                kap = kt[rows, g, ks:ks + 127 * r + 1:r]
                                p.op("pe", lambda e, sp_=sp_, col=col, kap=kap, qap=qap: e.matmul(
                                    sp_[:, col:col + 128], lhsT=kap, rhs=qap, start=True, stop=True),
                                    r=["kt", "qt"], w=[sk])
                        p.op("act", lambda e, sp_=sp_, pb=pb: e.activation(out=pb[:], in_=sp_[:], func=AF.Exp,
                                                                          scale=0.125),
                             r=[sk], w=[pk])
                        p.op("pool", lambda e, pb=pb: e.tensor_tensor(out=pb[:], in0=pb[:], in1=b.mask4_b[:],
                                                                      op=ALU.mult),
                             r=[pk, "mask4_b"], w=[pk])
                        for ui in range(2):
                            g, r, res, j = units[u0 + ui]
                            qs = res + j * 128 * r
                            for kb in range(2):
                                jb = j - 1 + kb
                                if (g, res, jb) not in vidx:
                                    continue
                                col = ui * 256 + kb * 128
                                bank = qs // 512
                                nsplit = 4 if r == 16 else 1
                                nq_ = 128 // nsplit
                                for qq in range(nsplit):
                                    q0 = qs + qq * nq_ * r
                                    bank = q0 // 512
                                    oap = acc[:, q0:q0 + (nq_ - 1) * r + 1:r]
                                    st = not started[bank]
                                    started[bank] = True
                                    c0 = col + qq * nq_
                                    p.op("pe", lambda e, oap=oap, v=vidx[(g, res, jb)], pb=pb, c0=c0, st=st, nq_=nq_:
                                         e.matmul(oap, lhsT=va[:, v, vcols], rhs=pb[:, c0:c0 + nq_], start=st,
                                                  stop=False),
                                         r=["va", pk], w=["acc"])
                    oth = slice(64, 128) if a == 0 else slice(0, 64)
                    for bk in range(4):
                        cs_ = slice(bk * 512, (bk + 1) * 512)
                        p.op("dve", lambda e, cs_=cs_: e.reciprocal(out=rden[rows, cs_], in_=acc[oth, cs_]),
                             r=["acc"], w=["rden"])
                        p.op("dve", lambda e, hp=hp, cs_=cs_: e.tensor_tensor(
                            out=ot[rows, hp, cs_], in0=acc[rows, cs_], in1=rden[rows, cs_], op=ALU.mult),
                            r=["acc", "rden"], w=[("ot", hp)])
            for tq in range(SP // 512):
                tk0 = t0 + tq * 512
                p.dma("sp", ht[:], Hv[:, :, tk0:tk0 + 512], r=["H"], w=["ht"])
                for o_ in range(8):
                    pi = o_ % 2
                    for hp in range(8):
                        p.op("pe", lambda e, pi=pi, o_=o_, hp=hp, tq=tq: e.matmul(
                            y_ps[pi][:], lhsT=wo[:, o_, hp, :], rhs=ot[:, hp, tq * 512:(tq + 1) * 512],
                            start=(hp == 0), stop=(hp == 7)),
                            r=["wo", ("ot", hp)], w=[("y_ps", pi)], sig=(hp == 7))
                    p.op("dve", lambda e, pi=pi, o_=o_: e.scalar_tensor_tensor(
                        out=ht[:, o_, :], in0=y_ps[pi][:], scalar=b.Gmod[:, ls * 8 + o_:ls * 8 + o_ + 1],
                        in1=ht[:, o_, :], op0=ALU.mult, op1=ALU.add),
                        r=[("y_ps", pi), "ht", "Gmod"], w=["ht"])
                p.dma("pool", Hv[:, :, tk0:tk0 + 512], ht[:], r=["ht"], w=["H"])
        keys = ["kt", "qt", "va", "rden", "wo", "ht", "acc"] + [("ot", i) for i in range(8)]
        keys += [(n_, i) for n_ in ("pt", "s_ps", "y_ps") for i in range(2)]
        self.phase_end(P, keys)

    def rope_norm(self, B, xp, xk, gcol, pgap, csb, ck, out_ap, out_key, rope=True, nope_ap=None, nope_key=None):
        b, p = self, self.p
        i2 = B["n"] % 2
        B["n"] += 1
        sq, xb, r2, t1, t2 = B["sq"][i2], B["xb"][i2], B["r2"][i2], B["t1"][i2], B["t2"][i2]
        s2_ps, rot_ps = B["s2_ps"], B["rot_ps"]
        p.op("act", lambda e: e.activation(out=sq[:], in_=xp, func=AF.Square), r=[xk], w=[("sq", i2)])
        p.op("pe", lambda e: e.matmul(s2_ps[:], lhsT=b.bones_b[:], rhs=sq[:], start=True, stop=True),
             r=[("sq", i2), "bones_b"], w=["s2_ps"])
        p.op("act", lambda e: e.activation(out=r2[:], in_=s2_ps[:], func=AF.Sqrt, scale=1.0 / 64, bias=EPS),
             r=["s2_ps"], w=[("r2", i2)])
        p.op("dve", lambda e: e.reciprocal(out=r2[:], in_=r2[:]), r=[("r2", i2)], w=[("r2", i2)])
        if nope_ap is not None:
            p.op("dve", lambda e: e.scalar_tensor_tensor(out=nope_ap, in0=xp, scalar=gcol, in1=r2[:],
                                                         op0=ALU.mult, op1=ALU.mult),
                 r=[xk, ("r2", i2), "gkv"], w=[nope_key])
        if not rope:
            return
        p.op("act", lambda e: e.activation(out=xb[:], in_=xp, func=AF.Copy), r=[xk], w=[("xb", i2)])
        p.op("pe", lambda e: e.matmul(rot_ps[:], lhsT=pgap, rhs=xb[:], start=True, stop=True),
             r=[("xb", i2), "pgk"], w=["rot_ps"])
        p.op("dve", lambda e: e.scalar_tensor_tensor(out=t1[:], in0=xp, scalar=gcol, in1=csb[:, 0, :],
                                                     op0=ALU.mult, op1=ALU.mult), r=[xk, ck, "gkv"], w=[("t1", i2)])
        p.op("dve", lambda e: e.tensor_tensor(out=t2[:], in0=rot_ps[:], in1=csb[:, 1, :], op=ALU.mult),
             r=["rot_ps", ck], w=[("t2", i2)])
        p.op("pool", lambda e: e.tensor_tensor(out=t1[:], in0=t1[:], in1=t2[:], op=ALU.add),
             r=[("t1", i2), ("t2", i2)], w=[("t1", i2)])
        p.op("pool", lambda e: e.tensor_tensor(out=out_ap, in0=t1[:], in1=r2[:], op=ALU.mult),
             r=[("t1", i2), ("r2", i2)], w=[out_key])

    def rn_bufs(self, P, TT):
        return {"n": 0,
                "sq": [P.sb("sq%d" % i, [128, TT], BF16) for i in range(2)],
                "xb": [P.sb("xb%d" % i, [128, TT], BF16) for i in range(2)],
                "r2": [P.sb("r2%d" % i, [128, TT], F32) for i in range(2)],
                "t1": [P.sb("t1%d" % i, [128, TT], F32) for i in range(2)],
                "t2": [P.sb("t2%d" % i, [128, TT], F32) for i in range(2)],
                "s2_ps": P.ps("s2_ps", [128, TT]), "rot_ps": P.ps("rot_ps", [128, TT])}

    RN_KEYS = ["s2_ps", "rot_ps"] + [(n_, i) for n_ in ("sq", "xb", "r2", "t1", "t2") for i in range(2)]

    def kv_phase(self, ntok, TT=512):
        b, p, nc = self, self.p, self.nc
        P = Pool(nc)
        ls = 6
        ht = [P.sb("ht%d" % i, [128, NKC, TT], F32) for i in range(2)]
        ut = [P.sb("ut%d" % i, [128, NKC, TT], BF16) for i in range(2)]
        ssq = [P.sb("ssq%d" % i, [128, TT], BF16) for i in range(2)]
        tmp = [P.sb("ntmp%d" % i, [128, TT], F32) for i in range(2)]
        rstd = P.sb("rstd", [128, TT], F32)
        cs = [P.sb("cs%d" % i, [128, 2, TT], F32) for i in range(2)]
        wf = P.sb("wf", [128, 4, NKC, 128], BF16)
        wvv = P.sb("wvv", [128, 2, NKC, 128], BF16)
        w1 = P.sb("w1", [128, 2, 32, 128], BF16)
        k0T = P.sb("k0T", [128, 2, ntok + 16], BF16)
        ko = [P.sb("ko%d" % i, [128, TT], BF16) for i in range(2)]
        vo = [P.sb("vo%d" % i, [128, 128], BF16) for i in range(2)]
        B = self.rn_bufs(P, TT)
        ss_ps = P.ps("ss_ps", [128, TT])
        x_ps = [P.ps("x_ps%d" % i, [128, TT]) for i in range(2)]
        v_ps = [P.ps("v_ps%d" % i, [128, 512]) for i in range(2)]
        srcv = b.H.rearrange("(kc p) t -> p kc t", p=128)
        nt = ntok // TT
        p.dma("sp", wf[:], b.wkvf_b.rearrange("p (c kc m) -> p c kc m", c=4, kc=NKC), r=["wkvf_b"], w=["wf"])
        p.dma("sp", wvv[:], b.wkvv_b.rearrange("p (c kc m) -> p c kc m", c=2, kc=NKC), r=["wkvv_b"], w=["wvv"])
        p.dma("sp", w1[:], b.w1r_b.rearrange("p (i q m) -> p i q m", i=2, q=32), r=["w1r_b"], w=["w1"])
        p.op("pool", lambda e: e.memset(k0T[:, :, ntok:ntok + 16], 0.0), w=["k0T"])
        xc = 0
        vc = 0
        p.dma("sp", ht[0][:], srcv[:, :, 0:TT], r=["H"], w=[("ht", 0)])
        for tt in range(nt):
            hb, hk, ub, uk = ht[tt % 2], ("ht", tt % 2), ut[tt % 2], ("ut", tt % 2)
            csb, ck = cs[tt % 2], ("cs", tt % 2)
            p.dma("sp", csb[:, 0, :], b.cosT[:, tt * TT:(tt + 1) * TT], w=[ck])
            p.dma("sp", csb[:, 1, :], b.sinT[:, tt * TT:(tt + 1) * TT], w=[ck])
            self.norm_tile(P, hb, hk, ub, uk, ssq, ss_ps, rstd, tmp, ls, TT)
            if tt + 1 < nt:
                p.dma("sp", ht[(tt + 1) % 2][:], srcv[:, :, (tt + 1) * TT:(tt + 2) * TT], r=["H"],
                      w=[("ht", (tt + 1) % 2)])
            for c in range(4):
                i2 = xc % 2
                xc += 1
                xp, xk = x_ps[i2], ("x_ps", i2)
                for kc in range(NKC):
                    p.op("pe", lambda e, xp=xp, c=c, kc=kc: e.matmul(
                        xp[:], lhsT=wf[:, c, kc, :], rhs=ub[:, kc, :], start=(kc == 0), stop=(kc == NKC - 1)),
                        r=["wf", uk], w=[xk], sig=(kc == NKC - 1))
                if c in (0, 3):
                    j = 0 if c == 0 else 1
                    p.op("act", lambda e, xp=xp, j=j, tt=tt: e.activation(
                        out=k0T[:, j, tt * TT:(tt + 1) * TT], in_=xp[:], func=AF.Copy), r=[xk], w=["k0T"])
                else:
                    kb, kk = ko[c % 2], ("ko", c % 2)
                    self.rope_norm(B, xp[:], xk, b.gkv[:, c:c + 1], b.pgk[:, c * 128:(c + 1) * 128], csb, ck,
                                   kb[:], kk)
                    p.dma("sp", b.KS[c - 1, :, tt * TT:(tt + 1) * TT], kb[:], r=[kk], w=[("KS", c - 1, tt)])
            for blk in range(TT // 128):
                for br in range(2):
                    vb, vk = vo[vc % 2], ("vo", vc % 2)
                    vp, vpk = v_ps[vc % 2], ("v_ps", vc % 2)
                    vc += 1
                    for kc in range(NKC):
                        p.op("pe", lambda e, vp=vp, kc=kc, br=br, blk=blk: e.matmul(
                            vp[:, 0:128], lhsT=ub[:, kc, blk * 128:(blk + 1) * 128], rhs=wvv[:, br, kc, :],
                            start=(kc == 0), stop=(kc == NKC - 1)),
                            r=["wvv", uk], w=[vpk], sig=(kc == NKC - 1))
                    p.op("act", lambda e, vp=vp, vb=vb: e.activation(out=vb[:], in_=vp[:, 0:128], func=AF.Copy),
                         r=[vpk], w=[vk])
                    t0 = tt * TT + blk * 128
                    p.dma("sp", b.VS[br, t0:t0 + 128, :], vb[:], r=[vk], w=[("VS", br, t0 // 128)])
        nc_ = ntok // 16
        assert nc_ <= 512
        sh = [P.sb("sh%d" % i, [128, 512], BF16) for i in range(2)]
        cb = P.sb("cbias", [128, 4], F32)
        kraw = P.sb("kraw", [128, 512], F32)
        hp_ = [x_ps[0], x_ps[1]]
        bias_ps = v_ps[0]
        o_ps = v_ps[1]
        p.op("pool", lambda e: e.memset(b.vca[:, :, 64:128], 1.0), w=["vca"])
        p.op("pool", lambda e: e.memset(b.vca[:, :, 0:64], 0.0), w=["vca"])
        p.op("pool", lambda e: e.memset(b.vca[:, :, 128:192], 0.0), w=["vca"])
        p.op("pool", lambda e: e.memset(b.kcT[:], 0.0), w=["kcT"])
        p.op("pool", lambda e: e.memset(sh[0][:], 0.0), w=[("sh", 0)])
        p.op("pool", lambda e: e.memset(sh[1][:], 0.0), w=[("sh", 1)])
        bbanks = [(x_ps[0], ("x_ps", 0)), (x_ps[1], ("x_ps", 1)), (v_ps[0], ("v_ps", 0)), (v_ps[1], ("v_ps", 1))]
        for i in range(2):
            for kvh in range(2):
                rows = slice(0, 64) if kvh == 0 else slice(64, 128)
                bp, bk = bbanks[i * 2 + kvh]
                for pos in range(32):
                    p.op("pe", lambda e, i=i, pos=pos, rows=rows, bp=bp: e.matmul(
                        bp[:, 0:1], lhsT=w1[rows, i, pos, :],
                        rhs=b.posT_b[rows, i * 32 + pos:i * 32 + pos + 1], start=(pos == 0), stop=(pos == 31)),
                        r=["w1", "posT_b"], w=[bk], sig=(pos == 31))
                p.op("dve", lambda e, i=i, kvh=kvh, bp=bp: e.tensor_copy(
                    out=cb[:, i * 2 + kvh:i * 2 + kvh + 1], in_=bp[:, 0:1]), r=[bk], w=["cbias"])
        for i in range(2):
            for kvh in range(2):
                rows = slice(0, 64) if kvh == 0 else slice(64, 128)
                hps, hk_ = hp_[kvh], ("x_ps", kvh)
                for pos in range(32):
                    p.op("pe", lambda e, i=i, pos=pos, rows=rows, hps=hps: e.matmul(
                        hps[:, 0:nc_], lhsT=w1[rows, i, pos, :], rhs=k0T[rows, i, pos:pos + 16 * (nc_ - 1) + 1:16],
                        start=(pos == 0), stop=(pos == 31)),
                        r=["w1", "k0T"], w=[hk_], sig=(pos == 31))
                p.op("act", lambda e, i=i, kvh=kvh, hps=hps: e.activation(
                    out=sh[kvh][:, 0:nc_], in_=hps[:, 0:nc_], func=AF.Silu, bias=cb[:, i * 2 + kvh:i * 2 + kvh + 1]),
                    r=[hk_, "cbias"], w=[("sh", kvh)])
            if i == 0:
                for kvh in range(2):
                    p.op("pe", lambda e, kvh=kvh: e.matmul(
                        o_ps[:, 0:nc_], lhsT=b.w2pad_b[:, kvh * 128:(kvh + 1) * 128], rhs=sh[kvh][:, 0:nc_],
                        start=(kvh == 0), stop=(kvh == 1)), r=["w2pad_b", ("sh", kvh)], w=[("v_ps", 1)], sig=(kvh == 1))
                self.rope_norm(B, o_ps[:, 0:512], ("v_ps", 1), b.gkv[:, 0:1], None, None, None, None, None,
                               rope=False, nope_ap=kraw[:], nope_key="kraw")
                p.op("dve", lambda e: e.tensor_copy(out=b.kcT[:, 0:nc_ - 1], in_=kraw[:, 0:nc_ - 1]),
                     r=["kraw"], w=["kcT"])
            else:
                for kvh in range(2):
                    for cbk in range((nc_ + 127) // 128):
                        n_ = min(128, nc_ - cbk * 128)
                        p.op("pe", lambda e, kvh=kvh, cbk=cbk, n_=n_: e.matmul(
                            o_ps[0:n_, 0:64], lhsT=sh[kvh][:, cbk * 128:cbk * 128 + n_], rhs=b.w2v_b[:],
                            start=True, stop=True), r=["w2v_b", ("sh", kvh)], w=[("v_ps", 1)])
                        c0 = 0 if kvh == 0 else 128
                        p.op("dve", lambda e, cbk=cbk, n_=n_, c0=c0: e.tensor_copy(
                            out=b.vca[0:n_, cbk, c0:c0 + 64], in_=o_ps[0:n_, 0:64]), r=[("v_ps", 1)], w=["vca"])
        if "kv" in b.dbg:
            o = b.dout("dbg_cb", [128, 4], F32)
            p.dma("sp", o, cb[:], r=["cbias"], w=["dbg_cb"])
            o = b.dout("dbg_k0T", [128, 2, ntok + 16], BF16)
            p.dma("sp", o, k0T[:], r=["k0T"], w=["dbg_k0T"])
            o = b.dout("dbg_sh", [2, 128, 512], BF16)
            p.dma("sp", o[0], sh[0][:], r=[("sh", 0)], w=["dbg_sh"])
            p.dma("sp", o[1], sh[1][:], r=[("sh", 1)], w=["dbg_sh"])
        keys = [("ht", 0), ("ht", 1), ("ut", 0), ("ut", 1), "ss_ps", "rstd", "wf", "wvv", "w1", "k0T", "cbias",
                "kraw"] + self.RN_KEYS
        keys += [(n_, i) for n_ in ("ssq", "ntmp", "cs", "x_ps", "v_ps", "vo", "ko", "sh") for i in range(2)]
        self.phase_end(P, keys)

    def nsa_phase(self, ntok, TT=256):
        b, p, nc = self, self.p, self.nc
        P = Pool(nc)
        ls = 4
        nblk = ntok // 128
        ht = P.sb("ht", [128, NKC, TT], F32)
        ut = P.sb("ut", [128, NKC, TT], BF16)
        ssq = [P.sb("ssq%d" % i, [128, TT], BF16) for i in range(2)]
        tmp = [P.sb("ntmp%d" % i, [128, TT], F32) for i in range(2)]
        rstd = P.sb("rstd", [128, TT], F32)
        csb = P.sb("cs", [128, 2, TT], F32)
        wq = P.sb("wq", [128, 8, NKC, 128], BF16)
        wg = P.sb("wg", [128, NKC, 48], BF16)
        wo = P.sb("wo", [128, 8, 8, 128], BF16)
        ex = P.sb("ex", [128, ntok], BF16)
        ks = P.sb("ks", [128, 1, ntok], BF16)
        vs = P.sb("vs", [128, 1, nblk, 192], BF16)
        kw = [P.sb("kw%d" % i, [128, 640], BF16) for i in range(2)]
        vw = [P.sb("vw%d" % i, [128, 5, 192], BF16) for i in range(2)]
        qn = P.sb("qn", [128, 8, TT], BF16)
        qr = P.sb("qr", [128, 8, TT], BF16)
        gates = P.sb("gates", [128, TT // 128, 48], F32)
        cm = P.sb("cm", [128, 512], BF16)
        cmT = P.sb("cmT", [128, 4, 128], BF16)
        M1 = P.sb("M1", [128, 128], F32)
        M2 = P.sb("M2", [128, 128], F32)
        E = [P.sb("E%d" % i, [128, 512], F32) for i in range(2)]
        den = P.sb("den", [128, 16], F32)
        pacc = P.sb("pacc", [128, 520], F32)
        imp = P.sb("imp", [128, 128], F32)
        impw = P.sb("impw", [128, 128], F32)
        m8 = P.sb("m8", [128, 16], F32)
        sel = P.sb("sel", [128, 128], BF16)
        selT = P.sb("selT", [128, 128], BF16)
        msk = P.sb("msk", [128, nblk, 128], BF16)
        pt = [P.sb("pt%d" % i, [128, 1024], BF16) for i in range(2)]
        accs = P.sb("accs", [128, 1024], F32)
        wgt = P.sb("wgt", [128, 8], F32)
        otok = P.sb("otok", [128, 16, 64], F32)
        otb = P.sb("otb", [128, 1024], BF16)
        oT = P.sb("oT", [128, 8, TT], BF16)
        B = self.rn_bufs(P, TT)
        ss_ps = B["s2_ps"]
        x_ps = P.ps("x_ps", [128, 512])
        s_ps = [P.ps("s_ps%d" % i, [128, 1024]) for i in range(1)]
        acc = P.ps("acc", [128, 1024])
        t_ps = x_ps
        Hv = b.H.rearrange("(kc p) t -> p kc t", p=128)
        p.dma("sp", wq[:], b.wqg_b.rearrange("p (c kc m) -> p c kc m", c=8, kc=NKC), r=["wqg_b"], w=["wq"])
        p.dma("sp", wg[:], b.wgate_b.rearrange("p (kc m) -> p kc m", kc=NKC), r=["wgate_b"], w=["wg"])
        p.dma("sp", wo[:], b.wo_b_b.rearrange("p (o hp m) -> p o hp m", o=8, hp=8), r=["wo_b_b"], w=["wo"])
        p.dma("sp", ex[:], b.Ex_b[:, 0:ntok], r=["Ex_b"], w=["ex"])
        for i in range(2):
            p.op("pool", lambda e, i=i: e.memset(vw[i][:, :, 64:128], 1.0), w=[("vw", i)])
        for br in range(1):
            p.dma("sp", ks[:, br, :], b.KS[br, :, 0:ntok], r=[("KS", br, tt) for tt in range(ntok // 512)], w=["ks"])
            for kvh in range(2):
                src = b.VS[br, 0:ntok, kvh * 64:(kvh + 1) * 64].rearrange("(kb i) d -> i kb d", i=128)
                c0 = 0 if kvh == 0 else 128
                p.dma("act", vs[:, br, :, c0:c0 + 64], src, r=[("VS", br, t) for t in range(nblk)], w=["vs"])
        p.op("pool", lambda e: e.memset(vs[:, :, :, 64:128], 1.0), w=["vs"])
        p.op("pool", lambda e: e.memset(cm[:], 0.0), w=["cm"])
        p.op("pool", lambda e: e.memset(M1[:], 0.0), w=["M1"])
        p.op("pool", lambda e: e.memset(M2[:], -BIG), w=["M2"])
        p.op("pool", lambda e: e.memset(pacc[:], 0.0), w=["pacc"])

        def attend(kvh, G, KT, keyblocks, Vt, qsrc, qb, masks, gcol0, first):
            rows = slice(0, 64) if kvh == 0 else slice(64, 128)
            for ki, kb in enumerate(keyblocks):
                pi = ki % 2
                pb, pk = pt[pi], ("pt", pi)
                for hf in range(2):
                    p.op("pe", lambda e, kb=kb, hf=hf: e.matmul(
                        s_ps[0][:, hf * 512:(hf + 1) * 512], lhsT=KT(kb, rows),
                        rhs=qsrc[rows, hf * 4:(hf + 1) * 4, qb * 128:(qb + 1) * 128], start=True, stop=True),
                        r=["ks", "kcT", "qn", "qr", ("kw", 0), ("kw", 1)], w=[("s_ps", 0)])
                p.op("act", lambda e, pb=pb: e.activation(out=pb[:], in_=s_ps[0][:], func=AF.Exp, scale=0.125),
                     r=[("s_ps", 0)], w=[pk])
                if masks.get(kb) is not None:
                    m = masks[kb]
                    p.op("dve", lambda e, pb=pb, m=m: e.tensor_tensor(
                        out=pb[:].rearrange("p (h q) -> p h q", h=8), in0=pb[:].rearrange("p (h q) -> p h q", h=8),
                        in1=m.unsqueeze(1).broadcast_to([128, 8, 128]), op=ALU.mult),
                        r=[pk, "msk", "cmT", "mgt_b", "mask4_b"], w=[pk])
                for hf in range(2):
                    p.op("pe", lambda e, kb=kb, hf=hf, pb=pb, ki=ki: e.matmul(
                        acc[:, hf * 512:(hf + 1) * 512], lhsT=Vt(kb, kvh), rhs=pb[:, hf * 512:(hf + 1) * 512],
                        start=(ki == 0), stop=(ki == len(keyblocks) - 1)),
                        r=[pk, "vs", "vca", ("vw", 0), ("vw", 1)], w=["acc"])
            p.op("act", lambda e: e.activation(out=accs[:], in_=acc[:], func=AF.Copy), r=["acc"], w=["accs"])
            for hf in range(2):
                for h4 in range(4):
                    h = hf * 4 + h4
                    p.op("pe", lambda e, h=h, h4=h4: e.transpose(
                        out=s_ps[0][:, h4 * 128:(h4 + 1) * 128], in_=accs[:, h * 128:(h + 1) * 128],
                        identity=b.ident_f[:]), r=["accs", "ident_f"], w=[("s_ps", 0)])
                tv = s_ps[0][:, 0:512].rearrange("p (h c) -> p h c", h=4)
                dcol = 64 if kvh == 0 else 0
                ocol = 0 if kvh == 0 else 64
                p.op("dve", lambda e, tv=tv, dcol=dcol: e.tensor_scalar(
                    out=wgt[:, 0:4], in0=tv[:, :, dcol], scalar1=1e-30, scalar2=None, op0=ALU.max),
                    r=[("s_ps", 0)], w=["wgt"])
                p.op("dve", lambda e: e.reciprocal(out=wgt[:, 0:4], in_=wgt[:, 0:4]), r=["wgt"], w=["wgt"])
                g0 = gcol0 + kvh * 8 + hf * 4
                p.op("dve", lambda e, g0=g0: e.tensor_tensor(out=wgt[:, 0:4], in0=wgt[:, 0:4],
                                                             in1=gates[:, qb, g0:g0 + 4], op=ALU.mult),
                     r=["wgt", "gates"], w=["wgt"])
                hd0 = kvh * 8 + hf * 4
                p.op("dve", lambda e, tv=tv, ocol=ocol: e.tensor_tensor(
                    out=E[0][:, 0:256].rearrange("p (h c) -> p h c", h=4), in0=tv[:, :, ocol:ocol + 64],
                    in1=wgt[:, 0:4].unsqueeze(2).broadcast_to([128, 4, 64]), op=ALU.mult),
                    r=[("s_ps", 0), "wgt"], w=[("E", 0)])
                if first:
                    p.op("pool", lambda e, hd0=hd0: e.tensor_copy(
                        out=otok[:, hd0:hd0 + 4, :], in_=E[0][:, 0:256].rearrange("p (h c) -> p h c", h=4)),
                        r=[("E", 0)], w=["otok"])
                else:
                    p.op("pool", lambda e, hd0=hd0: e.tensor_tensor(
                        out=otok[:, hd0:hd0 + 4, :], in0=otok[:, hd0:hd0 + 4, :],
                        in1=E[0][:, 0:256].rearrange("p (h c) -> p h c", h=4), op=ALU.add),
                        r=[("E", 0), "otok"], w=["otok"])

        for tt in range(ntok // TT):
            p.dma("sp", ht[:], Hv[:, :, tt * TT:(tt + 1) * TT], r=["H"], w=["ht"])
            p.dma("sp", csb[:, 0, :], b.cosT[:, tt * TT:(tt + 1) * TT], w=["cs"])
            p.dma("sp", csb[:, 1, :], b.sinT[:, tt * TT:(tt + 1) * TT], w=["cs"])
            self.norm_tile(P, ht, "ht", ut, "ut", ssq, ss_ps, rstd, tmp, ls, TT)
            for c in range(8):
                for kc in range(NKC):
                    p.op("pe", lambda e, c=c, kc=kc: e.matmul(
                        x_ps[:, 0:TT], lhsT=wq[:, c, kc, :], rhs=ut[:, kc, :], start=(kc == 0), stop=(kc == NKC - 1)),
                        r=["wq", "ut"], w=["x_ps"], sig=(kc == NKC - 1))
                self.rope_norm(B, x_ps[:, 0:TT], "x_ps", b.gkv[:, 3:4], b.pgk[:, 384:512], csb, "cs",
                               qr[:, c, :], "qr", rope=True, nope_ap=qn[:, c, :], nope_key="qn")
            for qb in range(TT // 128):
                for kc in range(NKC):
                    p.op("pe", lambda e, qb=qb, kc=kc: e.matmul(
                        x_ps[:, 0:48], lhsT=ut[:, kc, qb * 128:(qb + 1) * 128], rhs=wg[:, kc, :],
                        start=(kc == 0), stop=(kc == NKC - 1)), r=["wg", "ut"], w=["x_ps"], sig=(kc == NKC - 1))
                p.op("act", lambda e, qb=qb: e.activation(out=gates[:, qb, :], in_=x_ps[:, 0:48], func=AF.Sigmoid),
                     r=["x_ps"], w=["gates"])
            for qb in range(TT // 128):
                G = tt * (TT // 128) + qb
                lo = 8 * G - 1
                if G > 0:
                    p.op("pool", lambda e, lo=lo: e.memset(cm[:, max(lo - 8, 0):lo], 1.0), w=["cm"])
                c0 = max(lo, 0)
                p.op("pool", lambda e, lo=lo, c0=c0: e.tensor_copy(out=cm[:, c0:lo + 8],
                                                                   in_=b.pats_f[:, c0 - lo:8]),
                     r=["pats_f"], w=["cm"])
                if G > 1:
                    p.op("pool", lambda e, G=G: e.memset(M1[:, 2 * G - 3:2 * G - 1], 1.0), w=["M1"])
                    p.op("pool", lambda e, G=G: e.memset(M2[:, 2 * G - 3:2 * G - 1], 0.0), w=["M2"])
                elif G == 1:
                    pass
                a0 = max(2 * G - 1, 0)
                a1 = min(2 * G + 2, 128)
                p.op("pool", lambda e, G=G, a0=a0, a1=a1: e.tensor_copy(
                    out=M1[:, a0:a1], in_=b.pats_f[:, 8 + a0 - (2 * G - 1):8 + a1 - (2 * G - 1)]),
                    r=["pats_f"], w=["M1"])
                p.op("pool", lambda e, G=G, a0=a0, a1=a1: e.tensor_copy(
                    out=M2[:, a0:a1], in_=b.pats_f[:, 11 + a0 - (2 * G - 1):11 + a1 - (2 * G - 1)]),
                    r=["pats_f"], w=["M2"])
                if G >= 1:
                    p.op("pool", lambda e: e.memset(M1[:, 0:1], 0.0), w=["M1"])
                    p.op("pool", lambda e: e.memset(M2[:, 0:1], 3.0 * BIG), w=["M2"])
                ncb = (8 * G + 6) // 128 + 1
                for cbk in range(ncb):
                    p.op("pe", lambda e, cbk=cbk: e.transpose(
                        out=t_ps[:, cbk * 64:(cbk + 1) * 64].bitcast(BF16), in_=cm[:, cbk * 128:(cbk + 1) * 128],
                        identity=b.ident_b[:]), r=["cm", "ident_b"], w=["x_ps"])
                    p.op("act", lambda e, cbk=cbk: e.activation(
                        out=cmT[:, cbk, :], in_=t_ps[:, cbk * 64:(cbk + 1) * 64].bitcast(BF16), func=AF.Copy),
                        r=["x_ps"], w=["cmT"])
                kb0 = max(G - 4, 0)
                nkw = G + 1 - kb0
                wi_ = G % 2
                p.dma("sp", kw[wi_][:, 0:nkw * 128], b.KS[1, :, kb0 * 128:(G + 1) * 128],
                      r=[("KS", 1, t_) for t_ in range(kb0 * 128 // 512, (G * 128) // 512 + 1)], w=[("kw", wi_)])
                for kv_ in range(2):
                    src = b.VS[1, kb0 * 128:(G + 1) * 128, kv_ * 64:(kv_ + 1) * 64].rearrange("(kb i) d -> i kb d", i=128)
                    c0 = 0 if kv_ == 0 else 128
                    p.dma("sp", vw[wi_][:, 0:nkw, c0:c0 + 64], src, r=[("VS", 1, t_) for t_ in range(kb0, G + 1)],
                          w=[("vw", wi_)])
                for kvh in range(2):
                    rows = slice(0, 64) if kvh == 0 else slice(64, 128)
                    for c in range(8):
                        ei = c % 2
                        p.op("pe", lambda e, c=c, rows=rows, qb=qb: e.matmul(
                            s_ps[0][:, 0:512], lhsT=qn[rows, c, qb * 128:(qb + 1) * 128], rhs=b.kcT[rows, :],
                            start=True, stop=True), r=["qn", "kcT"], w=[("s_ps", 0)])
                        p.op("act", lambda e, ei=ei: e.activation(out=E[ei][:], in_=s_ps[0][:, 0:512], func=AF.Exp,
                                                                  scale=0.125), r=[("s_ps", 0)], w=[("E", ei)])
                        p.op("dve", lambda e, ei=ei: e.tensor_tensor(out=E[ei][:], in0=E[ei][:], in1=cm[:],
                                                                     op=ALU.mult), r=[("E", ei), "cm"], w=[("E", ei)])
                        p.op("dve", lambda e, ei=ei, c=c: e.reduce_sum(out=den[:, c:c + 1], in_=E[ei][:], axis=AX.X),
                             r=[("E", ei)], w=["den"])
                        p.op("dve", lambda e, c=c: e.tensor_scalar(out=den[:, c:c + 1], in0=den[:, c:c + 1],
                                                                   scalar1=1e-30, scalar2=None, op0=ALU.max),
                             r=["den"], w=["den"])
                        p.op("dve", lambda e, c=c: e.reciprocal(out=den[:, 8 + c:9 + c], in_=den[:, c:c + 1]),
                             r=["den"], w=["den"])
                        if c == 0:
                            p.op("dve", lambda e, ei=ei, c=c: e.tensor_scalar(
                                out=pacc[:, 1:513], in0=E[ei][:], scalar1=den[:, 8 + c:9 + c], scalar2=None,
                                op0=ALU.mult), r=[("E", ei), "den"], w=["pacc"])
                        else:
                            p.op("dve", lambda e, ei=ei, c=c: e.scalar_tensor_tensor(
                                out=pacc[:, 1:513], in0=E[ei][:], scalar=den[:, 8 + c:9 + c], in1=pacc[:, 1:513],
                                op0=ALU.mult, op1=ALU.add), r=[("E", ei), "den", "pacc"], w=["pacc"])
                    wts = (1.0, 2.0, 2.0, 2.0, 1.0)
                    for o in range(5):
                        src = pacc[:, o:o + 4 * 127 + 1:4]
                        if o == 0:
                            p.op("dve", lambda e, src=src: e.tensor_copy(out=imp[:], in_=src), r=["pacc"], w=["imp"])
                        else:
                            p.op("dve", lambda e, src=src, o=o: e.scalar_tensor_tensor(
                                out=imp[:], in0=src, scalar=wts[o], in1=imp[:], op0=ALU.mult, op1=ALU.add),
                                r=["pacc", "imp"], w=["imp"])
                    p.op("dve", lambda e: e.tensor_tensor(out=imp[:], in0=imp[:], in1=M1[:], op=ALU.mult),
                         r=["imp", "M1"], w=["imp"])
                    p.op("dve", lambda e: e.tensor_tensor(out=imp[:], in0=imp[:], in1=M2[:], op=ALU.add),
                         r=["imp", "M2"], w=["imp"])
                    p.op("dve", lambda e: e.max(out=m8[:, 0:8], in_=imp[:]), r=["imp"], w=["m8"])
                    p.op("dve", lambda e: e.match_replace(out=impw[:], in_to_replace=m8[:, 0:8], in_values=imp[:],
                                                          imm_value=-BIG), r=["imp", "m8"], w=["impw"])
                    p.op("dve", lambda e: e.max(out=m8[:, 8:16], in_=impw[:]), r=["impw"], w=["m8"])
                    p.op("dve", lambda e: e.tensor_scalar(out=m8[:, 15:16], in0=m8[:, 15:16], scalar1=-0.5 * BIG,
                                                          scalar2=None, op0=ALU.max), r=["m8"], w=["m8"])
                    p.op("dve", lambda e: e.tensor_scalar(out=sel[:], in0=imp[:], scalar1=m8[:, 15:16], scalar2=None,
                                                          op0=ALU.is_ge), r=["imp", "m8"], w=["sel"])
                    p.op("pe", lambda e: e.transpose(out=t_ps[:, 0:64].bitcast(BF16), in_=sel[:],
                                                     identity=b.ident_b[:]), r=["sel", "ident_b"], w=["x_ps"])
                    p.op("act", lambda e: e.activation(out=selT[:], in_=t_ps[:, 0:64].bitcast(BF16), func=AF.Copy),
                         r=["x_ps"], w=["selT"])
                    for k4 in range(0, G + 1, 4):
                        nk = min(4, G + 1 - k4)
                        for kk in range(nk):
                            kb = k4 + kk
                            p.op("pe", lambda e, kb=kb, kk=kk: e.matmul(
                                t_ps[:, kk * 128:(kk + 1) * 128], lhsT=ex[:, kb * 128:(kb + 1) * 128], rhs=selT[:],
                                start=True, stop=True), r=["ex", "selT"], w=["x_ps"])
                        p.op("act", lambda e, k4=k4, nk=nk: e.activation(
                            out=msk[:, k4:k4 + nk, :], in_=t_ps[:, 0:nk * 128].rearrange("p (k q) -> p k q", k=nk),
                            func=AF.Copy), r=["x_ps"], w=["msk"])
                    p.op("dve", lambda e, G=G: e.tensor_tensor(out=msk[:, G, :], in0=msk[:, G, :],
                                                               in1=b.mask4_b[:, 128:256], op=ALU.mult),
                         r=["msk", "mask4_b"], w=["msk"])
                    cbl = list(range(ncb))
                    import os
                    OB = os.environ.get("OB", "csw")
                    fst = [True]
                    def first_():
                        v = fst[0]
                        fst[0] = False
                        return v
                    if "c" in OB: attend(kvh, G, lambda kb, rows: b.kcT[rows, kb * 128:(kb + 1) * 128],
                           cbl, lambda kb, kvh: b.vca[:, kb, (0 if kvh == 0 else 64):(128 if kvh == 0 else 192)],
                           qn, qb, {kb: cmT[:, kb, :] for kb in cbl}, 0, first_())
                    sbl = list(range(G + 1))
                    if "s" in OB: attend(kvh, G, lambda kb, rows: ks[rows, 0, kb * 128:(kb + 1) * 128],
                           sbl, lambda kb, kvh: vs[:, 0, kb, (0 if kvh == 0 else 64):(128 if kvh == 0 else 192)],
                           qr, qb, {kb: msk[:, kb, :] for kb in sbl}, 16, first_())
                    wbl = list(range(max(G - 4, 0), G + 1))
                    wm = {G: b.mask4_b[:, 128:256]}
                    if G - 4 >= 0:
                        wm[G - 4] = b.mgt_b[:]
                    if "w" in OB: attend(kvh, G, lambda kb, rows: kw[wi_][rows, (kb - kb0) * 128:(kb - kb0 + 1) * 128],
                           wbl, lambda kb, kvh: vw[wi_][:, kb - kb0, (0 if kvh == 0 else 64):(128 if kvh == 0 else 192)],
                           qr, qb, wm, 32, first_())
                if "nsa" in b.dbg and G == b.dbgG:
                    o = b.dout("dbg_otok", [128, 1024], F32)
                    p.dma("sp", o, otok[:].rearrange("p h d -> p (h d)"), r=["otok"], w=["dbg_otok"])
                    o = b.dout("dbg_imp", [128, 128], F32)
                    p.dma("sp", o, imp[:], r=["imp"], w=["dbg_imp"])
                    o = b.dout("dbg_sel", [128, 128], BF16)
                    p.dma("sp", o, sel[:], r=["sel"], w=["dbg_sel"])
                    o = b.dout("dbg_gates", [128, 48], F32)
                    p.dma("sp", o, gates[:, qb, :], r=["gates"], w=["dbg_gates"])
                    o = b.dout("dbg_qn", [128, 8, 512], BF16)
                    p.dma("sp", o, qn[:], r=["qn"], w=["dbg_qn"])
                    o = b.dout("dbg_qr", [128, 8, 512], BF16)
                    p.dma("sp", o, qr[:], r=["qr"], w=["dbg_qr"])
                    o = b.dout("dbg_pacc", [128, 520], F32)
                    p.dma("sp", o, pacc[:], r=["pacc"], w=["dbg_pacc"])
                p.op("act", lambda e: e.activation(out=otb[:], in_=otok[:].rearrange("p h d -> p (h d)"),
                                                   func=AF.Copy), r=["otok"], w=["otb"])
                for hp in range(8):
                    p.op("pe", lambda e, hp=hp: e.transpose(
                        out=t_ps[:, (hp % 4) * 64:(hp % 4 + 1) * 64].bitcast(BF16), in_=otb[:, hp * 128:(hp + 1) * 128],
                        identity=b.ident_b[:]), r=["otb", "ident_b"], w=["x_ps"])
                    p.op("act", lambda e, hp=hp, qb=qb: e.activation(
                        out=oT[:, hp, qb * 128:(qb + 1) * 128],
                        in_=t_ps[:, (hp % 4) * 64:(hp % 4 + 1) * 64].bitcast(BF16), func=AF.Copy),
                        r=["x_ps"], w=["oT"])
            for o_ in range(8):
                for hp in range(8):
                    p.op("pe", lambda e, o_=o_, hp=hp: e.matmul(
                        x_ps[:, 0:TT], lhsT=wo[:, o_, hp, :], rhs=oT[:, hp, :], start=(hp == 0), stop=(hp == 7)),
                        r=["wo", "oT"], w=["x_ps"], sig=(hp == 7))
                p.op("dve", lambda e, o_=o_: e.scalar_tensor_tensor(
                    out=ht[:, o_, :], in0=x_ps[:, 0:TT], scalar=b.Gmod[:, ls * 8 + o_:ls * 8 + o_ + 1],
                    in1=ht[:, o_, :], op0=ALU.mult, op1=ALU.add), r=["x_ps", "ht", "Gmod"], w=["ht"])
            p.dma("pool", Hv[:, :, tt * TT:(tt + 1) * TT], ht[:], r=["ht"], w=["H"])
        keys = ["ht", "ut", "rstd", "cs", "wq", "wg", "wo", "ex", "ks", "vs", "qn", "qr", "gates", "cm", "cmT", "M1",
                "M2", "den", "pacc", "imp", "impw", "m8", "sel", "selT", "msk", "accs", "wgt", "otok", "otb", "oT",
                "x_ps", "acc", ("s_ps", 0)] + self.RN_KEYS
        keys += [(n_, i) for n_ in ("ssq", "ntmp", "E", "pt", "kw", "vw") for i in range(2)]
        print("nsa sbuf remaining", nc.sbuf_bytes_remaining)
        self.phase_end(P, keys)

    def copy_H_out(self, n):
        b, p = self, self.p
        P = Pool(self.nc)
        t = P.sb("cpy", [128, NKC, 512], F32)
        Hv = b.H.rearrange("(kc p) t -> p kc t", p=128)
        Ov = b.out.rearrange("(kc p) t -> p kc t", p=128)
        for i in range(n // 512):
            p.dma("sp", t[:], Hv[:, :, i * 512:(i + 1) * 512], r=["H"], w=["cpy"])
            p.dma("sp", Ov[:, :, i * 512:(i + 1) * 512], t[:], r=["cpy"], w=["outfinal"])
        self.phase_end(P, ["cpy"])

    def phase_end(self, P, keys):
        b, p = self, self.p
        p.op("dve", lambda e: e.tensor_copy(out=b.ones_b[:, 0:1], in_=b.ones_f[:, 0:1]), r=["ones_f"], w=keys)
        for en in ("pe", "act", "pool", "sp"):
            E = p.es[en]
            p._wait(E, {"dve": p.es["dve"].count})
        P.free()

    def build(self):
        b = self
        n = b.ntok_dbg or S
        b.declare()
        b.convert_weights()
        b.convert_weights2()
        b.consts()
        if "mod" in b.dbg:
            o = b.dout("dbg_mod", [128, 144])
            b.p.dma("sp", o[:, :], b.mod[:], r=["mod"], w=["dbg_mod"])
        if b.stop_after == "consts":
            o = b.dout("dbg_pg", [128, 768], BF16)
            b.p.dma("sp", o[:, :], b.pg[:], r=["pg"], w=["dbg_pg"])
            o = b.dout("dbg_bones", [128, 128], BF16)
            b.p.dma("sp", o[:, :], b.bones_b[:], r=["bones_b"], w=["dbg_bones"])
            b.p.finish()
            return b.nc
        import os
        if not os.environ.get("NOFFN"):
            b.ffn_phase(b.xT, b.out if b.stop_after == "ffn00" else b.H, 0, 0, n)
        if b.stop_after == "ffn00":
            b.p.finish()
            return b.nc
        if b.stop_after == "ffn0Hx":
            b.p.dma("sp", b.out[0:128, 0:128], b.ones_f[:], r=["ones_f"], w=["outfinal"])
            b.p.finish()
            return b.nc
        if b.stop_after == "ffn0H":
            b.copy_H_out(n)
            b.p.finish()
            return b.nc
        b.qkv_phase(n)
        if b.stop_after == "qkv0":
            b.copy_H_out(n)
            b.p.finish()
            return b.nc
        if b.stop_after == "qkv":
            for nm, src in (("dbg_qt", b.QT), ("dbg_kt", b.KT)):
                o = b.dout(nm, [3, 8, 128, n], BF16)
                for g in range(3):
                    b.p.dma("sp", o[g], src[g, :, :, 0:n], r=[("QK", g, qk, hp, tt) for qk in range(2) for hp in range(8) for tt in range(n // 512)], w=[nm])
            o = b.dout("dbg_v", [3, n, 1024], BF16)
            b.p.dma("sp", o[:, :, :], b.V[:, 0:n, :], r=[("V", g, t) for g in range(3) for t in range(n // 128)], w=["dbg_v"])
            b.p.finish()
            return b.nc
        b.attn_a_phase(n)
        if b.stop_after == "attn0":
            b.copy_H_out(n)
            b.p.finish()
            return b.nc
        b.ffn_phase(b.H, b.H, 1, 2, n)
        b.kv_phase(n)
        if b.stop_after == "kv":
            for nm, t in (("dbg_kcT", b.kcT), ("dbg_vca", b.vca)):
                o = b.dout(nm, list(t.shape), BF16)
                b.p.dma("sp", o, t[:], r=["kcT", "vca"], w=[nm])
            b.copy_H_out(n)
            b.p.finish()
            return b.nc
        b.ffn_phase(b.H, b.H, 2, 3, n)
        b.nsa_phase(n)
        if b.stop_after == "nsa":
            b.copy_H_out(n)
            b.p.finish()
            return b.nc
        b.ffn_phase(b.H, b.out, 3, 5, n)
        b.p.finish()
        return b.nc


def _chunkp(v, nk):
    return np.ascontiguousarray(v.reshape(nk, 128).T)


def _lhsT_layout(w, ncols_chunks):
    K, M = w.shape
    kc = K // 128
    j = M // 128
    return np.ascontiguousarray(w.reshape(kc, 128, j, 128).transpose(1, 2, 0, 3).reshape(128, j * kc * 128))


def prep_inputs(inputs, core):
    bi = core % 4
    x = np.asarray(inputs["x"])
    m = {}
    m["xT"] = np.ascontiguousarray(x[bi].T)
    m["c_in"] = _chunkp(np.asarray(inputs["c"])[bi], NKC)
    ng = np.asarray(inputs["norm_g"])
    m["normg"] = np.concatenate([_chunkp(ng[l, s], NKC) for l in range(2) for s in range(3)], axis=1)
    ba = np.asarray(inputs["b_ada"])
    m["bada"] = np.concatenate([_chunkp(ba[l], 72) for l in range(2)], axis=1)
    wa = np.asarray(inputs["w_ada"])
    m["wada"] = np.stack([_lhsT_layout(wa[l], 72) for l in range(2)])
    wi = np.asarray(inputs["ffn_w_in"])
    wo = np.asarray(inputs["ffn_w_out"])
    fin = []
    fout = []
    for l in range(2):
        for s in range(2):
            w = wi[l, s]
            cols = np.concatenate([np.concatenate([np.arange(jj * 128, jj * 128 + 128),
                                                   DFF + np.arange(jj * 128, jj * 128 + 128)]) for jj in range(NJ)])
            fin.append(_lhsT_layout(w[:, cols], 44))
            fout.append(_lhsT_layout(wo[l, s], 8))
    m["ffn_in"] = np.stack(fin)
    m["ffn_out"] = np.stack(fout)
    m["ones_in"] = np.ones((128, 128), np.float32)
    wq = np.asarray(inputs["a_w_qkv"])[0].reshape(D, 3, 3, 16, 64)
    chunks = []
    for g in range(3):
        for qk in range(2):
            chunks.append(wq[:, g, qk].reshape(D, 1024))
    m["wqk"] = _lhsT_layout(np.concatenate(chunks, axis=1), 48)
    wvv = np.stack([wq[:, g, 2].reshape(D, 1024) for g in range(3)])
    m["wv"] = np.ascontiguousarray(wvv.reshape(3, NKC, 128, 1024).transpose(2, 0, 1, 3).reshape(128, 3 * NKC * 1024))
    woa = np.asarray(inputs["a_w_o"])[0]
    m["wo_a"] = np.ascontiguousarray(woa.reshape(8, 128, 8, 128).transpose(1, 2, 0, 3).reshape(128, 8 * 8 * 128))
    qg = np.asarray(inputs["a_q_gain"])[0]
    kg = np.asarray(inputs["a_k_gain"])[0]
    ga = np.zeros((128, 6), np.float32)
    for g in range(3):
        ga[:, 2 * g] = np.tile(qg[g], 2)
        ga[:, 2 * g + 1] = np.tile(kg[g], 2)
    m["gains_a"] = ga
    m["wada_kv"] = _lhsT_layout(np.asarray(inputs["w_ada_kv"]), 16)
    m["bada_kv"] = _chunkp(np.asarray(inputs["b_ada_kv"]), 16)
    m["kvng"] = _chunkp(np.asarray(inputs["kv_norm_g"]), NKC)
    wkv = np.asarray(inputs["w_kv"]).reshape(D, 3, 2, 128)
    m["wkvf"] = _lhsT_layout(np.concatenate([wkv[:, 0, 0], wkv[:, 1, 0], wkv[:, 2, 0], wkv[:, 0, 1]], axis=1), 4)
    wvv2 = np.stack([wkv[:, 1, 1], wkv[:, 2, 1]])
    m["wkvv"] = np.ascontiguousarray(wvv2.reshape(2, NKC, 128, 128).transpose(2, 0, 1, 3).reshape(128, 2 * NKC * 128))
    kkg = np.asarray(inputs["kv_k_gain"])
    gk = np.zeros((128, 4), np.float32)
    for i in range(3):
        gk[:, i] = np.tile(kkg[i], 2)
    gk[:, 3] = np.tile(np.asarray(inputs["b_q_gain"])[0], 2)
    m["gains_kv"] = gk
    w1 = np.asarray(inputs["phi_w1"]).reshape(2, 32, 64, 128).transpose(2, 0, 1, 3)
    m["w1r"] = np.ascontiguousarray(np.concatenate([w1, w1], axis=0).reshape(128, 2 * 32 * 128))
    cp = np.asarray(inputs["cmp_pos"]).transpose(2, 0, 1).reshape(64, 64)
    m["posT"] = np.ascontiguousarray(np.concatenate([cp, cp], axis=0))
    w2 = np.asarray(inputs["phi_w2"])
    w2p = np.zeros((128, 2, 2, 128), np.float32)
    w2p[:, 0, 0, 0:64] = w2[0]
    w2p[:, 0, 1, 64:128] = w2[0]
    w2p[:, 1, 0, 0:64] = w2[1]
    w2p[:, 1, 1, 64:128] = w2[1]
    m["w2pad"] = np.ascontiguousarray(w2p.reshape(128, 512))
    m["w2v"] = np.ascontiguousarray(w2[1])
    wqg = np.asarray(inputs["b_w_qg"])[0]
    wq_ = wqg[:, :1024].reshape(D, 16, 64)
    chunks = [np.concatenate([wq_[:, c], wq_[:, 8 + c]], axis=1) for c in range(8)]
    m["wqg"] = _lhsT_layout(np.concatenate(chunks, axis=1), 8)
    m["wgate"] = np.ascontiguousarray(wqg[:, 1024:].reshape(NKC, 128, 48).transpose(1, 0, 2).reshape(128, NKC * 48))
    wob = np.asarray(inputs["b_w_o"])[0]
    m["wo_b"] = np.ascontiguousarray(wob.reshape(8, 128, 8, 128).transpose(1, 2, 0, 3).reshape(128, 8 * 8 * 128))
    m.update(_const_tables())
    return m


_CT = {}


def _const_tables():
    if _CT:
        return _CT
    pr = np.zeros((128, 128), np.float32)
    for mm in range(128):
        if mm % 64 < 32:
            pr[mm + 32, mm] = -1.0
        else:
            pr[mm - 32, mm] = 1.0
    _CT["prot"] = pr
    bo = np.zeros((128, 128), np.float32)
    bo[:64, :64] = 1.0
    bo[64:, 64:] = 1.0
    _CT["bones"] = bo
    half = 32
    inv = (10000.0 ** (-np.arange(half, dtype=np.float32) / half)).astype(np.float32)
    ang = np.arange(S, dtype=np.float32)[None, :] * inv[:, None]
    cos = np.cos(ang).astype(np.float32)
    sin = np.sin(ang).astype(np.float32)
    _CT["cosT"] = np.ascontiguousarray(np.tile(cos, (4, 1)))
    _CT["sinT"] = np.ascontiguousarray(np.tile(sin, (4, 1)))
    k = np.arange(128)[:, None]
    q = np.arange(128)[None, :]
    prev = (k >= q).astype(np.float32)
    diag = (k <= q).astype(np.float32)
    _CT["mask4"] = np.ascontiguousarray(np.concatenate([prev, diag, prev, diag], axis=1))
    _CT["maskgt"] = (k > q).astype(np.float32)
    _CT["ident"] = np.eye(128, dtype=np.float32)
    ex = np.zeros((128, S), np.float32)
    ex[np.arange(S) // 64, np.arange(S)] = 1.0
    _CT["Ex"] = ex
    pats = np.zeros((128, 16), np.float32)
    qq = np.arange(128)
    for i in range(8):
        pats[:, i] = (qq >= 16 * i + 15)
    lo = qq < 64
    pats[:, 8] = np.where(lo, 0.0, 1.0)
    pats[:, 9] = 0.0
    pats[:, 10] = 0.0
    pats[:, 11] = np.where(lo, 1.0 * BIG, 0.0)
    pats[:, 12] = np.where(lo, 2.0 * BIG, 1.0 * BIG)
    pats[:, 13] = np.where(lo, -BIG, 2.0 * BIG)
    _CT["pats"] = pats
    return _CT


_CACHE = {}


def kernel(**inputs):
    b = Builder()
    b.ntok_dbg = None
    nc = b.build()
    in_maps = [prep_inputs(inputs, c) for c in range(NCORES)]
    res = run_bass_kernel_spmd(nc, in_maps, core_ids=list(range(NCORES)))
    out = np.stack([np.ascontiguousarray(res.results[c]["outT"].T) for c in range(4)])
    return out.astype(np.float32)
```

```python
import numpy as np
import ml_dtypes
import concourse.bass as bass
import concourse.mybir as mybir
from concourse.bass_utils import run_bass_kernel_spmd

F32 = mybir.dt.float32
BF16 = mybir.dt.bfloat16
AF = mybir.ActivationFunctionType
ALU = mybir.AluOpType
AX = mybir.AxisListType

D = 1024
S = 8192
NKC = 8
DFF = 2816
NJ = 22
EPS = 1e-6
NCORES = 4
BIG = 1.0e30


class Stream:
    def __init__(self, name, sem, inc, handle=None):
        self.name, self.sem, self.inc, self.h = name, sem, inc, handle
        self.count = 0
        self.seen = {}
        self.snaps = {}
        self.observed = 0


class Prog:
    def __init__(self, nc, ndma=24):
        self.nc = nc
        self.es = {}
        self._ctx = []
        for name, h in (("pe", nc.tensor), ("act", nc.scalar), ("dve", nc.vector),
                        ("pool", nc.gpsimd), ("sp", nc.sync)):
            cm = nc.semaphore("sem_" + name)
            sem = cm.__enter__()
            self._ctx.append(cm)
            self.es[name] = Stream(name, sem, 1, h)
        self.dmas = []
        for i in range(ndma):
            cm = nc.semaphore("semd%d" % i)
            sem = cm.__enter__()
            self._ctx.append(cm)
            self.dmas.append(Stream("d%d" % i, sem, 16))
        self.streams = dict(self.es)
        for d in self.dmas:
            self.streams[d.name] = d
        self.res = {}
        self.dma_rr = 0
        self.nwaits = 0
        self.nops = 0

    def close(self):
        for cm in reversed(self._ctx):
            cm.__exit__(None, None, None)

    def _deps(self, r, w):
        deps = {}
        def add(sn):
            if sn is None:
                return
            s, n = sn
            if deps.get(s, 0) < n:
                deps[s] = n
        for k in r:
            st = self.res.get(k)
            if st:
                add(st[0])
        for k in w:
            st = self.res.get(k)
            if st:
                add(st[0])
                for s, n in st[1].items():
                    add((s, n))
        return deps

    def _wait(self, E, deps, skip_self=False):
        for s, n in deps.items():
            if skip_self and s == E.name:
                continue
            if E.seen.get(s, 0) >= n:
                continue
            S_ = self.streams[s]
            E.h.wait_ge(S_.sem, n * S_.inc)
            self.nwaits += 1
            E.seen[s] = n
            if S_.observed < n:
                S_.observed = n
            snap = S_.snaps.get(n)
            if snap:
                for k2, v2 in snap.items():
                    if E.seen.get(k2, 0) < v2:
                        E.seen[k2] = v2

    def _commit(self, sname, n, r, w):
        for k in r:
            st = self.res.setdefault(k, [None, {}])
            if st[1].get(sname, 0) < n:
                st[1][sname] = n
        for k in w:
            self.res[k] = [(sname, n), {}]

    def op(self, eng, fn, r=(), w=(), sig=True):
        E = self.es[eng]
        deps = self._deps(r, w)
        self._wait(E, deps, skip_self=(eng == "pe"))
        ins = fn(E.h)
        self.nops += 1
        n = E.count + 1
        if sig:
            ins.then_inc(E.sem, 1)
            E.count = n
            E.snaps[n] = dict(E.seen)
        self._commit(eng, n, r, w)
        return ins

    def dma(self, q, out, in_, r=(), w=()):
        E = self.es[q]
        deps = self._deps(r, w)
        self._wait(E, deps)
        Dm = None
        for i in range(len(self.dmas)):
            c = self.dmas[(self.dma_rr + i) % len(self.dmas)]
            if c.observed >= c.count:
                Dm = c
                self.dma_rr = (self.dma_rr + i + 1) % len(self.dmas)
                break
        if Dm is None:
            Dm = min(self.dmas, key=lambda c: c.count)
        if Dm.count > 0:
            self._wait(E, {Dm.name: Dm.count})
        E.h.dma_start(out=out, in_=in_).then_inc(Dm.sem, 16)
        Dm.count += 1
        Dm.snaps[Dm.count] = dict(E.seen)
        self._commit(Dm.name, Dm.count, r, w)

    def finish(self):
        E = self.es["sp"]
        deps = {}
        for d in self.dmas:
            if d.count > 0:
                deps[d.name] = d.count
        for e in ("pe", "act", "dve", "pool"):
            if self.es[e].count > 0:
                deps[e] = self.es[e].count
        self._wait(E, deps)


class Pool:
    _n = [0]

    def __init__(self, nc):
        self.nc = nc
        self.cms = []
        Pool._n[0] += 1
        self.sfx = "_%d" % Pool._n[0]

    def sb(self, name, shape, dt):
        cm = self.nc.sbuf_tensor(name + self.sfx, list(shape), dt)
        t = cm.__enter__()
        self.cms.append(cm)
        return t

    def ps(self, name, shape, dt=F32):
        cm = self.nc.psum_tensor(name + self.sfx, list(shape), dt)
        t = cm.__enter__()
        self.cms.append(cm)
        return t

    def free(self):
        for cm in reversed(self.cms):
            cm.__exit__(None, None, None)
        self.cms = []


class Builder:
    def __init__(self, stop_after=None, dbg=()):
        self.stop_after = stop_after
        self.dbg = set(dbg)
        self.nc = bass.Bass("TRN2", target_bir_lowering=False)
        self.p = Prog(self.nc)
        self.ins = {}
        self.outs = {}
        self.uid = 0
        self.ntok_dbg = None
        self.dbgG = 1
        self.only_branch = None

    def din(self, name, shape, dt=F32):
        t = self.nc.dram_tensor(name, list(shape), dt, kind="ExternalInput").ap()
        self.ins[name] = t
        return t

    def dout(self, name, shape, dt=F32):
        t = self.nc.dram_tensor(name, list(shape), dt, kind="ExternalOutput").ap()
        self.outs[name] = t
        return t

    def dscr(self, name, shape, dt):
        return self.nc.dram_tensor(name, list(shape), dt, kind="Internal").ap()

    def key(self, base):
        self.uid += 1
        return "%s#%d" % (base, self.uid)

    def declare(self):
        b = self
        b.xT = b.din("xT", [D, S])
        b.c_in = b.din("c_in", [128, NKC])
        b.normg = b.din("normg", [128, 2 * 3 * NKC])
        b.bada = b.din("bada", [128, 2 * 72])
        b.wada = b.din("wada", [2, 128, 72 * NKC * 128])
        b.ffn_in = b.din("ffn_in", [4, 128, 44 * NKC * 128])
        b.ffn_out = b.din("ffn_out", [4, 128, 8 * NJ * 128])
        b.ones_in = b.din("ones_in", [128, 128])
        b.wqk = b.din("wqk", [128, 48 * NKC * 128])
        b.wv = b.din("wv", [128, 3 * NKC * 1024])
        b.wo_a = b.din("wo_a", [128, 8 * 8 * 128])
        b.gains_a = b.din("gains_a", [128, 6])
        b.prot = b.din("prot", [128, 128])
        b.bones = b.din("bones", [128, 128])
        b.cosT = b.din("cosT", [128, S])
        b.sinT = b.din("sinT", [128, S])
        b.mask4 = b.din("mask4", [128, 512])
        b.wada_kv = b.din("wada_kv", [128, 16 * NKC * 128])
        b.bada_kv = b.din("bada_kv", [128, 16])
        b.kvng = b.din("kvng", [128, 8])
        b.wkvf = b.din("wkvf", [128, 4 * NKC * 128])
        b.wkvv = b.din("wkvv", [128, 2 * NKC * 128])
        b.gains_kv = b.din("gains_kv", [128, 4])
        b.w1r = b.din("w1r", [128, 2 * 32 * 128])
        b.posT = b.din("posT", [128, 64])
        b.w2pad = b.din("w2pad", [128, 2 * 2 * 128])
        b.w2v = b.din("w2v", [128, 64])
        b.wqg = b.din("wqg", [128, 8 * NKC * 128])
        b.wgate = b.din("wgate", [128, NKC * 48])
        b.wo_b = b.din("wo_b", [128, 8 * 8 * 128])
        b.Ex = b.din("Ex", [128, S])
        b.pats = b.din("pats", [128, 16])
        b.maskgt = b.din("maskgt", [128, 128])
        b.ident = b.din("ident", [128, 128])
        b.out = b.dout("outT", [D, S])
        b.wkvf_b = b.dscr("wkvf_b", [128, 4 * NKC * 128], BF16)
        b.wkvv_b = b.dscr("wkvv_b", [128, 2 * NKC * 128], BF16)
        b.w1r_b = b.dscr("w1r_b", [128, 2 * 32 * 128], BF16)
        b.wqg_b = b.dscr("wqg_b", [128, 8 * NKC * 128], BF16)
        b.wgate_b = b.dscr("wgate_b", [128, NKC * 48], BF16)
        b.wo_b_b = b.dscr("wo_b_b", [128, 8 * 8 * 128], BF16)
        b.Ex_b = b.dscr("Ex_b", [128, S], BF16)
        b.KS = b.dscr("KS", [2, 128, S], BF16)
        b.VS = b.dscr("VS", [2, S, 128], BF16)
        b.wqk_b = b.dscr("wqk_b", [128, 48 * NKC * 128], BF16)
        b.wv_b = b.dscr("wv_b", [128, 3 * NKC * 1024], BF16)
        b.wo_a_b = b.dscr("wo_a_b", [128, 8 * 8 * 128], BF16)
        b.QT = b.dscr("QT", [3, 8, 128, S], BF16)
        b.KT = b.dscr("KT", [3, 8, 128, S], BF16)
        b.V = b.dscr("V", [3, S, 1024], BF16)
        b.H = b.dscr("hres_scratch", [D, S], F32)
        b.ffn_in_b = b.dscr("ffn_in_b", [4, 128, 44 * NKC * 128], BF16)
        b.ffn_out_b = b.dscr("ffn_out_b", [4, 128, 8 * NJ * 128], BF16)

    def convert_weights(self):
        p = self.p
        for i in range(4):
            n = 4096
            for jg in range(11):
                p.dma("pool", self.ffn_in_b[i, :, jg * n:(jg + 1) * n], self.ffn_in[i, :, jg * n:(jg + 1) * n],
                      w=[("ffn_in_b", i, jg)])
            n = NJ * 128
            for o in range(8):
                p.dma("pool", self.ffn_out_b[i, :, o * n:(o + 1) * n], self.ffn_out[i, :, o * n:(o + 1) * n],
                      w=[("ffn_out_b", i, o)])

    def convert_weights2(self):
        p = self.p
        n = NKC * 128 * 8
        for i in range(6):
            p.dma("pool", self.wqk_b[:, i * n:(i + 1) * n], self.wqk[:, i * n:(i + 1) * n], w=[("wqk_b", i)])
        n = NKC * 1024
        for i in range(3):
            p.dma("pool", self.wv_b[:, i * n:(i + 1) * n], self.wv[:, i * n:(i + 1) * n], w=[("wv_b", i)])
        p.dma("pool", self.wo_a_b[:, :], self.wo_a[:, :], w=["wo_a_b"])
        for nm in ("wkvf", "wkvv", "w1r", "wqg", "wgate", "wo_b", "Ex"):
            p.dma("pool", getattr(self, nm + "_b")[:, :], getattr(self, nm)[:, :], w=[nm + "_b"])

    def consts(self):
        b, p, nc = self, self.p, self.nc
        P = self.cpool = Pool(nc)
        b.ones_f = P.sb("ones_f", [128, 128], F32)
        b.ones_b = P.sb("ones_b", [128, 128], BF16)
        b.mod = P.sb("mod", [128, 2 * 72 + 16], F32)
        b.ng = P.sb("ng", [128, 56], F32)
        b.cact = P.sb("cact", [128, NKC], F32)
        b.Amod = P.sb("Amod", [128, 56], F32)
        b.Gmod = P.sb("Gmod", [128, 48], F32)
        b.bones_b = P.sb("bones_b", [128, 128], BF16)
        b.prot_f = P.sb("prot_f", [128, 128], F32)
        b.pg = P.sb("pg", [128, 6 * 128], BF16)
        b.ga = P.sb("ga", [128, 6], F32)
        b.mask4_b = P.sb("mask4_b", [128, 512], BF16)
        p.dma("sp", b.mod[:, 144:160], b.bada_kv[:, :], w=["mod"])
        b.gkv = P.sb("gkv", [128, 4], F32)
        p.dma("sp", b.gkv[:], b.gains_kv[:, :], w=["gkv"])
        b.pgk = P.sb("pgk", [128, 4 * 128], BF16)
        b.ident_f = P.sb("ident_f", [128, 128], F32)
        b.ident_b = P.sb("ident_b", [128, 128], BF16)
        b.mgt_b = P.sb("mgt_b", [128, 128], BF16)
        b.pats_f = P.sb("pats_f", [128, 16], F32)
        b.posT_f = P.sb("posT_f", [128, 64], F32)
        b.posT_b = P.sb("posT_b", [128, 64], BF16)
        b.w2pad_b = P.sb("w2pad_b", [128, 512], BF16)
        b.w2v_b = P.sb("w2v_b", [128, 64], BF16)
        b.kcT = P.sb("kcT", [128, 512], BF16)
        b.vca = P.sb("vca", [128, 4, 192], BF16)
        p.dma("sp", b.ident_f[:], b.ident[:, :], w=["ident_f"])
        p.dma("sp", b.pats_f[:], b.pats[:, :], w=["pats_f"])
        p.dma("sp", b.posT_f[:], b.posT[:, :], w=["posT_f"])
        stg = P.sb("stg", [128, 512], F32)
        p.dma("sp", stg[:, 0:128], b.bones[:, :], w=["stg"])
        p.op("dve", lambda e: e.tensor_copy(out=b.bones_b[:], in_=stg[:, 0:128]), r=["stg"], w=["bones_b"])
        p.dma("sp", stg[:], b.mask4[:, :], w=["stg"])
        p.op("dve", lambda e: e.tensor_copy(out=b.mask4_b[:], in_=stg[:]), r=["stg"], w=["mask4_b"])
        p.dma("sp", b.prot_f[:], b.prot[:, :], w=["prot_f"])
        p.dma("sp", b.ga[:], b.gains_a[:, :], w=["ga"])
        for i in range(1, 4):
            p.op("dve", lambda e, i=i: e.tensor_scalar(out=b.pgk[:, i * 128:(i + 1) * 128], in0=b.prot_f[:],
                                                       scalar1=b.gkv[:, i:i + 1], scalar2=None, op0=ALU.mult),
                 r=["prot_f", "gkv"], w=["pgk"])
        p.op("dve", lambda e: e.tensor_copy(out=b.ident_b[:], in_=b.ident_f[:]), r=["ident_f"], w=["ident_b"])
        p.op("dve", lambda e: e.tensor_copy(out=b.posT_b[:], in_=b.posT_f[:]), r=["posT_f"], w=["posT_b"])
        p.dma("sp", stg[:, 0:128], b.maskgt[:, :], r=["mask4_b"], w=["stg"])
        p.op("dve", lambda e: e.tensor_copy(out=b.mgt_b[:], in_=stg[:, 0:128]), r=["stg"], w=["mgt_b"])
        p.dma("sp", stg[:], b.w2pad[:, :], r=["mgt_b"], w=["stg"])
        p.op("dve", lambda e: e.tensor_copy(out=b.w2pad_b[:], in_=stg[:]), r=["stg"], w=["w2pad_b"])
        p.dma("sp", stg[:, 0:64], b.w2v[:, :], r=["w2pad_b"], w=["stg"])
        p.op("dve", lambda e: e.tensor_copy(out=b.w2v_b[:], in_=stg[:, 0:64]), r=["stg"], w=["w2v_b"])
        for i in range(6):
            p.op("dve", lambda e, i=i: e.tensor_scalar(out=b.pg[:, i * 128:(i + 1) * 128], in0=b.prot_f[:],
                                                       scalar1=b.ga[:, i:i + 1], scalar2=None, op0=ALU.mult),
                 r=["prot_f", "ga"], w=["pg"])
        p.dma("sp", b.ones_f[:], b.ones_in[:, :], w=["ones_f"])
        p.dma("sp", b.ng[:, 0:48], b.normg[:, :], w=["ng"])
        p.dma("sp", b.ng[:, 48:56], b.kvng[:, :], w=["ng"])
        p.dma("sp", b.cact[:], b.c_in[:, :], w=["cact"])
        p.dma("sp", b.mod[:, 0:144], b.bada[:, :], w=["mod"])
        p.op("dve", lambda e: e.tensor_copy(out=b.ones_b[:], in_=b.ones_f[:]), r=["ones_f"], w=["ones_b"])
        p.op("act", lambda e: e.activation(out=b.cact[:], in_=b.cact[:], func=AF.Silu), r=["cact"], w=["cact"])
        T = Pool(nc)
        wbuf = [T.sb("wada%d" % i, [128, 8 * NKC * 128], F32) for i in range(2)]
        mps = T.ps("mod_ps", [128, 512])
        gi = 0
        for l in range(2):
            for jg in range(9):
                wb = wbuf[gi % 2]
                wk = ("wada", gi % 2)
                n = 8 * NKC * 128
                p.dma("sp" if gi % 2 == 0 else "act", wb[:], b.wada[l, :, jg * n:(jg + 1) * n], w=[wk])
                for jj in range(8):
                    j = jg * 8 + jj
                    for kc in range(NKC):
                        o = (jj * NKC + kc) * 128
                        p.op("pe", lambda e, wb=wb, o=o, kc=kc, col=l * 72 + j: e.matmul(
                            mps[:, col:col + 1], lhsT=wb[:, o:o + 128], rhs=b.cact[:, kc:kc + 1],
                            start=(kc == 0), stop=(kc == NKC - 1)),
                            r=[wk, "cact"], w=["mod_ps"], sig=(kc == NKC - 1))
                gi += 1
        for jg in range(2):
            wb = wbuf[gi % 2]
            wk = ("wada", gi % 2)
            n = 8 * NKC * 128
            p.dma("sp" if gi % 2 == 0 else "act", wb[:], b.wada_kv[:, jg * n:(jg + 1) * n], w=[wk])
            for jj in range(8):
                for kc in range(NKC):
                    o = (jj * NKC + kc) * 128
                    p.op("pe", lambda e, wb=wb, o=o, kc=kc, col=144 + jg * 8 + jj: e.matmul(
                        mps[:, col:col + 1], lhsT=wb[:, o:o + 128], rhs=b.cact[:, kc:kc + 1],
                        start=(kc == 0), stop=(kc == NKC - 1)),
                        r=[wk, "cact"], w=["mod_ps"], sig=(kc == NKC - 1))
            gi += 1
        p.op("dve", lambda e: e.tensor_tensor(out=b.mod[:], in0=mps[:, 0:160], in1=b.mod[:], op=ALU.add),
             r=["mod_ps", "mod"], w=["mod"])
        p.op("dve", lambda e: e.tensor_scalar(out=b.Amod[:, 48:56], in0=b.mod[:, 152:160], scalar1=1.0, scalar2=1.0,
                                              op0=ALU.add, op1=ALU.mult), r=["mod"], w=["Amod"])
        p.op("dve", lambda e: e.tensor_tensor(out=b.Amod[:, 48:56], in0=b.Amod[:, 48:56], in1=b.ng[:, 48:56],
                                              op=ALU.mult), r=["Amod", "ng"], w=["Amod"])
        for l in range(2):
            for s in range(3):
                c0 = (l * 3 + s) * NKC
                m0 = l * 72 + s * 24
                sc = b.mod[:, m0 + 8:m0 + 16]
                gt = b.mod[:, m0 + 16:m0 + 24]
                p.op("dve", lambda e, c0=c0, sc=sc: e.tensor_scalar(
                    out=b.Amod[:, c0:c0 + 8], in0=sc, scalar1=1.0, scalar2=1.0, op0=ALU.add, op1=ALU.mult),
                    r=["mod"], w=["Amod"])
                p.op("dve", lambda e, c0=c0: e.tensor_tensor(
                    out=b.Amod[:, c0:c0 + 8], in0=b.Amod[:, c0:c0 + 8], in1=b.ng[:, c0:c0 + 8], op=ALU.mult),
                    r=["Amod", "ng"], w=["Amod"])
                f = 1.0 if s == 1 else 0.5
                p.op("dve", lambda e, c0=c0, gt=gt, f=f: e.tensor_scalar(
                    out=b.Gmod[:, c0:c0 + 8], in0=gt, scalar1=1.0, scalar2=f, op0=ALU.add, op1=ALU.mult),
                    r=["mod"], w=["Gmod"])
        self.phase_end(T, [("wada", 0), ("wada", 1), "mod_ps"])

    def norm_tile(self, P, ht, hk, ut, uk, ssq, ss_ps, rstd, tmp, ls, TT, ssk="ss_ps"):
        b, p = self, self.p
        for kc in range(NKC):
            p.op("act", lambda e, kc=kc: e.activation(out=ssq[kc % 2][:], in_=ht[:, kc, :], func=AF.Square),
                 r=[hk], w=[("ssq", kc % 2)])
            p.op("pe", lambda e, kc=kc: e.matmul(ss_ps[:], lhsT=b.ones_b[:], rhs=ssq[kc % 2][:],
                                                  start=(kc == 0), stop=(kc == NKC - 1)),
                 r=[("ssq", kc % 2), "ones_b"], w=[ssk])
        p.op("act", lambda e: e.activation(out=rstd[:], in_=ss_ps[:], func=AF.Sqrt, scale=1.0 / D, bias=EPS),
             r=[ssk], w=["rstd"])
        p.op("dve", lambda e: e.reciprocal(out=rstd[:], in_=rstd[:]), r=["rstd"], w=["rstd"])
        l, s = divmod(ls, 3)
        m0 = l * 72 + s * 24
        for kc in range(NKC):
            p.op("dve", lambda e, kc=kc: e.tensor_tensor(out=tmp[kc % 2][:], in0=ht[:, kc, :], in1=rstd[:],
                                                         op=ALU.mult), r=[hk, "rstd"], w=[("ntmp", kc % 2)])
            p.op("pool", lambda e, kc=kc: e.tensor_scalar(
                out=ut[:, kc, :], in0=tmp[kc % 2][:], scalar1=b.Amod[:, ls * 8 + kc:ls * 8 + kc + 1],
                scalar2=b.mod[:, m0 + kc:m0 + kc + 1], op0=ALU.mult, op1=ALU.add),
                r=[("ntmp", kc % 2), "Amod", "mod"], w=[uk])

    def ffn_phase(self, src, dst, widx, ls, ntok, TT=512):
        b, p, nc = self, self.p, self.nc
        P = Pool(nc)
        u = b.key("ffn")
        ht = [P.sb("ht%d" % i, [128, NKC, TT], F32) for i in range(2)]
        ut = [P.sb("ut%d" % i, [128, NKC, TT], BF16) for i in range(2)]
        at = P.sb("at", [128, NJ, TT], BF16)
        ssq = [P.sb("ssq%d" % i, [128, TT], BF16) for i in range(2)]
        tmp = [P.sb("ntmp%d" % i, [128, TT], F32) for i in range(2)]
        rstd = P.sb("rstd", [128, TT], F32)
        sg = [P.sb("sg%d" % i, [128, TT], F32) for i in range(2)]
        GJ = 2
        NWI = 4
        wi = [P.sb("wi%d" % i, [128, GJ * 2 * NKC * 128], BF16) for i in range(NWI)]
        NWO = 4
        wo = [P.sb("wo%d" % i, [128, NJ * 128], BF16) for i in range(NWO)]
        ss_ps = P.ps("ss_ps", [128, TT])
        g_ps = [P.ps("g_ps%d" % i, [128, TT]) for i in range(2)]
        u_ps = [P.ps("u_ps%d" % i, [128, TT]) for i in range(2)]
        o_ps = [P.ps("o_ps%d" % i, [128, TT]) for i in range(2)]
        srcv = src.rearrange("(kc p) t -> p kc t", p=128)
        dstv = dst.rearrange("(kc p) t -> p kc t", p=128)
        nt = ntok // TT
        NG = NJ // GJ
        items = []
        for tt in range(nt):
            for jg in range(NG):
                items.append(("wi", jg))
            for o_ in range(NKC):
                items.append(("wo", o_))
        cnt = {"wi": 0, "wo": 0}
        slot = []
        for kind, idx in items:
            slot.append(cnt[kind] % (NWI if kind == "wi" else NWO))
            cnt[kind] += 1
        state = {"issued": 0}
        LEAD = 3

        def ensure(k):
            while state["issued"] <= min(k, len(items) - 1):
                i = state["issued"]
                kind, idx = items[i]
                if kind == "wi":
                    n = GJ * 2 * NKC * 128
                    p.dma("sp", wi[slot[i]][:], b.ffn_in_b[widx, :, idx * n:(idx + 1) * n],
                          r=[("ffn_in_b", widx, idx)], w=[("wi", slot[i])])
                else:
                    n = NJ * 128
                    p.dma("sp", wo[slot[i]][:], b.ffn_out_b[widx, :, idx * n:(idx + 1) * n],
                          r=[("ffn_out_b", widx, idx)], w=[("wo", slot[i])])
                state["issued"] += 1

        def load_h(tt):
            p.dma("sp", ht[tt % 2][:], srcv[:, :, tt * TT:(tt + 1) * TT], r=["H"], w=[("ht", tt % 2)])

        def norm(tt):
            self.norm_tile(P, ht[tt % 2], ("ht", tt % 2), ut[tt % 2], ("ut", tt % 2), ssq, ss_ps, rstd, tmp, ls, TT)

        load_h(0)
        ensure(LEAD)
        norm(0)
        it = 0
        for tt in range(nt):
            hb = ht[tt % 2]
            hk = ("ht", tt % 2)
            ub = ut[tt % 2]
            uk = ("ut", tt % 2)
            if tt + 1 < nt:
                load_h(tt + 1)
            for jg in range(NG):
                ensure(it + LEAD)
                wb = wi[slot[it]]
                wk = ("wi", slot[it])
                it += 1
                for j2 in range(GJ):
                    jj = jg * GJ + j2
                    pi = jj % 2
                    for half, pst, pk in ((0, g_ps[pi], ("g_ps", pi)), (1, u_ps[pi], ("u_ps", pi))):
                        for kc in range(NKC):
                            o = ((j2 * 2 + half) * NKC + kc) * 128
                            p.op("pe", lambda e, pst=pst, o=o, kc=kc, wb=wb: e.matmul(
                                pst[:], lhsT=wb[:, o:o + 128], rhs=ub[:, kc, :],
                                start=(kc == 0), stop=(kc == NKC - 1)),
                                r=[wk, uk], w=[pk], sig=(kc == NKC - 1))
                    p.op("act", lambda e, pi=pi: e.activation(out=sg[pi][:], in_=g_ps[pi][:], func=AF.Silu),
                         r=[("g_ps", pi)], w=[("sg", pi)])
                    p.op("dve", lambda e, pi=pi, jj=jj: e.tensor_tensor(
                        out=at[:, jj, :], in0=u_ps[pi][:], in1=sg[pi][:], op=ALU.mult),
                        r=[("u_ps", pi), ("sg", pi)], w=[("at", jj)])
            if tt + 1 < nt:
                norm(tt + 1)
            for o_ in range(NKC):
                ensure(it + LEAD)
                wb = wo[slot[it]]
                wk = ("wo", slot[it])
                it += 1
                pi = o_ % 2
                for kc in range(NJ):
                    p.op("pe", lambda e, pi=pi, kc=kc, wb=wb: e.matmul(
                        o_ps[pi][:], lhsT=wb[:, kc * 128:(kc + 1) * 128], rhs=at[:, kc, :],
                        start=(kc == 0), stop=(kc == NJ - 1)),
                        r=[wk, ("at", kc)], w=[("o_ps", pi)], sig=(kc == NJ - 1))
                p.op("dve", lambda e, pi=pi, o_=o_, hb=hb: e.scalar_tensor_tensor(
                    out=hb[:, o_, :], in0=o_ps[pi][:], scalar=b.Gmod[:, ls * 8 + o_:ls * 8 + o_ + 1],
                    in1=hb[:, o_, :], op0=ALU.mult, op1=ALU.add),
                    r=[("o_ps", pi), hk, "Gmod"], w=[hk])
            p.dma("pool", dstv[:, :, tt * TT:(tt + 1) * TT], hb[:], r=[hk], w=["H"])
        self.phase_end(P, [("ht", 0), ("ht", 1), ("ut", 0), ("ut", 1), "ss_ps", "rstd"]
                       + [("at", j) for j in range(NJ)]
                       + [("wi", i) for i in range(NWI)] + [("wo", i) for i in range(NWO)]
                       + [(n_, i) for n_ in ("g_ps", "u_ps", "o_ps", "sg", "ssq", "ntmp") for i in range(2)])

    def qkv_phase(self, ntok, TT=512):
        b, p, nc = self, self.p, self.nc
        P = Pool(nc)
        ls = 1
        ht = [P.sb("ht%d" % i, [128, NKC, TT], F32) for i in range(2)]
        ut = [P.sb("ut%d" % i, [128, NKC, TT], BF16) for i in range(2)]
        ssq = [P.sb("ssq%d" % i, [128, TT], BF16) for i in range(2)]
        tmp = [P.sb("ntmp%d" % i, [128, TT], F32) for i in range(2)]
        rstd = P.sb("rstd", [128, TT], F32)
        cs = [P.sb("cs%d" % i, [128, 2, TT], F32) for i in range(2)]
        wv = P.sb("wv", [128, 3, NKC, 1024], BF16)
        NW = 3
        wq = [P.sb("wq%d" % i, [128, 8 * NKC * 128], BF16) for i in range(NW)]
        sq = [P.sb("sq%d" % i, [128, TT], BF16) for i in range(2)]
        xb = [P.sb("xb%d" % i, [128, TT], BF16) for i in range(2)]
        r2 = [P.sb("r2%d" % i, [128, TT], F32) for i in range(2)]
        t1 = [P.sb("t1%d" % i, [128, TT], F32) for i in range(2)]
        t2 = [P.sb("t2%d" % i, [128, TT], F32) for i in range(2)]
        qo = [P.sb("qo%d" % i, [128, TT], BF16) for i in range(3)]
        vo = [P.sb("vo%d" % i, [128, 1024], BF16) for i in range(2)]
        ss_ps = P.ps("ss_ps", [128, TT])
        x_ps = [P.ps("x_ps%d" % i, [128, TT]) for i in range(2)]
        s2_ps = P.ps("s2_ps", [128, TT])
        rot_ps = P.ps("rot_ps", [128, TT])
        v_ps = [P.ps("v_ps%d" % i, [128, 512]) for i in range(2)]
        srcv = b.H.rearrange("(kc p) t -> p kc t", p=128)
        nt = ntok // TT
        print("qkv sbuf remaining", nc.sbuf_bytes_remaining)
        import os
        B2 = os.environ.get("BIS2", "")
        for g in range(0 if "w" in B2 else 3):
            p.dma("sp", wv[:, g, :, :], b.wv_b[:, g * NKC * 1024:(g + 1) * NKC * 1024].rearrange(
                "p (kc n) -> p kc n", kc=NKC), r=[("wv_b", g)], w=[("wv", g)])
        wc = 0
        tc_ = 0
        vc = 0
        p.dma("sp", ht[0][:], srcv[:, :, 0:TT], r=["H"], w=[("ht", 0)])
        for tt in range(nt):
            hb, hk, ub, uk = ht[tt % 2], ("ht", tt % 2), ut[tt % 2], ("ut", tt % 2)
            csb, ck = cs[tt % 2], ("cs", tt % 2)
            if "c" not in B2:
                p.dma("sp", csb[:, 0, :], b.cosT[:, tt * TT:(tt + 1) * TT], w=[ck])
                p.dma("sp", csb[:, 1, :], b.sinT[:, tt * TT:(tt + 1) * TT], w=[ck])
            if "n" not in B2:
                self.norm_tile(P, hb, hk, ub, uk, ssq, ss_ps, rstd, tmp, ls, TT)
            if tt + 1 < nt:
                p.dma("sp", ht[(tt + 1) % 2][:], srcv[:, :, (tt + 1) * TT:(tt + 2) * TT], r=["H"],
                      w=[("ht", (tt + 1) % 2)])
            import os
            BIS = int(os.environ.get('BIS', '0'))
            for wg in range(0 if BIS in (2, 3) else 6):
                g, qk = divmod(wg, 2)
                wb, wk = wq[wc % NW], ("wq", wc % NW)
                wc += 1
                n = 8 * NKC * 128
                p.dma("sp", wb[:], b.wqk_b[:, wg * n:(wg + 1) * n], r=[("wqk_b", wg)], w=[wk])
                for hp in range(8):
                    i2 = tc_ % 2
                    i3 = tc_ % 3
                    tc_ += 1
                    xp, xk = x_ps[i2], ("x_ps", i2)
                    for kc in range(NKC):
                        o = (hp * NKC + kc) * 128
                        p.op("pe", lambda e, xp=xp, o=o, kc=kc, wb=wb: e.matmul(
                            xp[:], lhsT=wb[:, o:o + 128], rhs=ub[:, kc, :], start=(kc == 0), stop=(kc == NKC - 1)),
                            r=[wk, uk], w=[xk], sig=(kc == NKC - 1))
                    p.op("act", lambda e, i2=i2, xp=xp: e.activation(out=sq[i2][:], in_=xp[:], func=AF.Square),
                         r=[xk], w=[("sq", i2)])
                    p.op("act", lambda e, i2=i2, xp=xp: e.activation(out=xb[i2][:], in_=xp[:], func=AF.Copy),
                         r=[xk], w=[("xb", i2)])
                    p.op("pe", lambda e, i2=i2: e.matmul(s2_ps[:], lhsT=b.bones_b[:], rhs=sq[i2][:],
                                                         start=True, stop=True),
                         r=[("sq", i2), "bones_b"], w=["s2_ps"])
                    p.op("pe", lambda e, i2=i2, wg=wg: e.matmul(rot_ps[:], lhsT=b.pg[:, wg * 128:(wg + 1) * 128],
                                                                 rhs=xb[i2][:], start=True, stop=True),
                         r=[("xb", i2), "pg"], w=["rot_ps"])
                    p.op("act", lambda e, i2=i2: e.activation(out=r2[i2][:], in_=s2_ps[:], func=AF.Sqrt,
                                                              scale=1.0 / 64, bias=EPS),
                         r=["s2_ps"], w=[("r2", i2)])
                    p.op("dve", lambda e, i2=i2: e.reciprocal(out=r2[i2][:], in_=r2[i2][:]),
                         r=[("r2", i2)], w=[("r2", i2)])
                    p.op("dve", lambda e, i2=i2, xp=xp, wg=wg: e.scalar_tensor_tensor(
                        out=t1[i2][:], in0=xp[:], scalar=b.ga[:, wg:wg + 1], in1=csb[:, 0, :],
                        op0=ALU.mult, op1=ALU.mult), r=[xk, ck, "ga"], w=[("t1", i2)])
                    p.op("dve", lambda e, i2=i2: e.tensor_tensor(out=t2[i2][:], in0=rot_ps[:], in1=csb[:, 1, :],
                                                                 op=ALU.mult),
                         r=["rot_ps", ck], w=[("t2", i2)])
                    p.op("pool", lambda e, i2=i2: e.tensor_tensor(out=t1[i2][:], in0=t1[i2][:], in1=t2[i2][:],
                                                                  op=ALU.add),
                         r=[("t1", i2), ("t2", i2)], w=[("t1", i2)])
                    p.op("pool", lambda e, i2=i2, i3=i3: e.tensor_tensor(out=qo[i3][:], in0=t1[i2][:],
                                                                         in1=r2[i2][:], op=ALU.mult),
                         r=[("t1", i2), ("r2", i2)], w=[("qo", i3)])
                    dst = (b.QT if qk == 0 else b.KT)[g, hp, :, tt * TT:(tt + 1) * TT]
                    p.dma("sp", dst, qo[i3][:], r=[("qo", i3)], w=[("QK", g, qk, hp, tt)])
            for blk in range(0 if BIS in (1, 3) else TT // 128):
                for g in range(3):
                    vb, vk = vo[vc % 2], ("vo", vc % 2)
                    vc += 1
                    for hf in range(2):
                        vp, vpk = v_ps[hf], ("v_ps", hf)
                        for kc in range(NKC):
                            p.op("pe", lambda e, vp=vp, kc=kc, g=g, hf=hf, blk=blk: e.matmul(
                                vp[:], lhsT=ub[:, kc, blk * 128:(blk + 1) * 128],
                                rhs=wv[:, g, kc, hf * 512:(hf + 1) * 512], start=(kc == 0), stop=(kc == NKC - 1)),
                                r=[("wv", g), uk], w=[vpk], sig=(kc == NKC - 1))
                        p.op("act", lambda e, vp=vp, vb=vb, hf=hf: e.activation(
                            out=vb[:, hf * 512:(hf + 1) * 512], in_=vp[:], func=AF.Copy), r=[vpk], w=[vk])
                    t0 = tt * TT + blk * 128
                    p.dma("sp", b.V[g, t0:t0 + 128, :], vb[:], r=[vk], w=[("V", g, t0 // 128)])
        keys = [("ht", 0), ("ht", 1), ("ut", 0), ("ut", 1), "ss_ps", "rstd", "s2_ps", "rot_ps",
                ("wv", 0), ("wv", 1), ("wv", 2)]
        keys += [(n_, i) for n_ in ("ssq", "ntmp", "cs", "sq", "xb", "r2", "t1", "t2", "x_ps", "v_ps", "vo")
                 for i in range(2)]
        keys += [("wq", i) for i in range(NW)] + [("qo", i) for i in range(3)]
        self.phase_end(P, keys)

    def attn_a_phase(self, ntok):
        b, p, nc = self, self.p, self.nc
        P = Pool(nc)
        SP = 2048
        RATES = (1, 4, 16)
        kt = P.sb("kt", [128, 3, 2 * SP], BF16)
        qt = P.sb("qt", [128, 3, SP], BF16)
        NVB = 17 + 20 + 32
        va = P.sb("va", [128, NVB, 192], BF16)
        pt = [P.sb("pt%d" % i, [128, 512], BF16) for i in range(2)]
        ot = P.sb("ot", [128, 8, SP], BF16)
        rden = P.sb("rden", [128, SP], F32)
        wo = P.sb("wo", [128, 8, 8, 128], BF16)
        ht = P.sb("ht", [128, NKC, 512], F32)
        acc = P.ps("acc", [128, SP])
        s_ps = [P.ps("s_ps%d" % i, [128, 512]) for i in range(2)]
        y_ps = [P.ps("y_ps%d" % i, [128, 512]) for i in range(2)]
        Hv = b.H.rearrange("(kc p) t -> p kc t", p=128)
        p.dma("sp", wo[:], b.wo_a_b.rearrange("p (o hp m) -> p o hp m", o=8, hp=8), r=["wo_a_b"], w=["wo"])
        p.op("pool", lambda e: e.memset(va[:, :, 64:128], 1.0), w=["va"])
        ls = 1
        uc = 0
        for sp in range(ntok // SP):
            t0 = sp * SP
            for hp in range(8):
                for g in range(3):
                    if sp > 0:
                        p.dma("sp", kt[:, g, :], b.KT[g, hp, :, t0 - SP:t0 + SP],
                              r=[("QK", g, 1, hp, tt) for tt in range((t0 - SP) // 512, (t0 + SP) // 512)], w=["kt"])
                    else:
                        p.dma("sp", kt[:, g, SP:], b.KT[g, hp, :, 0:SP],
                              r=[("QK", g, 1, hp, tt) for tt in range(0, SP // 512)], w=["kt"])
                    p.dma("sp", qt[:, g, :], b.QT[g, hp, :, t0:t0 + SP],
                          r=[("QK", g, 0, hp, tt) for tt in range(t0 // 512, (t0 + SP) // 512)], w=["qt"])
                vidx = {}
                vi = 0
                for g, r in enumerate(RATES):
                    nq = SP // (128 * r)
                    for res in range(r):
                        j0 = -1 if sp > 0 else 0
                        nb = nq - j0
                        Vg = b.V[g]
                        tstart = ((sp * nq + j0) * 128) * r + res
                        for half in range(2):
                            tb_ = tstart - res
                            src = Vg[tb_:tb_ + nb * 128 * r, hp * 128 + half * 64: hp * 128 + half * 64 + 64]
                            src = src.rearrange("(jb i r) d -> i jb r d", i=128, r=r)[:, :, res, :]
                            dstc = 0 if half == 0 else 128
                            p.dma("act" if half else "sp", va[:, vi:vi + nb, dstc:dstc + 64], src,
                                  r=[("V", g, tb) for tb in range(tstart // 128, (tstart + nb * 128 * r + 127) // 128)],
                                  w=["va"])
                        for jb in range(j0, nq):
                            vidx[(g, res, jb)] = vi
                            vi += 1
                for a in range(2):
                    rows = slice(0, 64) if a == 0 else slice(64, 128)
                    vcols = slice(0, 128) if a == 0 else slice(64, 192)
                    started = [False] * 4
                    units = []
                    for g, r in enumerate(RATES):
                        nq = SP // (128 * r)
                        for res in range(r):
                            for j in range(nq):
                                units.append((g, r, res, j))
                    npair = len(units) // 2
                    pis = []

                    def qk_(ip):
                        nonlocal uc
                        u0 = ip * 2
                        pi = uc % 2
                        uc += 1
                        pis.append(pi)
                        sp_, sk = s_ps[pi], ("s_ps", pi)
                        for ui in range(2):
                            g, r, res, j = units[u0 + ui]
                            qs = res + j * 128 * r
                            qap = qt[rows, g, qs:qs + 127 * r + 1:r]
                            for kb in range(2):
                                jb = j - 1 + kb
                                col = ui * 256 + kb * 128
                                if (g, res, jb) not in vidx:
                                    p.op("dve", lambda e, sp_=sp_, col=col: e.memset(sp_[:, col:col + 128], -100.0),
                                         w=[sk])
                                    continue
                                ks = SP + res + jb * 128 * r
                                kap = kt[rows, g, ks:ks + 127 * r + 1:r]
                                p.op("pe", lambda e, sp_=sp_, col=col, kap=kap, qap=qap: e.matmul(
                                    sp_[:, col:col + 128], lhsT=kap, rhs=qap, start=True, stop=True),
                                    r=["kt", "qt"], w=[sk])

                    def mid_(ip):
                        pi = pis[ip]
                        sp_, sk = s_ps[pi], ("s_ps", pi)
                        pb, pk = pt[pi], ("pt", pi)
                        p.op("act", lambda e, sp_=sp_, pb=pb: e.activation(out=pb[:], in_=sp_[:], func=AF.Exp,
                                                                          scale=0.125),
                             r=[sk], w=[pk])
                        p.op("pool", lambda e, pb=pb: e.tensor_tensor(out=pb[:], in0=pb[:], in1=b.mask4_b[:],
                                                                      op=ALU.mult),
                             r=[pk, "mask4_b"], w=[pk])

                    def pv_(ip):
                        u0 = ip * 2
                        pi = pis[ip]
                        pb, pk = pt[pi], ("pt", pi)
                        for ui in range(2):
                            g, r, res, j = units[u0 + ui]
                            qs = res + j * 128 * r
                            for kb in range(2):
                                jb = j - 1 + kb
                                if (g, res, jb) not in vidx:
                                    continue
                                col = ui * 256 + kb * 128
                                nsplit = 4 if r == 16 else 1
                                nq_ = 128 // nsplit
                                for qq in range(nsplit):
                                    q0 = qs + qq * nq_ * r
                                    bank = q0 // 512
                                    oap = acc[:, q0:q0 + (nq_ - 1) * r + 1:r]
                                    st = not started[bank]
                                    started[bank] = True
                                    c0 = col + qq * nq_
                                    p.op("pe", lambda e, oap=oap, v=vidx[(g, res, jb)], pb=pb, c0=c0, st=st, nq_=nq_:
                                         e.matmul(oap, lhsT=va[:, v, vcols], rhs=pb[:, c0:c0 + nq_], start=st,
                                                  stop=False),
                                         r=["va", pk], w=["acc"])

                    qk_(0)
                    for ip in range(npair):
                        if ip + 1 < npair:
                            qk_(ip + 1)
                        mid_(ip)
                        pv_(ip)
                    oth = slice(64, 128) if a == 0 else slice(0, 64)
                    for bk in range(4):
                        cs_ = slice(bk * 512, (bk + 1) * 512)
                        p.op("dve", lambda e, cs_=cs_: e.reciprocal(out=rden[rows, cs_], in_=acc[oth, cs_]),
                             r=["acc"], w=["rden"])
                        p.op("dve", lambda e, hp=hp, cs_=cs_: e.tensor_tensor(
                            out=ot[rows, hp, cs_], in0=acc[rows, cs_], in1=rden[rows, cs_], op=ALU.mult),
                            r=["acc", "rden"], w=[("ot", hp)])
            for tq in range(SP // 512):
                tk0 = t0 + tq * 512
                p.dma("sp", ht[:], Hv[:, :, tk0:tk0 + 512], r=["H"], w=["ht"])
                for o_ in range(8):
                    pi = o_ % 2
                    for hp in range(8):
                        p.op("pe", lambda e, pi=pi, o_=o_, hp=hp, tq=tq: e.matmul(
                            y_ps[pi][:], lhsT=wo[:, o_, hp, :], rhs=ot[:, hp, tq * 512:(tq + 1) * 512],
                            start=(hp == 0), stop=(hp == 7)),
                            r=["wo", ("ot", hp)], w=[("y_ps", pi)], sig=(hp == 7))
                    p.op("dve", lambda e, pi=pi, o_=o_: e.scalar_tensor_tensor(
                        out=ht[:, o_, :], in0=y_ps[pi][:], scalar=b.Gmod[:, ls * 8 + o_:ls * 8 + o_ + 1],
                        in1=ht[:, o_, :], op0=ALU.mult, op1=ALU.add),
                        r=[("y_ps", pi), "ht", "Gmod"], w=["ht"])
                p.dma("pool", Hv[:, :, tk0:tk0 + 512], ht[:], r=["ht"], w=["H"])
        keys = ["kt", "qt", "va", "rden", "wo", "ht", "acc"] + [("ot", i) for i in range(8)]
        keys += [(n_, i) for n_ in ("pt", "s_ps", "y_ps") for i in range(2)]
        self.phase_end(P, keys)

    def rope_norm(self, B, xp, xk, gcol, pgap, csb, ck, out_ap, out_key, rope=True, nope_ap=None, nope_key=None):
        b, p = self, self.p
        i2 = B["n"] % 2
        B["n"] += 1
        sq, xb, r2, t1, t2 = B["sq"][i2], B["xb"][i2], B["r2"][i2], B["t1"][i2], B["t2"][i2]
        s2_ps, rot_ps = B["s2_ps"], B["rot_ps"]
        s2k, rotk = B.get("s2k", "s2_ps"), B.get("rotk", "rot_ps")
        p.op("act", lambda e: e.activation(out=sq[:], in_=xp, func=AF.Square), r=[xk], w=[("sq", i2)])
        p.op("pe", lambda e: e.matmul(s2_ps[:], lhsT=b.bones_b[:], rhs=sq[:], start=True, stop=True),
             r=[("sq", i2), "bones_b"], w=[s2k])
        p.op("act", lambda e: e.activation(out=r2[:], in_=s2_ps[:], func=AF.Sqrt, scale=1.0 / 64, bias=EPS),
             r=[s2k], w=[("r2", i2)])
        p.op("dve", lambda e: e.reciprocal(out=r2[:], in_=r2[:]), r=[("r2", i2)], w=[("r2", i2)])
        if nope_ap is not None:
            p.op("dve", lambda e: e.scalar_tensor_tensor(out=nope_ap, in0=xp, scalar=gcol, in1=r2[:],
                                                         op0=ALU.mult, op1=ALU.mult),
                 r=[xk, ("r2", i2), "gkv"], w=[nope_key])
        if not rope:
            return
        p.op("act", lambda e: e.activation(out=xb[:], in_=xp, func=AF.Copy), r=[xk], w=[("xb", i2)])
        p.op("pe", lambda e: e.matmul(rot_ps[:], lhsT=pgap, rhs=xb[:], start=True, stop=True),
             r=[("xb", i2), "pgk"], w=[rotk])
        p.op("dve", lambda e: e.scalar_tensor_tensor(out=t1[:], in0=xp, scalar=gcol, in1=csb[:, 0, :],
                                                     op0=ALU.mult, op1=ALU.mult), r=[xk, ck, "gkv"], w=[("t1", i2)])
        p.op("dve", lambda e: e.tensor_tensor(out=t2[:], in0=rot_ps[:], in1=csb[:, 1, :], op=ALU.mult),
             r=[rotk, ck], w=[("t2", i2)])
        p.op("pool", lambda e: e.tensor_tensor(out=t1[:], in0=t1[:], in1=t2[:], op=ALU.add),
             r=[("t1", i2), ("t2", i2)], w=[("t1", i2)])
        p.op("pool", lambda e: e.tensor_tensor(out=out_ap, in0=t1[:], in1=r2[:], op=ALU.mult),
             r=[("t1", i2), ("r2", i2)], w=[out_key])

    def rn_bufs(self, P, TT, ps=True):
        return {"n": 0,
                "sq": [P.sb("sq%d" % i, [128, TT], BF16) for i in range(2)],
                "xb": [P.sb("xb%d" % i, [128, TT], BF16) for i in range(2)],
                "r2": [P.sb("r2%d" % i, [128, TT], F32) for i in range(2)],
                "t1": [P.sb("t1%d" % i, [128, TT], F32) for i in range(2)],
                "t2": [P.sb("t2%d" % i, [128, TT], F32) for i in range(2)],
                "s2_ps": P.ps("s2_ps", [128, TT]) if ps else None,
                "rot_ps": P.ps("rot_ps", [128, TT]) if ps else None}

    RN_KEYS = ["s2_ps", "rot_ps"] + [(n_, i) for n_ in ("sq", "xb", "r2", "t1", "t2") for i in range(2)]

    def kv_phase(self, ntok, TT=512):
        b, p, nc = self, self.p, self.nc
        P = Pool(nc)
        ls = 6
        ht = [P.sb("ht%d" % i, [128, NKC, TT], F32) for i in range(2)]
        ut = [P.sb("ut%d" % i, [128, NKC, TT], BF16) for i in range(2)]
        ssq = [P.sb("ssq%d" % i, [128, TT], BF16) for i in range(2)]
        tmp = [P.sb("ntmp%d" % i, [128, TT], F32) for i in range(2)]
        rstd = P.sb("rstd", [128, TT], F32)
        cs = [P.sb("cs%d" % i, [128, 2, TT], F32) for i in range(2)]
        wf = P.sb("wf", [128, 4, NKC, 128], BF16)
        wvv = P.sb("wvv", [128, 2, NKC, 128], BF16)
        w1 = P.sb("w1", [128, 2, 32, 128], BF16)
        k0T = P.sb("k0T", [128, 2, ntok + 16], BF16)
        ko = [P.sb("ko%d" % i, [128, TT], BF16) for i in range(2)]
        vo = [P.sb("vo%d" % i, [128, 128], BF16) for i in range(2)]
        B = self.rn_bufs(P, TT)
        ss_ps = P.ps("ss_ps", [128, TT])
        x_ps = [P.ps("x_ps%d" % i, [128, TT]) for i in range(2)]
        v_ps = [P.ps("v_ps%d" % i, [128, 512]) for i in range(2)]
        srcv = b.H.rearrange("(kc p) t -> p kc t", p=128)
        nt = ntok // TT
        p.dma("sp", wf[:], b.wkvf_b.rearrange("p (c kc m) -> p c kc m", c=4, kc=NKC), r=["wkvf_b"], w=["wf"])
        p.dma("sp", wvv[:], b.wkvv_b.rearrange("p (c kc m) -> p c kc m", c=2, kc=NKC), r=["wkvv_b"], w=["wvv"])
        p.dma("sp", w1[:], b.w1r_b.rearrange("p (i q m) -> p i q m", i=2, q=32), r=["w1r_b"], w=["w1"])
        p.op("pool", lambda e: e.memset(k0T[:, :, ntok:ntok + 16], 0.0), w=["k0T"])
        xc = 0
        vc = 0
        p.dma("sp", ht[0][:], srcv[:, :, 0:TT], r=["H"], w=[("ht", 0)])
        for tt in range(nt):
            hb, hk, ub, uk = ht[tt % 2], ("ht", tt % 2), ut[tt % 2], ("ut", tt % 2)
            csb, ck = cs[tt % 2], ("cs", tt % 2)
            p.dma("sp", csb[:, 0, :], b.cosT[:, tt * TT:(tt + 1) * TT], w=[ck])
            p.dma("sp", csb[:, 1, :], b.sinT[:, tt * TT:(tt + 1) * TT], w=[ck])
            self.norm_tile(P, hb, hk, ub, uk, ssq, ss_ps, rstd, tmp, ls, TT)
            if tt + 1 < nt:
                p.dma("sp", ht[(tt + 1) % 2][:], srcv[:, :, (tt + 1) * TT:(tt + 2) * TT], r=["H"],
                      w=[("ht", (tt + 1) % 2)])
            for c in range(4):
                i2 = xc % 2
                xc += 1
                xp, xk = x_ps[i2], ("x_ps", i2)
                for kc in range(NKC):
                    p.op("pe", lambda e, xp=xp, c=c, kc=kc: e.matmul(
                        xp[:], lhsT=wf[:, c, kc, :], rhs=ub[:, kc, :], start=(kc == 0), stop=(kc == NKC - 1)),
                        r=["wf", uk], w=[xk], sig=(kc == NKC - 1))
                if c in (0, 3):
                    j = 0 if c == 0 else 1
                    p.op("act", lambda e, xp=xp, j=j, tt=tt: e.activation(
                        out=k0T[:, j, tt * TT:(tt + 1) * TT], in_=xp[:], func=AF.Copy), r=[xk], w=["k0T"])
                else:
                    kb, kk = ko[c % 2], ("ko", c % 2)
                    self.rope_norm(B, xp[:], xk, b.gkv[:, c:c + 1], b.pgk[:, c * 128:(c + 1) * 128], csb, ck,
                                   kb[:], kk)
                    p.dma("sp", b.KS[c - 1, :, tt * TT:(tt + 1) * TT], kb[:], r=[kk], w=[("KS", c - 1, tt)])
            for blk in range(TT // 128):
                for br in range(2):
                    vb, vk = vo[vc % 2], ("vo", vc % 2)
                    vp, vpk = v_ps[vc % 2], ("v_ps", vc % 2)
                    vc += 1
                    for kc in range(NKC):
                        p.op("pe", lambda e, vp=vp, kc=kc, br=br, blk=blk: e.matmul(
                            vp[:, 0:128], lhsT=ub[:, kc, blk * 128:(blk + 1) * 128], rhs=wvv[:, br, kc, :],
                            start=(kc == 0), stop=(kc == NKC - 1)),
                            r=["wvv", uk], w=[vpk], sig=(kc == NKC - 1))
                    p.op("act", lambda e, vp=vp, vb=vb: e.activation(out=vb[:], in_=vp[:, 0:128], func=AF.Copy),
                         r=[vpk], w=[vk])
                    t0 = tt * TT + blk * 128
                    p.dma("sp", b.VS[br, t0:t0 + 128, :], vb[:], r=[vk], w=[("VS", br, t0 // 128)])
        nc_ = ntok // 16
        assert nc_ <= 512
        sh = [P.sb("sh%d" % i, [128, 512], BF16) for i in range(2)]
        cb = P.sb("cbias", [128, 4], F32)
        kraw = P.sb("kraw", [128, 512], F32)
        hp_ = [x_ps[0], x_ps[1]]
        bias_ps = v_ps[0]
        o_ps = v_ps[1]
        p.op("pool", lambda e: e.memset(b.vca[:, :, 64:128], 1.0), w=["vca"])
        p.op("pool", lambda e: e.memset(b.vca[:, :, 0:64], 0.0), w=["vca"])
        p.op("pool", lambda e: e.memset(b.vca[:, :, 128:192], 0.0), w=["vca"])
        p.op("pool", lambda e: e.memset(b.kcT[:], 0.0), w=["kcT"])
        p.op("pool", lambda e: e.memset(sh[0][:], 0.0), w=[("sh", 0)])
        p.op("pool", lambda e: e.memset(sh[1][:], 0.0), w=[("sh", 1)])
        bbanks = [(x_ps[0], ("x_ps", 0)), (x_ps[1], ("x_ps", 1)), (v_ps[0], ("v_ps", 0)), (v_ps[1], ("v_ps", 1))]
        for i in range(2):
            for kvh in range(2):
                rows = slice(0, 64) if kvh == 0 else slice(64, 128)
                bp, bk = bbanks[i * 2 + kvh]
                for pos in range(32):
                    p.op("pe", lambda e, i=i, pos=pos, rows=rows, bp=bp: e.matmul(
                        bp[:, 0:1], lhsT=w1[rows, i, pos, :],
                        rhs=b.posT_b[rows, i * 32 + pos:i * 32 + pos + 1], start=(pos == 0), stop=(pos == 31)),
                        r=["w1", "posT_b"], w=[bk], sig=(pos == 31))
                p.op("dve", lambda e, i=i, kvh=kvh, bp=bp: e.tensor_copy(
                    out=cb[:, i * 2 + kvh:i * 2 + kvh + 1], in_=bp[:, 0:1]), r=[bk], w=["cbias"])
        for i in range(2):
            for kvh in range(2):
                rows = slice(0, 64) if kvh == 0 else slice(64, 128)
                hps, hk_ = hp_[kvh], ("x_ps", kvh)
                for pos in range(32):
                    p.op("pe", lambda e, i=i, pos=pos, rows=rows, hps=hps: e.matmul(
                        hps[:, 0:nc_], lhsT=w1[rows, i, pos, :], rhs=k0T[rows, i, pos:pos + 16 * (nc_ - 1) + 1:16],
                        start=(pos == 0), stop=(pos == 31)),
                        r=["w1", "k0T"], w=[hk_], sig=(pos == 31))
                p.op("act", lambda e, i=i, kvh=kvh, hps=hps: e.activation(
                    out=sh[kvh][:, 0:nc_], in_=hps[:, 0:nc_], func=AF.Silu, bias=cb[:, i * 2 + kvh:i * 2 + kvh + 1]),
                    r=[hk_, "cbias"], w=[("sh", kvh)])
            if i == 0:
                for kvh in range(2):
                    p.op("pe", lambda e, kvh=kvh: e.matmul(
                        o_ps[:, 0:nc_], lhsT=b.w2pad_b[:, kvh * 128:(kvh + 1) * 128], rhs=sh[kvh][:, 0:nc_],
                        start=(kvh == 0), stop=(kvh == 1)), r=["w2pad_b", ("sh", kvh)], w=[("v_ps", 1)], sig=(kvh == 1))
                self.rope_norm(B, o_ps[:, 0:512], ("v_ps", 1), b.gkv[:, 0:1], None, None, None, None, None,
                               rope=False, nope_ap=kraw[:], nope_key="kraw")
                p.op("dve", lambda e: e.tensor_copy(out=b.kcT[:, 0:nc_ - 1], in_=kraw[:, 0:nc_ - 1]),
                     r=["kraw"], w=["kcT"])
            else:
                for kvh in range(2):
                    for cbk in range((nc_ + 127) // 128):
                        n_ = min(128, nc_ - cbk * 128)
                        p.op("pe", lambda e, kvh=kvh, cbk=cbk, n_=n_: e.matmul(
                            o_ps[0:n_, 0:64], lhsT=sh[kvh][:, cbk * 128:cbk * 128 + n_], rhs=b.w2v_b[:],
                            start=True, stop=True), r=["w2v_b", ("sh", kvh)], w=[("v_ps", 1)])
                        c0 = 0 if kvh == 0 else 128
                        p.op("dve", lambda e, cbk=cbk, n_=n_, c0=c0: e.tensor_copy(
                            out=b.vca[0:n_, cbk, c0:c0 + 64], in_=o_ps[0:n_, 0:64]), r=[("v_ps", 1)], w=["vca"])
        if "kv" in b.dbg:
            o = b.dout("dbg_cb", [128, 4], F32)
            p.dma("sp", o, cb[:], r=["cbias"], w=["dbg_cb"])
            o = b.dout("dbg_k0T", [128, 2, ntok + 16], BF16)
            p.dma("sp", o, k0T[:], r=["k0T"], w=["dbg_k0T"])
            o = b.dout("dbg_sh", [2, 128, 512], BF16)
            p.dma("sp", o[0], sh[0][:], r=[("sh", 0)], w=["dbg_sh"])
            p.dma("sp", o[1], sh[1][:], r=[("sh", 1)], w=["dbg_sh"])
        keys = [("ht", 0), ("ht", 1), ("ut", 0), ("ut", 1), "ss_ps", "rstd", "wf", "wvv", "w1", "k0T", "cbias",
                "kraw"] + self.RN_KEYS
        keys += [(n_, i) for n_ in ("ssq", "ntmp", "cs", "x_ps", "v_ps", "vo", "ko", "sh") for i in range(2)]
        self.phase_end(P, keys)

    def nsa_phase(self, ntok, TT=256):
        b, p, nc = self, self.p, self.nc
        P = Pool(nc)
        ls = 4
        nblk = ntok // 128
        ht = P.sb("ht", [128, NKC, TT], F32)
        ut = P.sb("ut", [128, NKC, TT], BF16)
        ssq = [P.sb("ssq%d" % i, [128, TT], BF16) for i in range(2)]
        tmp = [P.sb("ntmp%d" % i, [128, TT], F32) for i in range(2)]
        rstd = P.sb("rstd", [128, TT], F32)
        csb = P.sb("cs", [128, 2, TT], F32)
        wq = P.sb("wq", [128, 8, NKC, 128], BF16)
        wg = P.sb("wg", [128, NKC, 48], BF16)
        wo = P.sb("wo", [128, 8, 8, 128], BF16)
        ex = P.sb("ex", [128, ntok], BF16)
        ks = P.sb("ks", [128, 1, ntok], BF16)
        vs = P.sb("vs", [128, 1, nblk, 192], BF16)
        kw = [P.sb("kw%d" % i, [128, 640], BF16) for i in range(2)]
        vw = [P.sb("vw%d" % i, [128, 5, 192], BF16) for i in range(2)]
        qn = P.sb("qn", [128, 8, TT], BF16)
        qr = P.sb("qr", [128, 8, TT], BF16)
        gates = P.sb("gates", [128, TT // 128, 48], F32)
        cm = P.sb("cm", [128, 512], BF16)
        cmT = P.sb("cmT", [128, 4, 128], BF16)
        M1 = P.sb("M1", [128, 128], F32)
        M2 = P.sb("M2", [128, 128], F32)
        E = [P.sb("E%d" % i, [128, 512], F32) for i in range(2)]
        den = P.sb("den", [128, 16], F32)
        pacc = P.sb("pacc", [128, 520], F32)
        imp = P.sb("imp", [128, 128], F32)
        impw = P.sb("impw", [128, 128], F32)
        m8 = P.sb("m8", [128, 16], F32)
        sel = P.sb("sel", [128, 128], BF16)
        selT = P.sb("selT", [128, 128], BF16)
        msk = P.sb("msk", [128, nblk, 128], BF16)
        pt = [P.sb("pt%d" % i, [128, 1024], BF16) for i in range(2)]
        accs = P.sb("accs", [128, 1024], F32)
        wgt = P.sb("wgt", [128, 8], F32)
        otok = P.sb("otok", [128, 16, 64], F32)
        otb = P.sb("otb", [128, 1024], BF16)
        oT = P.sb("oT", [128, 8, TT], BF16)
        B = self.rn_bufs(P, TT, ps=False)
        x_ps = P.ps("x_ps", [128, 512])
        s_ps = [P.ps("s_ps%d" % i, [128, 1024]) for i in range(2)]
        B["s2_ps"] = s_ps[1][:, 0:TT]
        B["rot_ps"] = s_ps[1][:, 512:512 + TT]
        B["s2k"] = ("s_ps", 1)
        B["rotk"] = ("s_ps", 1)
        ss_ps = s_ps[1][:, 0:TT]
        sc_ = {"n": 0}
        acc = P.ps("acc", [128, 1024])
        t_ps = x_ps
        Hv = b.H.rearrange("(kc p) t -> p kc t", p=128)
        p.dma("sp", wq[:], b.wqg_b.rearrange("p (c kc m) -> p c kc m", c=8, kc=NKC), r=["wqg_b"], w=["wq"])
        p.dma("sp", wg[:], b.wgate_b.rearrange("p (kc m) -> p kc m", kc=NKC), r=["wgate_b"], w=["wg"])
        p.dma("sp", wo[:], b.wo_b_b.rearrange("p (o hp m) -> p o hp m", o=8, hp=8), r=["wo_b_b"], w=["wo"])
        p.dma("sp", ex[:], b.Ex_b[:, 0:ntok], r=["Ex_b"], w=["ex"])
        for i in range(2):
            p.op("pool", lambda e, i=i: e.memset(vw[i][:, :, 64:128], 1.0), w=[("vw", i)])
        for br in range(1):
            p.dma("sp", ks[:, br, :], b.KS[br, :, 0:ntok], r=[("KS", br, tt) for tt in range(ntok // 512)], w=["ks"])
            for kvh in range(2):
                src = b.VS[br, 0:ntok, kvh * 64:(kvh + 1) * 64].rearrange("(kb i) d -> i kb d", i=128)
                c0 = 0 if kvh == 0 else 128
                p.dma("act", vs[:, br, :, c0:c0 + 64], src, r=[("VS", br, t) for t in range(nblk)], w=["vs"])
        p.op("pool", lambda e: e.memset(vs[:, :, :, 64:128], 1.0), w=["vs"])
        p.op("pool", lambda e: e.memset(cm[:], 0.0), w=["cm"])
        p.op("pool", lambda e: e.memset(M1[:], 0.0), w=["M1"])
        p.op("pool", lambda e: e.memset(M2[:], -BIG), w=["M2"])
        p.op("pool", lambda e: e.memset(pacc[:], 0.0), w=["pacc"])

        def attend(kvh, G, KT, keyblocks, Vt, qsrc, qb, masks, gcol0, first):
            rows = slice(0, 64) if kvh == 0 else slice(64, 128)
            nkb = len(keyblocks)
            sis = []

            def qk_(ki):
                kb = keyblocks[ki]
                si = sc_["n"] % 2
                sc_["n"] += 1
                sis.append(si)
                sb_, sk_ = s_ps[si], ("s_ps", si)
                for hf in range(2):
                    p.op("pe", lambda e, kb=kb, hf=hf, sb_=sb_: e.matmul(
                        sb_[:, hf * 512:(hf + 1) * 512], lhsT=KT(kb, rows),
                        rhs=qsrc[rows, hf * 4:(hf + 1) * 4, qb * 128:(qb + 1) * 128], start=True, stop=True),
                        r=["ks", "kcT", "qn", "qr", ("kw", 0), ("kw", 1)], w=[sk_], sig=(hf == 1))

            def mid_(ki):
                kb = keyblocks[ki]
                si = sis[ki]
                sb_, sk_ = s_ps[si], ("s_ps", si)
                pb, pk = pt[si], ("pt", si)
                p.op("act", lambda e, pb=pb, sb_=sb_: e.activation(out=pb[:], in_=sb_[:], func=AF.Exp, scale=0.125),
                     r=[sk_], w=[pk])
                if masks.get(kb) is not None:
                    m = masks[kb]
                    p.op("pool" if (ki % 3 == 2) else "dve", lambda e, pb=pb, m=m: e.tensor_tensor(
                        out=pb[:].rearrange("p (h q) -> p h q", h=8), in0=pb[:].rearrange("p (h q) -> p h q", h=8),
                        in1=m.unsqueeze(1).broadcast_to([128, 8, 128]), op=ALU.mult),
                        r=[pk, "msk", "cmT", "mgt_b", "mask4_b"], w=[pk])

            def pv_(ki):
                kb = keyblocks[ki]
                si = sis[ki]
                pb, pk = pt[si], ("pt", si)
                for hf in range(2):
                    p.op("pe", lambda e, kb=kb, hf=hf, pb=pb, ki=ki: e.matmul(
                        acc[:, hf * 512:(hf + 1) * 512], lhsT=Vt(kb, kvh), rhs=pb[:, hf * 512:(hf + 1) * 512],
                        start=(ki == 0), stop=(ki == nkb - 1)),
                        r=[pk, "vs", "vca", ("vw", 0), ("vw", 1)], w=["acc"], sig=(hf == 1))

            qk_(0)
            for ki in range(nkb):
                if ki + 1 < nkb:
                    qk_(ki + 1)
                mid_(ki)
                pv_(ki)
            p.op("act", lambda e: e.activation(out=accs[:], in_=acc[:], func=AF.Copy), r=["acc"], w=["accs"])
            for hf in range(2):
                si = sc_["n"] % 2
                sc_["n"] += 1
                sb_, sk_ = s_ps[si], ("s_ps", si)
                ek = ("E", si)
                Eb = E[si]
                for h4 in range(4):
                    h = hf * 4 + h4
                    p.op("pe", lambda e, h=h, h4=h4, sb_=sb_: e.transpose(
                        out=sb_[:, h4 * 128:(h4 + 1) * 128], in_=accs[:, h * 128:(h + 1) * 128],
                        identity=b.ident_f[:]), r=["accs", "ident_f"], w=[sk_], sig=(h4 == 3))
                tv = sb_[:, 0:512].rearrange("p (h c) -> p h c", h=4)
                dcol = 64 if kvh == 0 else 0
                ocol = 0 if kvh == 0 else 64
                p.op("dve", lambda e, tv=tv, dcol=dcol: e.tensor_scalar(
                    out=wgt[:, 0:4], in0=tv[:, :, dcol], scalar1=1e-30, scalar2=None, op0=ALU.max),
                    r=[sk_], w=["wgt"])
                p.op("dve", lambda e: e.reciprocal(out=wgt[:, 0:4], in_=wgt[:, 0:4]), r=["wgt"], w=["wgt"])
                g0 = gcol0 + kvh * 8 + hf * 4
                p.op("dve", lambda e, g0=g0: e.tensor_tensor(out=wgt[:, 0:4], in0=wgt[:, 0:4],
                                                             in1=gates[:, qb, g0:g0 + 4], op=ALU.mult),
                     r=["wgt", "gates"], w=["wgt"])
                hd0 = kvh * 8 + hf * 4
                p.op("dve", lambda e, tv=tv, ocol=ocol, Eb=Eb: e.tensor_tensor(
                    out=Eb[:, 0:256].rearrange("p (h c) -> p h c", h=4), in0=tv[:, :, ocol:ocol + 64],
                    in1=wgt[:, 0:4].unsqueeze(2).broadcast_to([128, 4, 64]), op=ALU.mult),
                    r=[sk_, "wgt"], w=[ek])
                if first:
                    p.op("pool", lambda e, hd0=hd0, Eb=Eb: e.tensor_copy(
                        out=otok[:, hd0:hd0 + 4, :], in_=Eb[:, 0:256].rearrange("p (h c) -> p h c", h=4)),
                        r=[ek], w=["otok"])
                else:
                    p.op("pool", lambda e, hd0=hd0, Eb=Eb: e.tensor_tensor(
                        out=otok[:, hd0:hd0 + 4, :], in0=otok[:, hd0:hd0 + 4, :],
                        in1=Eb[:, 0:256].rearrange("p (h c) -> p h c", h=4), op=ALU.add),
                        r=[ek, "otok"], w=["otok"])

        for tt in range(ntok // TT):
            p.dma("sp", ht[:], Hv[:, :, tt * TT:(tt + 1) * TT], r=["H"], w=["ht"])
            p.dma("sp", csb[:, 0, :], b.cosT[:, tt * TT:(tt + 1) * TT], w=["cs"])
            p.dma("sp", csb[:, 1, :], b.sinT[:, tt * TT:(tt + 1) * TT], w=["cs"])
            self.norm_tile(P, ht, "ht", ut, "ut", ssq, ss_ps, rstd, tmp, ls, TT, ssk=("s_ps", 1))
            for c in range(8):
                for kc in range(NKC):
                    p.op("pe", lambda e, c=c, kc=kc: e.matmul(
                        x_ps[:, 0:TT], lhsT=wq[:, c, kc, :], rhs=ut[:, kc, :], start=(kc == 0), stop=(kc == NKC - 1)),
                        r=["wq", "ut"], w=["x_ps"], sig=(kc == NKC - 1))
                self.rope_norm(B, x_ps[:, 0:TT], "x_ps", b.gkv[:, 3:4], b.pgk[:, 384:512], csb, "cs",
                               qr[:, c, :], "qr", rope=True, nope_ap=qn[:, c, :], nope_key="qn")
            for qb in range(TT // 128):
                for kc in range(NKC):
                    p.op("pe", lambda e, qb=qb, kc=kc: e.matmul(
                        x_ps[:, 0:48], lhsT=ut[:, kc, qb * 128:(qb + 1) * 128], rhs=wg[:, kc, :],
                        start=(kc == 0), stop=(kc == NKC - 1)), r=["wg", "ut"], w=["x_ps"], sig=(kc == NKC - 1))
                p.op("act", lambda e, qb=qb: e.activation(out=gates[:, qb, :], in_=x_ps[:, 0:48], func=AF.Sigmoid),
                     r=["x_ps"], w=["gates"])
            for qb in range(TT // 128):
                G = tt * (TT // 128) + qb
                lo = 8 * G - 1
                if G > 0:
                    p.op("pool", lambda e, lo=lo: e.memset(cm[:, max(lo - 8, 0):lo], 1.0), w=["cm"])
                c0 = max(lo, 0)
                p.op("pool", lambda e, lo=lo, c0=c0: e.tensor_copy(out=cm[:, c0:lo + 8],
                                                                   in_=b.pats_f[:, c0 - lo:8]),
                     r=["pats_f"], w=["cm"])
                if G > 1:
                    p.op("pool", lambda e, G=G: e.memset(M1[:, 2 * G - 3:2 * G - 1], 1.0), w=["M1"])
                    p.op("pool", lambda e, G=G: e.memset(M2[:, 2 * G - 3:2 * G - 1], 0.0), w=["M2"])
                elif G == 1:
                    pass
                a0 = max(2 * G - 1, 0)
                a1 = min(2 * G + 2, 128)
                p.op("pool", lambda e, G=G, a0=a0, a1=a1: e.tensor_copy(
                    out=M1[:, a0:a1], in_=b.pats_f[:, 8 + a0 - (2 * G - 1):8 + a1 - (2 * G - 1)]),
                    r=["pats_f"], w=["M1"])
                p.op("pool", lambda e, G=G, a0=a0, a1=a1: e.tensor_copy(
                    out=M2[:, a0:a1], in_=b.pats_f[:, 11 + a0 - (2 * G - 1):11 + a1 - (2 * G - 1)]),
                    r=["pats_f"], w=["M2"])
                if G >= 1:
                    p.op("pool", lambda e: e.memset(M1[:, 0:1], 0.0), w=["M1"])
                    p.op("pool", lambda e: e.memset(M2[:, 0:1], 3.0 * BIG), w=["M2"])
                ncb = (8 * G + 6) // 128 + 1
                for cbk in range(ncb):
                    p.op("pe", lambda e, cbk=cbk: e.transpose(
                        out=t_ps[:, cbk * 64:(cbk + 1) * 64].bitcast(BF16), in_=cm[:, cbk * 128:(cbk + 1) * 128],
                        identity=b.ident_b[:]), r=["cm", "ident_b"], w=["x_ps"])
                    p.op("act", lambda e, cbk=cbk: e.activation(
                        out=cmT[:, cbk, :], in_=t_ps[:, cbk * 64:(cbk + 1) * 64].bitcast(BF16), func=AF.Copy),
                        r=["x_ps"], w=["cmT"])
                kb0 = max(G - 4, 0)
                nkw = G + 1 - kb0
                wi_ = G % 2
                p.dma("sp", kw[wi_][:, 0:nkw * 128], b.KS[1, :, kb0 * 128:(G + 1) * 128],
                      r=[("KS", 1, t_) for t_ in range(kb0 * 128 // 512, (G * 128) // 512 + 1)], w=[("kw", wi_)])
                for kv_ in range(2):
                    src = b.VS[1, kb0 * 128:(G + 1) * 128, kv_ * 64:(kv_ + 1) * 64].rearrange("(kb i) d -> i kb d", i=128)
                    c0 = 0 if kv_ == 0 else 128
                    p.dma("sp", vw[wi_][:, 0:nkw, c0:c0 + 64], src, r=[("VS", 1, t_) for t_ in range(kb0, G + 1)],
                          w=[("vw", wi_)])
                for kvh in range(2):
                    rows = slice(0, 64) if kvh == 0 else slice(64, 128)
                    for c in range(8):
                        ei = sc_["n"] % 2
                        sc_["n"] += 1
                        p.op("pe", lambda e, c=c, rows=rows, qb=qb, ei=ei: e.matmul(
                            s_ps[ei][:, 0:512], lhsT=qn[rows, c, qb * 128:(qb + 1) * 128], rhs=b.kcT[rows, :],
                            start=True, stop=True), r=["qn", "kcT"], w=[("s_ps", ei)])
                        p.op("act", lambda e, ei=ei: e.activation(out=E[ei][:], in_=s_ps[ei][:, 0:512], func=AF.Exp,
                                                                  scale=0.125), r=[("s_ps", ei)], w=[("E", ei)])
                        p.op("dve", lambda e, ei=ei: e.tensor_tensor(out=E[ei][:], in0=E[ei][:], in1=cm[:],
                                                                     op=ALU.mult), r=[("E", ei), "cm"], w=[("E", ei)])
                        p.op("dve", lambda e, ei=ei, c=c: e.reduce_sum(out=den[:, c:c + 1], in_=E[ei][:], axis=AX.X),
                             r=[("E", ei)], w=["den"])
                        p.op("dve", lambda e, c=c: e.tensor_scalar(out=den[:, c:c + 1], in0=den[:, c:c + 1],
                                                                   scalar1=1e-30, scalar2=None, op0=ALU.max),
                             r=["den"], w=["den"])
                        p.op("dve", lambda e, c=c: e.reciprocal(out=den[:, 8 + c:9 + c], in_=den[:, c:c + 1]),
                             r=["den"], w=["den"])
                        if c == 0:
                            p.op("dve", lambda e, ei=ei, c=c: e.tensor_scalar(
                                out=pacc[:, 1:513], in0=E[ei][:], scalar1=den[:, 8 + c:9 + c], scalar2=None,
                                op0=ALU.mult), r=[("E", ei), "den"], w=["pacc"])
                        else:
                            p.op("dve", lambda e, ei=ei, c=c: e.scalar_tensor_tensor(
                                out=pacc[:, 1:513], in0=E[ei][:], scalar=den[:, 8 + c:9 + c], in1=pacc[:, 1:513],
                                op0=ALU.mult, op1=ALU.add), r=[("E", ei), "den", "pacc"], w=["pacc"])
                    wts = (1.0, 2.0, 2.0, 2.0, 1.0)
                    for o in range(5):
                        src = pacc[:, o:o + 4 * 127 + 1:4]
                        if o == 0:
                            p.op("dve", lambda e, src=src: e.tensor_copy(out=imp[:], in_=src), r=["pacc"], w=["imp"])
                        else:
                            p.op("dve", lambda e, src=src, o=o: e.scalar_tensor_tensor(
                                out=imp[:], in0=src, scalar=wts[o], in1=imp[:], op0=ALU.mult, op1=ALU.add),
                                r=["pacc", "imp"], w=["imp"])
                    p.op("dve", lambda e: e.tensor_tensor(out=imp[:], in0=imp[:], in1=M1[:], op=ALU.mult),
                         r=["imp", "M1"], w=["imp"])
                    p.op("dve", lambda e: e.tensor_tensor(out=imp[:], in0=imp[:], in1=M2[:], op=ALU.add),
                         r=["imp", "M2"], w=["imp"])
                    p.op("dve", lambda e: e.max(out=m8[:, 0:8], in_=imp[:]), r=["imp"], w=["m8"])
                    p.op("dve", lambda e: e.match_replace(out=impw[:], in_to_replace=m8[:, 0:8], in_values=imp[:],
                                                          imm_value=-BIG), r=["imp", "m8"], w=["impw"])
                    p.op("dve", lambda e: e.max(out=m8[:, 8:16], in_=impw[:]), r=["impw"], w=["m8"])
                    p.op("dve", lambda e: e.tensor_scalar(out=m8[:, 15:16], in0=m8[:, 15:16], scalar1=-0.5 * BIG,
                                                          scalar2=None, op0=ALU.max), r=["m8"], w=["m8"])
                    p.op("dve", lambda e: e.tensor_scalar(out=sel[:], in0=imp[:], scalar1=m8[:, 15:16], scalar2=None,
                                                          op0=ALU.is_ge), r=["imp", "m8"], w=["sel"])
                    p.op("pe", lambda e: e.transpose(out=t_ps[:, 0:64].bitcast(BF16), in_=sel[:],
                                                     identity=b.ident_b[:]), r=["sel", "ident_b"], w=["x_ps"])
                    p.op("act", lambda e: e.activation(out=selT[:], in_=t_ps[:, 0:64].bitcast(BF16), func=AF.Copy),
                         r=["x_ps"], w=["selT"])
                    for k4 in range(0, G + 1, 4):
                        nk = min(4, G + 1 - k4)
                        for kk in range(nk):
                            kb = k4 + kk
                            p.op("pe", lambda e, kb=kb, kk=kk: e.matmul(
                                t_ps[:, kk * 128:(kk + 1) * 128], lhsT=ex[:, kb * 128:(kb + 1) * 128], rhs=selT[:],
                                start=True, stop=True), r=["ex", "selT"], w=["x_ps"])
                        p.op("act", lambda e, k4=k4, nk=nk: e.activation(
                            out=msk[:, k4:k4 + nk, :], in_=t_ps[:, 0:nk * 128].rearrange("p (k q) -> p k q", k=nk),
                            func=AF.Copy), r=["x_ps"], w=["msk"])
                    p.op("dve", lambda e, G=G: e.tensor_tensor(out=msk[:, G, :], in0=msk[:, G, :],
                                                               in1=b.mask4_b[:, 128:256], op=ALU.mult),
                         r=["msk", "mask4_b"], w=["msk"])
                    cbl = list(range(ncb))
                    import os
                    OB = os.environ.get("OB", "csw")
                    fst = [True]
                    def first_():
                        v = fst[0]
                        fst[0] = False
                        return v
                    if "c" in OB: attend(kvh, G, lambda kb, rows: b.kcT[rows, kb * 128:(kb + 1) * 128],
                           cbl, lambda kb, kvh: b.vca[:, kb, (0 if kvh == 0 else 64):(128 if kvh == 0 else 192)],
                           qn, qb, {kb: cmT[:, kb, :] for kb in cbl}, 0, first_())
                    sbl = list(range(G + 1))
                    if "s" in OB: attend(kvh, G, lambda kb, rows: ks[rows, 0, kb * 128:(kb + 1) * 128],
                           sbl, lambda kb, kvh: vs[:, 0, kb, (0 if kvh == 0 else 64):(128 if kvh == 0 else 192)],
                           qr, qb, {kb: msk[:, kb, :] for kb in sbl}, 16, first_())
                    wbl = list(range(max(G - 4, 0), G + 1))
                    wm = {G: b.mask4_b[:, 128:256]}
                    if G - 4 >= 0:
                        wm[G - 4] = b.mgt_b[:]
                    if "w" in OB: attend(kvh, G, lambda kb, rows: kw[wi_][rows, (kb - kb0) * 128:(kb - kb0 + 1) * 128],
                           wbl, lambda kb, kvh: vw[wi_][:, kb - kb0, (0 if kvh == 0 else 64):(128 if kvh == 0 else 192)],
                           qr, qb, wm, 32, first_())
                if "nsa" in b.dbg and G == b.dbgG:
                    o = b.dout("dbg_otok", [128, 1024], F32)
                    p.dma("sp", o, otok[:].rearrange("p h d -> p (h d)"), r=["otok"], w=["dbg_otok"])
                    o = b.dout("dbg_imp", [128, 128], F32)
                    p.dma("sp", o, imp[:], r=["imp"], w=["dbg_imp"])
                    o = b.dout("dbg_sel", [128, 128], BF16)
                    p.dma("sp", o, sel[:], r=["sel"], w=["dbg_sel"])
                    o = b.dout("dbg_gates", [128, 48], F32)
                    p.dma("sp", o, gates[:, qb, :], r=["gates"], w=["dbg_gates"])
                    o = b.dout("dbg_qn", [128, 8, 512], BF16)
                    p.dma("sp", o, qn[:], r=["qn"], w=["dbg_qn"])
                    o = b.dout("dbg_qr", [128, 8, 512], BF16)
                    p.dma("sp", o, qr[:], r=["qr"], w=["dbg_qr"])
                    o = b.dout("dbg_pacc", [128, 520], F32)
                    p.dma("sp", o, pacc[:], r=["pacc"], w=["dbg_pacc"])
                p.op("act", lambda e: e.activation(out=otb[:], in_=otok[:].rearrange("p h d -> p (h d)"),
                                                   func=AF.Copy), r=["otok"], w=["otb"])
                for hp in range(8):
                    p.op("pe", lambda e, hp=hp: e.transpose(
                        out=t_ps[:, (hp % 4) * 64:(hp % 4 + 1) * 64].bitcast(BF16), in_=otb[:, hp * 128:(hp + 1) * 128],
                        identity=b.ident_b[:]), r=["otb", "ident_b"], w=["x_ps"])
                    p.op("act", lambda e, hp=hp, qb=qb: e.activation(
                        out=oT[:, hp, qb * 128:(qb + 1) * 128],
                        in_=t_ps[:, (hp % 4) * 64:(hp % 4 + 1) * 64].bitcast(BF16), func=AF.Copy),
                        r=["x_ps"], w=["oT"])
            for o_ in range(8):
                for hp in range(8):
                    p.op("pe", lambda e, o_=o_, hp=hp: e.matmul(
                        x_ps[:, 0:TT], lhsT=wo[:, o_, hp, :], rhs=oT[:, hp, :], start=(hp == 0), stop=(hp == 7)),
                        r=["wo", "oT"], w=["x_ps"], sig=(hp == 7))
                p.op("dve", lambda e, o_=o_: e.scalar_tensor_tensor(
                    out=ht[:, o_, :], in0=x_ps[:, 0:TT], scalar=b.Gmod[:, ls * 8 + o_:ls * 8 + o_ + 1],
                    in1=ht[:, o_, :], op0=ALU.mult, op1=ALU.add), r=["x_ps", "ht", "Gmod"], w=["ht"])
            p.dma("pool", Hv[:, :, tt * TT:(tt + 1) * TT], ht[:], r=["ht"], w=["H"])
        keys = ["ht", "ut", "rstd", "cs", "wq", "wg", "wo", "ex", "ks", "vs", "qn", "qr", "gates", "cm", "cmT", "M1",
                "M2", "den", "pacc", "imp", "impw", "m8", "sel", "selT", "msk", "accs", "wgt", "otok", "otb", "oT",
                "x_ps", "acc", ("s_ps", 0), ("s_ps", 1)] + self.RN_KEYS
        keys += [(n_, i) for n_ in ("ssq", "ntmp", "E", "pt", "kw", "vw") for i in range(2)]
        print("nsa sbuf remaining", nc.sbuf_bytes_remaining)
        self.phase_end(P, keys)

    def copy_H_out(self, n):
        b, p = self, self.p
        P = Pool(self.nc)
        t = P.sb("cpy", [128, NKC, 512], F32)
        Hv = b.H.rearrange("(kc p) t -> p kc t", p=128)
        Ov = b.out.rearrange("(kc p) t -> p kc t", p=128)
        for i in range(n // 512):
            p.dma("sp", t[:], Hv[:, :, i * 512:(i + 1) * 512], r=["H"], w=["cpy"])
            p.dma("sp", Ov[:, :, i * 512:(i + 1) * 512], t[:], r=["cpy"], w=["outfinal"])
        self.phase_end(P, ["cpy"])

    def phase_end(self, P, keys):
        b, p = self, self.p
        p.op("dve", lambda e: e.tensor_copy(out=b.ones_b[:, 0:1], in_=b.ones_f[:, 0:1]), r=["ones_f"], w=keys)
        for en in ("pe", "act", "pool", "sp"):
            E = p.es[en]
            p._wait(E, {"dve": p.es["dve"].count})
        P.free()

    def build(self):
        b = self
        n = b.ntok_dbg or S
        b.declare()
        b.convert_weights()
        b.convert_weights2()
        b.consts()
        if "mod" in b.dbg:
            o = b.dout("dbg_mod", [128, 144])
            b.p.dma("sp", o[:, :], b.mod[:], r=["mod"], w=["dbg_mod"])
        if b.stop_after == "consts":
            o = b.dout("dbg_pg", [128, 768], BF16)
            b.p.dma("sp", o[:, :], b.pg[:], r=["pg"], w=["dbg_pg"])
            o = b.dout("dbg_bones", [128, 128], BF16)
            b.p.dma("sp", o[:, :], b.bones_b[:], r=["bones_b"], w=["dbg_bones"])
            b.p.finish()
            return b.nc
        import os
        if not os.environ.get("NOFFN"):
            b.ffn_phase(b.xT, b.out if b.stop_after == "ffn00" else b.H, 0, 0, n)
        if b.stop_after == "ffn00":
            b.p.finish()
            return b.nc
        if b.stop_after == "ffn0Hx":
            b.p.dma("sp", b.out[0:128, 0:128], b.ones_f[:], r=["ones_f"], w=["outfinal"])
            b.p.finish()
            return b.nc
        if b.stop_after == "ffn0H":
            b.copy_H_out(n)
            b.p.finish()
            return b.nc
        b.qkv_phase(n)
        if b.stop_after == "qkv0":
            b.copy_H_out(n)
            b.p.finish()
            return b.nc
        if b.stop_after == "qkv":
            for nm, src in (("dbg_qt", b.QT), ("dbg_kt", b.KT)):
                o = b.dout(nm, [3, 8, 128, n], BF16)
                for g in range(3):
                    b.p.dma("sp", o[g], src[g, :, :, 0:n], r=[("QK", g, qk, hp, tt) for qk in range(2) for hp in range(8) for tt in range(n // 512)], w=[nm])
            o = b.dout("dbg_v", [3, n, 1024], BF16)
            b.p.dma("sp", o[:, :, :], b.V[:, 0:n, :], r=[("V", g, t) for g in range(3) for t in range(n // 128)], w=["dbg_v"])
            b.p.finish()
            return b.nc
        b.attn_a_phase(n)
        if b.stop_after == "attn0":
            b.copy_H_out(n)
            b.p.finish()
            return b.nc
        b.ffn_phase(b.H, b.H, 1, 2, n)
        b.kv_phase(n)
        if b.stop_after == "kv":
            for nm, t in (("dbg_kcT", b.kcT), ("dbg_vca", b.vca)):
                o = b.dout(nm, list(t.shape), BF16)
                b.p.dma("sp", o, t[:], r=["kcT", "vca"], w=[nm])
            b.copy_H_out(n)
            b.p.finish()
            return b.nc
        b.ffn_phase(b.H, b.H, 2, 3, n)
        b.nsa_phase(n)
        if b.stop_after == "nsa":
            b.copy_H_out(n)
            b.p.finish()
            return b.nc
        b.ffn_phase(b.H, b.out, 3, 5, n)
        b.p.finish()
        return b.nc


def _chunkp(v, nk):
    return np.ascontiguousarray(v.reshape(nk, 128).T)


def _lhsT_layout(w, ncols_chunks):
    K, M = w.shape
    kc = K // 128
    j = M // 128
    return np.ascontiguousarray(w.reshape(kc, 128, j, 128).transpose(1, 2, 0, 3).reshape(128, j * kc * 128))


def prep_inputs(inputs, core):
    bi = core % 4
    x = np.asarray(inputs["x"])
    m = {}
    m["xT"] = np.ascontiguousarray(x[bi].T)
    m["c_in"] = _chunkp(np.asarray(inputs["c"])[bi], NKC)
    ng = np.asarray(inputs["norm_g"])
    m["normg"] = np.concatenate([_chunkp(ng[l, s], NKC) for l in range(2) for s in range(3)], axis=1)
    ba = np.asarray(inputs["b_ada"])
    m["bada"] = np.concatenate([_chunkp(ba[l], 72) for l in range(2)], axis=1)
    wa = np.asarray(inputs["w_ada"])
    m["wada"] = np.stack([_lhsT_layout(wa[l], 72) for l in range(2)])
    wi = np.asarray(inputs["ffn_w_in"])
    wo = np.asarray(inputs["ffn_w_out"])
    fin = []
    fout = []
    for l in range(2):
        for s in range(2):
            w = wi[l, s]
            cols = np.concatenate([np.concatenate([np.arange(jj * 128, jj * 128 + 128),
                                                   DFF + np.arange(jj * 128, jj * 128 + 128)]) for jj in range(NJ)])
            fin.append(_lhsT_layout(w[:, cols], 44))
            fout.append(_lhsT_layout(wo[l, s], 8))
    m["ffn_in"] = np.stack(fin)
    m["ffn_out"] = np.stack(fout)
    m["ones_in"] = np.ones((128, 128), np.float32)
    wq = np.asarray(inputs["a_w_qkv"])[0].reshape(D, 3, 3, 16, 64)
    chunks = []
    for g in range(3):
        for qk in range(2):
            chunks.append(wq[:, g, qk].reshape(D, 1024))
    m["wqk"] = _lhsT_layout(np.concatenate(chunks, axis=1), 48)
    wvv = np.stack([wq[:, g, 2].reshape(D, 1024) for g in range(3)])
    m["wv"] = np.ascontiguousarray(wvv.reshape(3, NKC, 128, 1024).transpose(2, 0, 1, 3).reshape(128, 3 * NKC * 1024))
    woa = np.asarray(inputs["a_w_o"])[0]
    m["wo_a"] = np.ascontiguousarray(woa.reshape(8, 128, 8, 128).transpose(1, 2, 0, 3).reshape(128, 8 * 8 * 128))
    qg = np.asarray(inputs["a_q_gain"])[0]
    kg = np.asarray(inputs["a_k_gain"])[0]
    ga = np.zeros((128, 6), np.float32)
    for g in range(3):
        ga[:, 2 * g] = np.tile(qg[g], 2)
        ga[:, 2 * g + 1] = np.tile(kg[g], 2)
    m["gains_a"] = ga
    m["wada_kv"] = _lhsT_layout(np.asarray(inputs["w_ada_kv"]), 16)
    m["bada_kv"] = _chunkp(np.asarray(inputs["b_ada_kv"]), 16)
    m["kvng"] = _chunkp(np.asarray(inputs["kv_norm_g"]), NKC)
    wkv = np.asarray(inputs["w_kv"]).reshape(D, 3, 2, 128)
    m["wkvf"] = _lhsT_layout(np.concatenate([wkv[:, 0, 0], wkv[:, 1, 0], wkv[:, 2, 0], wkv[:, 0, 1]], axis=1), 4)
    wvv2 = np.stack([wkv[:, 1, 1], wkv[:, 2, 1]])
    m["wkvv"] = np.ascontiguousarray(wvv2.reshape(2, NKC, 128, 128).transpose(2, 0, 1, 3).reshape(128, 2 * NKC * 128))
    kkg = np.asarray(inputs["kv_k_gain"])
    gk = np.zeros((128, 4), np.float32)
    for i in range(3):
        gk[:, i] = np.tile(kkg[i], 2)
    gk[:, 3] = np.tile(np.asarray(inputs["b_q_gain"])[0], 2)
    m["gains_kv"] = gk
    w1 = np.asarray(inputs["phi_w1"]).reshape(2, 32, 64, 128).transpose(2, 0, 1, 3)
    m["w1r"] = np.ascontiguousarray(np.concatenate([w1, w1], axis=0).reshape(128, 2 * 32 * 128))
    cp = np.asarray(inputs["cmp_pos"]).transpose(2, 0, 1).reshape(64, 64)
    m["posT"] = np.ascontiguousarray(np.concatenate([cp, cp], axis=0))
    w2 = np.asarray(inputs["phi_w2"])
    w2p = np.zeros((128, 2, 2, 128), np.float32)
    w2p[:, 0, 0, 0:64] = w2[0]
    w2p[:, 0, 1, 64:128] = w2[0]
    w2p[:, 1, 0, 0:64] = w2[1]
    w2p[:, 1, 1, 64:128] = w2[1]
    m["w2pad"] = np.ascontiguousarray(w2p.reshape(128, 512))
    m["w2v"] = np.ascontiguousarray(w2[1])
    wqg = np.asarray(inputs["b_w_qg"])[0]
    wq_ = wqg[:, :1024].reshape(D, 16, 64)
    chunks = [np.concatenate([wq_[:, c], wq_[:, 8 + c]], axis=1) for c in range(8)]
    m["wqg"] = _lhsT_layout(np.concatenate(chunks, axis=1), 8)
    m["wgate"] = np.ascontiguousarray(wqg[:, 1024:].reshape(NKC, 128, 48).transpose(1, 0, 2).reshape(128, NKC * 48))
    wob = np.asarray(inputs["b_w_o"])[0]
    m["wo_b"] = np.ascontiguousarray(wob.reshape(8, 128, 8, 128).transpose(1, 2, 0, 3).reshape(128, 8 * 8 * 128))
    m.update(_const_tables())
    return m


_CT = {}


def _const_tables():
    if _CT:
        return _CT
    pr = np.zeros((128, 128), np.float32)
    for mm in range(128):
        if mm % 64 < 32:
            pr[mm + 32, mm] = -1.0
        else:
            pr[mm - 32, mm] = 1.0
    _CT["prot"] = pr
    bo = np.zeros((128, 128), np.float32)
    bo[:64, :64] = 1.0
    bo[64:, 64:] = 1.0
    _CT["bones"] = bo
    half = 32
    inv = (10000.0 ** (-np.arange(half, dtype=np.float32) / half)).astype(np.float32)
    ang = np.arange(S, dtype=np.float32)[None, :] * inv[:, None]
    cos = np.cos(ang).astype(np.float32)
    sin = np.sin(ang).astype(np.float32)
    _CT["cosT"] = np.ascontiguousarray(np.tile(cos, (4, 1)))
    _CT["sinT"] = np.ascontiguousarray(np.tile(sin, (4, 1)))
    k = np.arange(128)[:, None]
    q = np.arange(128)[None, :]
    prev = (k >= q).astype(np.float32)
    diag = (k <= q).astype(np.float32)
    _CT["mask4"] = np.ascontiguousarray(np.concatenate([prev, diag, prev, diag], axis=1))
    _CT["maskgt"] = (k > q).astype(np.float32)
    _CT["ident"] = np.eye(128, dtype=np.float32)
    ex = np.zeros((128, S), np.float32)
    ex[np.arange(S) // 64, np.arange(S)] = 1.0
    _CT["Ex"] = ex
    pats = np.zeros((128, 16), np.float32)
    qq = np.arange(128)
    for i in range(8):
        pats[:, i] = (qq >= 16 * i + 15)
    lo = qq < 64
    pats[:, 8] = np.where(lo, 0.0, 1.0)
    pats[:, 9] = 0.0
    pats[:, 10] = 0.0
    pats[:, 11] = np.where(lo, 1.0 * BIG, 0.0)
    pats[:, 12] = np.where(lo, 2.0 * BIG, 1.0 * BIG)
    pats[:, 13] = np.where(lo, -BIG, 2.0 * BIG)
    _CT["pats"] = pats
    return _CT


_CACHE = {}


def kernel(**inputs):
    b = Builder()
    b.ntok_dbg = None
    nc = b.build()
    in_maps = [prep_inputs(inputs, c) for c in range(NCORES)]
    res = run_bass_kernel_spmd(nc, in_maps, core_ids=list(range(NCORES)))
    out = np.stack([np.ascontiguousarray(res.results[c]["outT"].T) for c in range(4)])
    return out.astype(np.float32)
```
